# Optimizing a Trainium2 kernel written in Bass

```python
import jax, jax.numpy as jnp
from jax import lax
import numpy as np

D_MODEL = 1024
BATCH = 8
SEQ = 8192
DEPTH = 4
DEC_BATCH = 4
DEC_SEQ = 4096
PAST_LEN = 128

D_A = D_MODEL
HEAD_A = 64
H_A = D_A // HEAD_A
R_W = 64
R_A = 64
GN_EPS = 64e-5
HD = 64
HQ = D_MODEL // HD
HKV = HQ // 4
GRP = HQ // HKV
D_B = HQ * HD
D_KV = HKV * HD
WINDOW = 128
BLOCK = 128
ROPE_THETA = 10000.0
NORM_EPS = 1e-6
C_SHIFT = 3 * D_A + 2 * R_W + 2 * R_A
SPLITS = [C_SHIFT,
          C_SHIFT + D_A,
          C_SHIFT + D_A + D_B,
          C_SHIFT + D_A + D_B + D_KV,
          C_SHIFT + D_A + D_B + 2 * D_KV,
          C_SHIFT + D_A + 2 * D_B + 2 * D_KV]
N_IN = SPLITS[-1] + 2 * D_MODEL

kernel_name = "hybrid_rwkv7_swa_gated_encoder"


def rms_norm(x, g, eps=NORM_EPS):
    xf = x.astype(jnp.float32)
    y = xf * lax.rsqrt(jnp.mean(xf * xf, axis=-1, keepdims=True) + eps)
    return (y * g.astype(jnp.float32)).astype(x.dtype)


def centred_shift(p, mu):
    pad = jnp.pad(p, ((0, 0), (1, 1), (0, 0)))
    nbr = 0.5 * (pad[:, :-2] + pad[:, 2:])
    return p + mu * (nbr - p)


def rope(x, T):
    inv = 1.0 / (ROPE_THETA ** (jnp.arange(0, HD, 2, dtype=jnp.float32) / HD))
    ang = jnp.arange(T, dtype=jnp.float32)[:, None] * inv[None, :]
    cos = jnp.cos(ang)[None, :, None, :]
    sin = jnp.sin(ang)[None, :, None, :]
    x1, x2 = x[..., :HD // 2], x[..., HD // 2:]
    return jnp.concatenate([x1 * cos - x2 * sin, x2 * cos + x1 * sin], axis=-1)


def rwkv_step(S, inp):
    r, w, k, v, kk, akk = inp
    s_kk = jnp.einsum('dbhvk,dbhk->dbhv', S, kk)
    S = S * w[..., None, :] - s_kk[..., None] * akk[..., None, :] + v[..., None] * k[..., None, :]
    y = jnp.einsum('dbhvk,dbhk->dbhv', S, r)
    return S, y


def rwkv7_bidir(slab, w0, w_lora_up, a0, a_lora_up, k_k, k_a, r_k, ln_g, ln_b):
    B, T, _ = slab.shape
    f = jnp.swapaxes(slab, 0, 1).astype(jnp.float32)
    r, k, v, wd, ad = jnp.split(f, [D_A, 2 * D_A, 3 * D_A, 3 * D_A + 2 * R_W], axis=-1)

    def both(fwd, bwd):
        return jnp.stack([fwd, bwd[::-1]], axis=1)

    r2, k2, v2 = both(r, r), both(k, k), both(v, v)
    wd2 = both(wd[..., :R_W], wd[..., R_W:])
    ad2 = both(ad[..., :R_A], ad[..., R_A:])
    w_log = -jax.nn.softplus(-(w0.astype(jnp.float32)[None, :, None, :]
                               + jnp.einsum('tdbr,drc->tdbc', jnp.tanh(wd2), w_lora_up.astype(jnp.float32)))) - 0.5
    decay = jnp.exp(-jnp.exp(w_log))
    a = jax.nn.sigmoid(a0.astype(jnp.float32)[None, :, None, :]
                       + jnp.einsum('tdbr,drc->tdbc', ad2, a_lora_up.astype(jnp.float32)))
    heads = lambda z: z.reshape(T, 2, B, H_A, HEAD_A)
    kk = heads(k2 * k_k.astype(jnp.float32))
    kk = kk / jnp.maximum(jnp.sqrt(jnp.sum(kk * kk, axis=-1, keepdims=True)), 1e-12)
    kmod = heads(k2 * (1.0 + (a - 1.0) * k_a.astype(jnp.float32)))
    r2h, v2h, ah, dh = heads(r2), heads(v2), heads(a), heads(decay)
    bonus2 = jnp.sum(r2h * kmod * r_k.astype(jnp.float32), axis=-1, keepdims=True) * v2h
    S0 = jnp.zeros((2, B, H_A, HEAD_A, HEAD_A), jnp.float32)
    _, y2 = lax.scan(rwkv_step, S0, (r2h, dh, kmod, v2h, kk, kk * ah))
    y = y2[:, 0] + y2[::-1, 1]
    bonus = bonus2[:, 0] + bonus2[::-1, 1]
    mu = jnp.mean(y, axis=-1, keepdims=True)
    var = jnp.mean(jnp.square(y - mu), axis=-1, keepdims=True)
    yn = ((y - mu) * lax.rsqrt(var + GN_EPS)).reshape(T, B, D_A)
    yn = yn * ln_g.astype(jnp.float32) + ln_b.astype(jnp.float32) + bonus.reshape(T, B, D_A)
    return jnp.swapaxes(yn, 0, 1)


def banded_attention(q, k, v, q_g, k_g, sink):
    B, T, _ = q.shape
    q = q.astype(jnp.float32).reshape(B, T, HQ, HD)
    k = k.astype(jnp.float32).reshape(B, T, HKV, HD)
    v = v.astype(jnp.float32).reshape(B, T, HKV, HD)
    q = rope(rms_norm(q, q_g), T) * (HD ** -0.5)
    k = rope(rms_norm(k, k_g), T)
    NB = T // BLOCK
    CTX = BLOCK + 2 * WINDOW
    qb = jnp.transpose(q.reshape(B, NB, BLOCK, HKV, GRP, HD), (1, 0, 2, 3, 4, 5))
    kpad = jnp.pad(k, ((0, 0), (WINDOW, WINDOW), (0, 0), (0, 0)))
    vpad = jnp.pad(v, ((0, 0), (WINDOW, WINDOW), (0, 0), (0, 0)))
    sink_f = sink.astype(jnp.float32).reshape(HKV, GRP)[None, :, :, None, None]

    def one_block(args):
        n, qn = args
        start = n * BLOCK
        ks = lax.dynamic_slice_in_dim(kpad, start, CTX, axis=1)
        vs = lax.dynamic_slice_in_dim(vpad, start, CTX, axis=1)
        s = jnp.einsum('bqhgd,bkhd->bhgqk', qn, ks)
        qpos = start + jnp.arange(BLOCK)
        kpos = start - WINDOW + jnp.arange(CTX)
        valid = (jnp.abs(qpos[:, None] - kpos[None, :]) <= WINDOW) & (kpos >= 0)[None, :] & (kpos < T)[None, :]
        s = jnp.where(valid, s, -jnp.inf)
        m = jnp.maximum(jnp.max(s, axis=-1, keepdims=True), sink_f)
        p = jnp.exp(s - m)
        denom = jnp.sum(p, axis=-1, keepdims=True) + jnp.exp(sink_f - m)
        return jnp.einsum('bhgqk,bkhd->bqhgd', p / denom, vs)

    o = lax.map(one_block, (jnp.arange(NB), qb))
    return jnp.transpose(o, (1, 0, 2, 3, 4, 5)).reshape(B, T, D_B)


def layer(x, norm_g, w_in, shift_mu, w0, w_lora_up, a0, a_lora_up, k_k, k_a, r_k, ln_g, ln_b,
          q_g, k_g, sink, w_proj_a, w_proj_b, w_out):
    h = rms_norm(x, norm_g)
    p = jnp.einsum('btd,dc->btc', h, w_in)
    slab, gate_a, q, k, v, gate_b, merge = jnp.split(p, SPLITS, axis=-1)
    slab = centred_shift(slab, shift_mu)
    ya = rwkv7_bidir(slab, w0, w_lora_up, a0, a_lora_up, k_k, k_a, r_k, ln_g, ln_b)
    ya = jnp.einsum('btc,cd->btd', (ya * jax.nn.silu(gate_a.astype(jnp.float32))).astype(x.dtype), w_proj_a)
    yb = banded_attention(q, k, v, q_g, k_g, sink)
    yb = jnp.einsum('btc,cd->btd', (yb * jax.nn.silu(gate_b.astype(jnp.float32))).astype(x.dtype), w_proj_b)
    g_a, g_b = jnp.split(merge, 2, axis=-1)
    mixed = jax.nn.sigmoid(g_a) * ya + jax.nn.sigmoid(g_b) * yb
    return x + jnp.einsum('btd,de->bte', mixed, w_out)


def trunk(x, norm_g, w_in, shift_mu, w0, w_lora_up, a0, a_lora_up, k_k, k_a, r_k, ln_x_g, ln_x_b,
          q_norm_g, k_norm_g, sink, w_proj_a, w_proj_b, w_out):
    for l in range(DEPTH):
        x = layer(x, norm_g[l], w_in[l], shift_mu[l], w0[l], w_lora_up[l], a0[l], a_lora_up[l],
                  k_k[l], k_a[l], r_k[l], ln_x_g[l], ln_x_b[l], q_norm_g[l], k_norm_g[l], sink[l],
                  w_proj_a[l], w_proj_b[l], w_out[l])
    return x


def setup_inputs(seed: int = 0) -> dict:
    key = jax.random.key(seed)
    ks = jax.random.split(key, 20)
    n = lambda i, shape: jax.random.normal(ks[i], shape, jnp.float32)
    L = DEPTH
    return {
        "x_prompt": n(0, (BATCH, SEQ, D_MODEL)),
        "x_sample": n(1, (DEC_BATCH, DEC_SEQ, D_MODEL)),
        "norm_g": 1.0 + 0.05 * n(2, (L, D_MODEL)),
        "w_in": n(3, (L, D_MODEL, N_IN)) * D_MODEL ** -0.5,
        "shift_mu": jax.random.uniform(ks[4], (L, C_SHIFT), jnp.float32),
        "w0": -1.5 + n(5, (L, 2, D_A)),
        "w_lora_up": n(6, (L, 2, R_W, D_A)) * 0.3 * R_W ** -0.5,
        "a0": 0.5 * n(7, (L, 2, D_A)),
        "a_lora_up": n(8, (L, 2, R_A, D_A)) * 0.3 * R_A ** -0.5,
        "k_k": 0.85 + 0.05 * n(9, (L, D_A)),
        "k_a": 1.0 + 0.05 * n(10, (L, D_A)),
        "r_k": 0.1 * n(11, (L, H_A, HEAD_A)),
        "ln_x_g": 1.0 + 0.05 * n(12, (L, D_A)),
        "ln_x_b": 0.02 * n(13, (L, D_A)),
        "q_norm_g": 1.0 + 0.05 * n(14, (L, HD)),
        "k_norm_g": 1.0 + 0.05 * n(15, (L, HD)),
        "sink": 0.5 * n(16, (L, HQ)),
        "w_proj_a": n(17, (L, D_A, D_MODEL)) * D_A ** -0.5,
        "w_proj_b": n(18, (L, D_B, D_MODEL)) * D_B ** -0.5,
        "w_out": n(19, (L, D_MODEL, D_MODEL)) * D_MODEL ** -0.5,
    }


def reference(x_prompt, x_sample, norm_g, w_in, shift_mu, w0, w_lora_up, a0, a_lora_up, k_k, k_a, r_k,
              ln_x_g, ln_x_b, q_norm_g, k_norm_g, sink, w_proj_a, w_proj_b, w_out):
    y_prompt = trunk(x_prompt, norm_g, w_in, shift_mu, w0, w_lora_up, a0, a_lora_up, k_k, k_a, r_k,
                     ln_x_g, ln_x_b, q_norm_g, k_norm_g, sink, w_proj_a, w_proj_b, w_out)
    y_sample = trunk(x_sample, norm_g, w_in, shift_mu, w0, w_lora_up, a0, a_lora_up, k_k, k_a, r_k,
                     ln_x_g, ln_x_b, q_norm_g, k_norm_g, sink, w_proj_a, w_proj_b, w_out)
    return (y_prompt, y_sample)
```

```python
import numpy as np
from contextlib import ExitStack
import concourse.bass as bass
import concourse.mybir as mybir
from concourse.bass_utils import run_bass_kernel_spmd

F32 = mybir.dt.float32
BF16 = mybir.dt.bfloat16
AF = mybir.ActivationFunctionType
ALU = mybir.AluOpType
AX = mybir.AxisListType

D = 1024
NIN = 8960
C = 128
NORM_EPS = 1e-6
GN_EPS = 64e-5
NEG_E = -float(np.exp(-0.5))

PC_NG = 0
PC_MU = 8
PC_MUWD = 32
PC_MUAD = 34
PC_KK = 36
PC_KA = 44
PC_RK = 52
NPC = 60
BR_LNG = 0
BR_LNB = 1024
BR_QG = 2048
BR_KG = 2112
BR_SINK = 2176
NBR = 2192
CO_ID = 0
CO_MTF = 128
CO_MTB = CO_MTF + 512
CO_M3F = CO_MTB + 512
CO_M3B = CO_M3F + 128
CO_TRF = CO_M3B + 128
CO_TRB = CO_TRF + 256
CO_BO = CO_TRB + 256
CO_HS = CO_BO + 128
CO_MPREV = CO_HS + 2
CO_MNEXT = CO_MPREV + 128
NCO = CO_MNEXT + 128


def make_consts():
    c = np.zeros((128, NCO), np.float32)
    i = np.arange(128)
    s, t = i[:, None], i[None, :]
    c[:, CO_ID:CO_ID + 128] = (s == t)
    for (o, lt, le) in ((CO_MTF, s < t, s <= t), (CO_MTB, s > t, s >= t)):
        c[:, o:o + 128] = lt
        c[:, o + 128:o + 256] = le
        c[:, o + 256:o + 384] = -1.0 * le
        c[:, o + 384:o + 512] = -1.0 * lt
    c[:, CO_M3F:CO_M3F + 128] = -1.0 * (t < s)
    c[:, CO_M3B:CO_M3B + 128] = -1.0 * (t > s)
    c[:, CO_TRF:CO_TRF + 128] = NEG_E * (s <= t)
    c[:, CO_TRF + 128:CO_TRF + 256] = NEG_E * (s < t)
    c[:, CO_TRB:CO_TRB + 128] = NEG_E * (s >= t)
    c[:, CO_TRB + 128:CO_TRB + 256] = NEG_E * (s > t)
    c[:, CO_BO:CO_BO + 128] = (s // 64 == t // 64)
    c[:, CO_HS] = (i // 64 == 0)
    c[:, CO_HS + 1] = (i // 64 == 1)
    c[:, CO_MPREV:CO_MPREV + 128] = (s >= t)
    c[:, CO_MNEXT:CO_MNEXT + 128] = (s <= t)
    return c


def make_rope(tmax):
    inv = 1.0 / (10000.0 ** (np.arange(0, 64, 2, dtype=np.float32) / 64))
    ang = np.arange(tmax, dtype=np.float32)[:, None] * inv[None, :].astype(np.float32)
    return np.concatenate([np.cos(ang), np.sin(ang)], axis=1).astype(np.float32)


class Buf:
    __slots__ = ("name", "writers", "readers", "base", "excl")

    def __init__(self, name, excl=False):
        self.name = name
        self.excl = excl
        self.writers = {}
        self.readers = {}
        self.base = {}


class Eng:
    def __init__(self, name, h, sem):
        self.name = name
        self.h = h
        self.sem = sem
        self.count = 0
        self.waited = {}


class DmaQ:
    def __init__(self, eng, sems):
        self.eng = eng
        self.sems = sems
        self.n = 0


class Sched:
    def __init__(self, nc, es, n_dma_sems=8):
        self.nc = nc
        mk = lambda nm: es.enter_context(nc.semaphore(nm))
        self.pe = Eng("pe", nc.tensor, mk("s_pe"))
        self.act = Eng("act", nc.scalar, mk("s_act"))
        self.dve = Eng("dve", nc.vector, mk("s_dve"))
        self.pool = Eng("pool", nc.gpsimd, mk("s_pool"))
        self.sp = Eng("sp", nc.sync, mk("s_sp"))
        self.engs = (self.pe, self.act, self.dve, self.pool, self.sp)
        self.sems = {}
        for e in self.engs:
            self.sems[id(e.sem)] = e.sem
        self.q_ld = DmaQ(self.sp, [mk(f"s_ld{i}") for i in range(n_dma_sems)])
        self.q_st = DmaQ(self.pool, [mk(f"s_st{i}") for i in range(n_dma_sems)])
        self.q_l2 = DmaQ(self.act, [mk(f"s_lb{i}") for i in range(n_dma_sems)])
        self.queues = (self.q_ld, self.q_st, self.q_l2)
        for q in self.queues:
            for s in q.sems:
                self.sems[id(s)] = s
        self.n_inst = 0

    def _wait(self, eng, deps, same_ok=True):
        for sid, val in deps.items():
            if eng.waited.get(sid, 0) >= val:
                continue
            if same_ok and sid == id(eng.sem):
                continue
            eng.h.wait_ge(self.sems[sid], val)
            eng.waited[sid] = val

    @staticmethod
    def _merge(d, o):
        for k, v in o.items():
            if d.get(k, 0) < v:
                d[k] = v

    def _deps(self, reads, writes, acc):
        deps = {}
        for b in reads:
            self._merge(deps, b.writers)
            if b.excl:
                self._merge(deps, b.readers)
        for b in writes:
            self._merge(deps, b.readers)
            if not acc:
                self._merge(deps, b.writers)
            else:
                self._merge(deps, b.base)
        return deps

    def _commit(self, reads, writes, tok, acc, deps=None):
        sid, val = tok
        for b in reads:
            if b.readers.get(sid, 0) < val:
                b.readers[sid] = val
        for b in writes:
            if acc:
                if b.writers.get(sid, 0) < val:
                    b.writers[sid] = val
            else:
                b.writers = {sid: val}
                b.readers = {}
                b.base = dict(deps or {})

    def op(self, eng, fn, reads=(), writes=(), acc=False):
        deps = self._deps(reads, writes, acc)
        self._wait(eng, deps, same_ok=(eng is self.pe))
        inst = fn()
        eng.count += 1
        inst.then_inc(eng.sem, 1)
        self._commit(reads, writes, (id(eng.sem), eng.count), acc, deps)
        self.n_inst += 1
        return inst

    def dma(self, q, out, in_, reads=(), writes=(), acc=False, **kw):
        eng = q.eng
        K = len(q.sems)
        i = q.n
        sem = q.sems[i % K]
        val = 16 * (i // K + 1)
        deps = self._deps(reads, writes, acc)
        if i >= K:
            self._merge(deps, {id(sem): val - 16})
        self._wait(eng, deps, same_ok=False)
        eng.h.dma_start(out=out, in_=in_, **kw).then_inc(sem, 16)
        q.n += 1
        self._commit(reads, writes, (id(sem), val), acc, deps)
        self.n_inst += 1

    def all_tokens(self):
        deps = {}
        for e in self.engs:
            if e.count:
                deps[id(e.sem)] = e.count
        for q in self.queues:
            K = len(q.sems)
            for j, s in enumerate(q.sems):
                n = (q.n - j + K - 1) // K if q.n > j else 0
                if n:
                    deps[id(s)] = 16 * n
        return deps

    def barrier(self):
        deps = self.all_tokens()
        for e in self.engs:
            self._wait(e, deps, same_ok=True)

    def finish(self):
        self._wait(self.sp, self.all_tokens(), same_ok=False)


class Cfg:
    def __init__(self, TP, TS, depth, debug=False):
        self.TP, self.TS, self.depth, self.debug = TP, TS, depth, debug
        self.p2stage = 99
        self.TA = TP + TS
        self.seqs = [(0, TP), (TP, TS)]


def build(cfg):
    nc = bass.Bass("TRN2", target_bir_lowering=False)
    L = cfg.depth
    TA = cfg.TA
    dt_in = lambda name, shape: nc.dram_tensor(name, shape, F32, kind="ExternalInput").ap()
    xp = dt_in("xp", [cfg.TP, D])
    xs = dt_in("xs", [cfg.TS, D])
    w_in = dt_in("w_in", [L, D, NIN])
    wpa_d = dt_in("w_proj_a", [L, D, D])
    wpb_d = dt_in("w_proj_b", [L, D, D])
    wo_d = dt_in("w_out", [L, D, D])
    pcol_d = dt_in("pcol", [L, 128, NPC])
    brow_d = dt_in("brow", [L, NBR])
    wup_d = dt_in("wup_aug", [L, 2, 65, D])
    aup_d = dt_in("aup_aug", [L, 2, 65, D])
    consts_d = dt_in("consts", [128, NCO])
    rope_d = dt_in("rope", [max(cfg.TP, cfg.TS), 64])
    yp = nc.dram_tensor("yp", [cfg.TP, D], F32, kind="ExternalOutput").ap()
    ys = nc.dram_tensor("ys", [cfg.TS, D], F32, kind="ExternalOutput").ap()
    dbg_kind = "ExternalOutput" if cfg.debug else "Internal"
    _pt_parts = [(0, 3328, nc.dram_tensor("pT_slab", [3328, TA], F32, kind=dbg_kind).ap()),
                 (3328, 4352, nc.dram_tensor("pT_ga", [1024, TA], F32).ap()),
                 (5888, 6912, nc.dram_tensor("pT_gb", [1024, TA], F32).ap()),
                 (6912, 8960, nc.dram_tensor("pT_mg", [2048, TA], F32).ap())]

    class _PT:
        def __getitem__(self, key):
            rs, cs = key
            for (a, b, ap) in _pt_parts:
                if a <= rs.start and rs.stop <= b:
                    return ap[rs.start - a:rs.stop - a, cs]
            raise KeyError(key)
    pT = _PT()
    qkv = nc.dram_tensor("qkv", [TA, 1536], F32, kind=dbg_kind).ap()
    ysc = nc.dram_tensor("ysc", [4, TA, D], F32, kind=dbg_kind).ap()
    xsc = [nc.dram_tensor(f"xsc{i}", [TA, D], F32).ap() for i in range(2)]

    def x_src(l, g0, n):
        if l == 0:
            return xp[g0:g0 + n, :] if g0 < cfg.TP else xs[g0 - cfg.TP:g0 - cfg.TP + n, :]
        return xsc[(l - 1) % 2][g0:g0 + n, :]

    def x_dst(l, g0, n):
        if l == L - 1:
            return yp[g0:g0 + n, :] if g0 < cfg.TP else ys[g0 - cfg.TP:g0 - cfg.TP + n, :]
        return xsc[l % 2][g0:g0 + n, :]

    with ExitStack() as es:
        S = Sched(nc, es)
        pe, act, dve, pool = S.pe, S.act, S.dve, S.pool
        V, G, A, PE = nc.vector, nc.gpsimd, nc.scalar, nc.tensor

        def EH(e):
            return {"dve": V, "pool": G, "act": A}[e.name]

        def tt(e, out, in0, in1, op, R, W, acc=False):
            h = EH(e)
            S.op(e, lambda: h.tensor_tensor(out=out, in0=in0, in1=in1, op=op), R, W, acc)

        def ts(e, out, in0, s1, op0, R, W, s2=None, op1=None, acc=False):
            h = EH(e)
            if op1 is None:
                S.op(e, lambda: h.tensor_scalar(out=out, in0=in0, scalar1=s1, scalar2=None, op0=op0), R, W, acc)
            else:
                S.op(e, lambda: h.tensor_scalar(out=out, in0=in0, scalar1=s1, scalar2=s2, op0=op0, op1=op1), R, W, acc)

        def actf(out, in_, func, R, W, bias=0.0, scale=1.0, acc=False):
            S.op(act, lambda: A.activation(out=out, in_=in_, func=func, bias=bias, scale=scale), R, W, acc)

        def cp(e, out, in_, R, W, acc=False):
            if e is act:
                S.op(act, lambda: A.copy(out=out, in_=in_), R, W, acc)
            else:
                h = EH(e)
                S.op(e, lambda: h.tensor_copy(out=out, in_=in_), R, W, acc)

        def mm(out, lhsT, rhs, start, stop, R, W, acc=False, skip=False):
            if skip:
                S.op(pe, lambda: PE.matmul(out, lhsT, rhs, start=start, stop=stop, skip_group_check=True), R, W, acc)
            else:
                S.op(pe, lambda: PE.matmul(out, lhsT, rhs, start=start, stop=stop), R, W, acc)

        def trp(out, in_, ident, R, W):
            S.op(pe, lambda: PE.transpose(out, in_, ident), R, W)

        def ld(out, in_, W, R=(), q=None, **kw):
            S.dma(q or S.q_ld, out, in_, reads=R, writes=W, **kw)

        def st(out, in_, R, W, acc=True):
            S.dma(S.q_st, out, in_, reads=R, writes=W, acc=acc)

        uid = [0]

        def T_(st_, name, shape, dt=F32):
            uid[0] += 1
            return st_.enter_context(nc.sbuf_tensor(f"{name}_{uid[0]}", shape, dt))

        def P_(st_, name, shape, dt=F32):
            uid[0] += 1
            return st_.enter_context(nc.psum_tensor(f"{name}_{uid[0]}", shape, dt))
        cst = T_(es, "cst", [128, NCO])
        identb = T_(es, "identb", [128, 128], BF16)
        b_cst = Buf("cst")
        ld(cst[:], consts_d[:, :], [b_cst])
        cp(pool, identb[:], cst[:, CO_ID:CO_ID + 128], [b_cst], [b_cst])
        identf = cst[:, CO_ID:CO_ID + 128]
        b_pT, b_qkv, b_ysc = Buf("pT"), Buf("qkv"), Buf("ysc")
        b_x = [Buf("x_in"), Buf("xsc0"), Buf("xsc1"), Buf("y_out")]

        def bx_src(l):
            return b_x[0] if l == 0 else b_x[1 + (l - 1) % 2]

        def bx_dst(l):
            return b_x[3] if l == L - 1 else b_x[1 + l % 2]

        for l in range(L):
            with ExitStack() as ls:
                pcol = T_(ls, "pcol", [128, NPC])
                pder = T_(ls, "pder", [128, 64])
                b_par = Buf("par")
                S.barrier()
                ld(pcol[:], pcol_d[l, :, :], [b_par])
                ts(dve, pder[:, 0:24], pcol[:, PC_MU:PC_MU + 24], 0.5, ALU.mult, [b_par], [b_par])
                ts(dve, pder[:, 24:48], pcol[:, PC_MU:PC_MU + 24], -1.0, ALU.mult, [b_par], [b_par], 1.0, ALU.add)
                ts(dve, pder[:, 48:56], pcol[:, PC_KA:PC_KA + 8], -1.0, ALU.mult, [b_par], [b_par], 1.0, ALU.add)
                ts(dve, pder[:, 56:60], pcol[:, PC_MUWD:PC_MUWD + 4], 0.5, ALU.mult, [b_par], [b_par])
                ts(dve, pder[:, 60:64], pcol[:, PC_MUWD:PC_MUWD + 4], -1.0, ALU.mult, [b_par], [b_par], 1.0, ALU.add)

                phase1(nc, cfg, S, l, locals())
                if cfg.debug == "p1":
                    break
                phase2(nc, cfg, S, l, locals())
                if cfg.debug == "p2":
                    break
                phase3(nc, cfg, S, l, locals())
        S.barrier()
        S.finish()
    return nc


FM_BLOCKS = list(range(0, 34)) + list(range(46, 70))
TM_COL0 = 4352


def phase1(nc, cfg, S, l, E):
    g = lambda k: E[k]
    T_, P_, ld, st, tt, ts, actf, cp, mm, trp = (g(k) for k in
                                                   "T_ P_ ld st tt ts actf cp mm trp".split())
    pe, act, dve, pool = S.pe, S.act, S.dve, S.pool
    pcol, b_par, identb, b_cst = g("pcol"), g("b_par"), g("identb"), g("b_cst")
    w_in, pT, qkv, b_pT, b_qkv = g("w_in"), g("pT"), g("qkv"), g("b_pT"), g("b_qkv")
    x_src, bx_src = g("x_src"), g("bx_src")
    V, G, A, PE = nc.vector, nc.gpsimd, nc.scalar, nc.tensor
    TW = 512
    with ExitStack() as ps:
        S.barrier()
        wbf = T_(ps, "wbf", [128, 8, NIN], BF16)
        wst = [T_(ps, f"wst{i}", [128, 896]) for i in range(2)]
        hT2 = [T_(ps, f"hT{i}", [128, 8, TW], BF16) for i in range(2)]
        xt = [T_(ps, f"xt{i}", [128, D]) for i in range(2)]
        xsq = T_(ps, "xsq", [128, D])
        xn = [T_(ps, f"xn{i}", [128, D], BF16) for i in range(2)]
        ss = [T_(ps, f"ss{i}", [128, 2]) for i in range(2)]
        stg = [T_(ps, f"stg{i}", [128, TW]) for i in range(4)]
        ptr = [P_(ps, f"ptr{i}", [128, 8, 128], BF16) for i in range(2)]
        pout = [P_(ps, f"pout{i}", [128, TW]) for i in range(4)]
        b_w = Buf("wbf")
        b_wst = [Buf("wst0"), Buf("wst1")]
        b_hT2 = [Buf("hT0"), Buf("hT1")]
        b_xt = [Buf("xt0"), Buf("xt1")]
        b_xsq = Buf("xsq")
        b_xn = [Buf("xn0"), Buf("xn1")]
        b_ss = [Buf("ss0"), Buf("ss1")]
        b_stg = [Buf(f"stg{i}") for i in range(4)]
        b_ptr = [Buf("ptr0", True), Buf("ptr1", True)]
        b_pout = [Buf(f"pout{i}", True) for i in range(4)]
        k = 0
        cast_engs = (act, pool, dve)
        for kc in range(8):
            for cc in range(10):
                i = k % 2
                ld(wst[i][:], w_in[l, kc * 128:(kc + 1) * 128, cc * 896:(cc + 1) * 896], [b_wst[i]])
                cp(cast_engs[k % 3], wbf[:, kc, cc * 896:(cc + 1) * 896], wst[i][:], [b_wst[i]], [b_w], acc=True)
                k += 1
        gcol = pcol[:, PC_NG:PC_NG + 8]
        n_tiles = cfg.TA // TW
        sub = 0
        oi = 0
        for ti in range(n_tiles):
            g0 = ti * TW
            hT, b_hT = hT2[ti % 2], b_hT2[ti % 2]
            for s4 in range(TW // 128):
                i = sub % 2
                sub += 1
                gs = g0 + s4 * 128
                ld(xt[i][:], x_src(l, gs, 128), [b_xt[i]], R=[bx_src(l)])
                actf(xsq[:], xt[i][:], AF.Square, [b_xt[i]], [b_xsq])
                S.op(dve, lambda: V.reduce_sum(out=ss[i][:, 0:1], in_=xsq[:], axis=AX.X), [b_xsq], [b_ss[i]])
                actf(ss[i][:, 1:2], ss[i][:, 0:1], AF.Sqrt, [b_ss[i]], [b_ss[i]], bias=NORM_EPS, scale=1.0 / D)
                S.op(dve, lambda: V.reciprocal(out=ss[i][:, 1:2], in_=ss[i][:, 1:2]), [b_ss[i]], [b_ss[i]])
                ts(dve, xn[i][:], xt[i][:], ss[i][:, 1:2], ALU.mult, [b_xt[i], b_ss[i]], [b_xn[i]])
                for kc in range(8):
                    trp(ptr[i][:, kc, :], xn[i][:, kc * 128:(kc + 1) * 128], identb[:], [b_xn[i], b_cst], [b_ptr[i]])
                tt(dve, hT[:, :, s4 * 128:(s4 + 1) * 128], ptr[i][:],
                   gcol.unsqueeze(2).to_broadcast([128, 8, 128]), ALU.mult, [b_ptr[i], b_par], [b_hT])
            for fb in FM_BLOCKS:
                o = oi % 4
                oi += 1
                for kc in range(8):
                    mm(pout[o][:], wbf[:, kc, fb * 128:(fb + 1) * 128], hT[:, kc, :], kc == 0, kc == 7,
                       [b_w, b_hT], [b_pout[o]])
                cp(act if o % 2 == 0 else dve, stg[o][:], pout[o][:], [b_pout[o]], [b_stg[o]])
                st(pT[fb * 128:(fb + 1) * 128, g0:g0 + TW], stg[o][:], [b_stg[o]], [b_pT])
            for s4 in range(TW // 128):
                for cg in range(3):
                    o = oi % 4
                    oi += 1
                    for kc in range(8):
                        mm(pout[o][:], hT[:, kc, s4 * 128:(s4 + 1) * 128],
                           wbf[:, kc, TM_COL0 + cg * 512:TM_COL0 + (cg + 1) * 512], kc == 0, kc == 7,
                           [b_w, b_hT], [b_pout[o]])
                    cp(act if o % 2 == 0 else dve, stg[o][:], pout[o][:], [b_pout[o]], [b_stg[o]])
                    st(qkv[g0 + s4 * 128:g0 + (s4 + 1) * 128, cg * 512:(cg + 1) * 512], stg[o][:],
                       [b_stg[o]], [b_qkv])
        S.barrier()


def phase2(nc, cfg, S, l, E):
    g = lambda k: E[k]
    T_, P_, ld, st, tt, ts, actf, cp, mm, trp = (g(k) for k in
                                                   "T_ P_ ld st tt ts actf cp mm trp".split())
    pe, act, dve, pool = S.pe, S.act, S.dve, S.pool
    pcol, pder, b_par, identb, b_cst, cst = (g(k) for k in "pcol pder b_par identb b_cst cst".split())
    pT, b_pT, ysc, b_ysc = g("pT"), g("b_pT"), g("ysc"), g("b_ysc")
    wup_d, aup_d = g("wup_d"), g("aup_d")
    V, G, A, PE = nc.vector, nc.gpsimd, nc.scalar, nc.tensor
    identf = cst[:, CO_ID:CO_ID + 128]
    BO = cst[:, CO_BO:CO_BO + 128]
    HS = cst[:, CO_HS:CO_HS + 2]
    C2, C4 = 2 * C, 4 * C
    NG = 4
    with ExitStack() as ps:
        S.barrier()
        wup = T_(ps, "wup", [65, 2, D])
        aup = T_(ps, "aup", [65, 2, D])
        H = T_(ps, "H", [128, 4, 8, 64])
        Hq = T_(ps, "Hq", [128, 4, 8, 64], BF16)
        slab = T_(ps, "slab", [128, 24, C + 2])
        wdt = T_(ps, "wdt", [64, C + 2])
        adt = T_(ps, "adt", [64, C + 2])
        wtmp = T_(ps, "wtmp", [64, C])
        atmp = T_(ps, "atmp", [64, C])
        wds = T_(ps, "wds", [65, C])
        ads = T_(ps, "ads", [65, C])
        sg = T_(ps, "sg", [128, D])
        eN = T_(ps, "eN", [128, 8, C])
        eX = T_(ps, "eX", [128, 8, C])
        aT = T_(ps, "aT", [128, 8, C])
        tmp24 = T_(ps, "tmp24", [128, 24, C])
        sh24 = T_(ps, "sh24", [128, 24, C])
        kkn = T_(ps, "kkn", [128, 8, C])
        sq = T_(ps, "sq", [128, 8, C])
        bb = T_(ps, "bb", [128, 8, C])
        kmod = T_(ps, "kmod", [128, 8, C])
        prod = T_(ps, "prod", [128, 8, C])
        cc = T_(ps, "cc", [128, 16])
        bon = T_(ps, "bon", [128, D])
        eL2 = [T_(ps, f"eL{i}", [128, 8, C]) for i in range(2)]
        QR2 = [T_(ps, f"QR{i}", [128, 8, 3, C], BF16) for i in range(2)]
        KhT2 = [T_(ps, f"KhT{i}", [128, 8, C], BF16) for i in range(2)]
        BhT2 = [T_(ps, f"BhT{i}", [128, 8, C], BF16) for i in range(2)]
        Vb2 = [T_(ps, f"Vb{i}", [128, D], BF16) for i in range(2)]
        Kh2 = [T_(ps, f"Kh{i}", [128, D], BF16) for i in range(2)]
        Bhn2 = [T_(ps, f"Bhn{i}", [128, D], BF16) for i in range(2)]
        M12 = [T_(ps, f"M12_{i}", [128, C4 + C + 64], BF16) for i in range(NG)]
        XU = [[T_(ps, f"XU_{i}_{lv}", [128, C2 + 64], BF16) for lv in range(6)] for i in range(NG)]
        Uall = T_(ps, "Uall", [128, D], BF16)
        Ysb = T_(ps, "Ysb", [128, D])
        tmpH = T_(ps, "tmpH", [128, 4, 64])
        pr = P_(ps, "pr", [128, D])
        pSX = [P_(ps, f"pSX{i}", [128, 512]) for i in range(NG)]
        pY = P_(ps, "pY", [128, 512])
        pH = P_(ps, "pH", [128, 4, 128])
        B = {n: Buf(n) for n in ("lora H0 H1 H2 H3 Hq0 Hq1 Hq2 Hq3 slab wdt adt wtmp atmp wds ads sg eN eX aT "
                                 "tmp24_0 tmp24_1 tmp24_2 sh24_0 sh24_1 sh24_2 kkn sq bb kmod prod cc bon Uall Ysb tmpH "
                                 "eL0 eL1 QR0 QR1 KhT0 KhT1 BhT0 BhT1 Vb0 Vb1 Kh0 Kh1 Bhn0 Bhn1").split()}
        for n_ in ["pr0", "pr1", "pY", "pH"] + [f"pS{i}" for i in range(NG)]:
            B[n_] = Buf(n_, True)
        bM = [Buf(f"M12_{i}") for i in range(NG)]
        bXU = [[Buf(f"XU{i}{lv}") for lv in range(6)] for i in range(NG)]
        bPR = [B["pr0"], B["pr1"]]
        for d in range(2):
            ld(wup[:, d, :], wup_d[l, d, :, :], [B["lora"]], acc=True)
            ld(aup[:, d, :], aup_d[l, d, :, :], [B["lora"]], acc=True)
        for sd in range(4):
            S.op(pool, lambda: G.memset(H[:, sd, :, :], 0.0), (), [B[f"H{sd}"]])
            S.op(pool, lambda: G.memset(Hq[:, sd, :, :], 0.0), (), [B[f"Hq{sd}"]])
        S.op(pool, lambda: G.memset(wds[64:65, :], 1.0), (), [B["wds"]])
        S.op(pool, lambda: G.memset(ads[64:65, :], 1.0), (), [B["ads"]])

        hmu_bc = pder[:, 0:24].unsqueeze(2).to_broadcast([128, 24, C])
        omu_bc = pder[:, 24:48].unsqueeze(2).to_broadcast([128, 24, C])
        omka_bc = pder[:, 48:56].unsqueeze(2).to_broadcast([128, 8, C])
        kk_bc = pcol[:, PC_KK:PC_KK + 8].unsqueeze(2).to_broadcast([128, 8, C])
        ka_bc = pcol[:, PC_KA:PC_KA + 8].unsqueeze(2).to_broadcast([128, 8, C])
        rk_bc = pcol[:, PC_RK:PC_RK + 8].unsqueeze(2).to_broadcast([128, 8, C])

        class U_:
            pass

        def mk_unit(si, ch, d, up):
            u = U_()
            u.si, u.ch, u.d, u.up = si, ch, d, up
            u.s0, T = cfg.seqs[si]
            u.nch = T // C
            u.g0 = u.s0 + ch * C
            u.sd = si * 2 + d
            return u

        def prep(u):
            si, ch, d, up, g0, sd = u.si, u.ch, u.d, u.up, u.g0, u.sd
            eL, QR, KhT, BhT, Vb, Kh, Bhn = eL2[up], QR2[up], KhT2[up], BhT2[up], Vb2[up], Kh2[up], Bhn2[up]
            beL, bQR, bKhT, bBhT, bVb, bKh, bBhn = (B[f"{n}{up}"] for n in ("eL", "QR", "KhT", "BhT", "Vb", "Kh", "Bhn"))
            TRo = CO_TRF if d == 0 else CO_TRB
            TRinc = cst[:, TRo:TRo + C]
            TRexc = cst[:, TRo + C:TRo + C2]
            first, lastc = (ch == 0), (ch == u.nch - 1)
            lo = 1 if first else 0
            hi = C + 1 if lastc else C + 2
            if first:
                S.op(pool, lambda: G.memset(slab[:, :, 0:1], 0.0), (), [B["slab"]])
                S.op(pool, lambda: G.memset(wdt[:, 0:1], 0.0), (), [B["wdt"]])
                S.op(pool, lambda: G.memset(adt[:, 0:1], 0.0), (), [B["adt"]])
            if lastc:
                S.op(pool, lambda: G.memset(slab[:, :, C + 1:C + 2], 0.0), (), [B["slab"]], acc=not first)
                S.op(pool, lambda: G.memset(wdt[:, C + 1:C + 2], 0.0), (), [B["wdt"]], acc=not first)
                S.op(pool, lambda: G.memset(adt[:, C + 1:C + 2], 0.0), (), [B["adt"]], acc=not first)
            c0 = g0 - 1 + lo
            c1 = g0 - 1 + hi
            ld(wdt[:, lo:hi], pT[3072 + d * 64:3072 + (d + 1) * 64, c0:c1], [B["wdt"]], R=[b_pT], acc=(first or lastc))
            ld(adt[:, lo:hi], pT[3200 + d * 64:3200 + (d + 1) * 64, c0:c1], [B["adt"]], R=[b_pT], acc=(first or lastc))
            for jb in range(6):
                src = pT[jb * 512:(jb + 1) * 512, c0:c1].rearrange("(j p) t -> p j t", p=128)
                ld(slab[:, jb * 4:(jb + 1) * 4, lo:hi], src, [B["slab"]], R=[b_pT], acc=(jb > 0 or first or lastc))
            yield
            for (src_t, tmp_t, dst_t, bs, bt, bd, co) in ((wdt, wtmp, wds, "wdt", "wtmp", "wds", d),
                                                          (adt, atmp, ads, "adt", "atmp", "ads", 2 + d)):
                tt(dve, tmp_t[:], src_t[:, 0:C], src_t[:, 2:C + 2], ALU.add, [B[bs]], [B[bt]])
                ts(dve, tmp_t[:], tmp_t[:], pder[0:64, 56 + co:57 + co], ALU.mult, [B[bt], b_par], [B[bt]])
                S.op(dve, lambda: V.scalar_tensor_tensor(out=dst_t[0:64, :], in0=src_t[:, 1:C + 1],
                                                         scalar=pder[0:64, 60 + co:61 + co], in1=tmp_t[:],
                                                         op0=ALU.mult, op1=ALU.add),
                     [B[bs], B[bt], b_par], [B[bd]])
            actf(wds[0:64, :], wds[0:64, :], AF.Tanh, [B["wds"]], [B["wds"]])
            yield
            for hf in range(2):
                mm(pr[:, hf * 512:(hf + 1) * 512], wds[:, :], wup[:, d, hf * 512:(hf + 1) * 512], True, True,
                   [B["wds"], B["lora"]], [bPR[hf]])
            actf(sg[:], pr[:], AF.Sigmoid, bPR, [B["sg"]])
            for (th, e1, e2) in ((1, pool, dve), (0, dve, pool), (2, pool, dve)):
                t8 = slice(th * 8, th * 8 + 8)
                bt_, bs_ = B[f"tmp24_{th}"], B[f"sh24_{th}"]
                hm = pder[:, th * 8:th * 8 + 8].unsqueeze(2).to_broadcast([128, 8, C])
                om = pder[:, 24 + th * 8:24 + th * 8 + 8].unsqueeze(2).to_broadcast([128, 8, C])
                tt(e1, tmp24[:, t8, :], slab[:, t8, 0:C], slab[:, t8, 2:C + 2], ALU.add, [B["slab"]], [bt_])
                tt(e1, tmp24[:, t8, :], tmp24[:, t8, :], hm, ALU.mult, [bt_, b_par], [bt_])
                tt(e2, sh24[:, t8, :], slab[:, t8, 1:C + 1], om, ALU.mult, [B["slab"], b_par], [bs_])
                tt(e2, sh24[:, t8, :], sh24[:, t8, :], tmp24[:, t8, :], ALU.add, [bs_, bt_], [bs_])
            rs, ks, vs = sh24[:, 0:8, :], sh24[:, 8:16, :], sh24[:, 16:24, :]
            yield
            for j in range(8):
                mm(pr[:, j * C:(j + 1) * C], sg[:, j * 128:(j + 1) * 128], TRinc, True, True,
                   [B["sg"], b_cst], [bPR[j // 4]])
            pr3 = pr[:].rearrange("p (j c) -> p j c", j=8)
            actf(eL[:], pr3, AF.Exp, bPR, [beL])
            actf(eN[:], pr3, AF.Exp, bPR, [B["eN"]], scale=-1.0)
            tt(pool, kkn[:], ks, kk_bc, ALU.mult, [B["sh24_1"], b_par], [B["kkn"]])
            yield
            for j in range(8):
                mm(pr[:, j * C:(j + 1) * C], sg[:, j * 128:(j + 1) * 128], TRexc, True, True,
                   [B["sg"], b_cst], [bPR[j // 4]])
            actf(eX[:], pr3, AF.Exp, bPR, [B["eX"]])
            yield
            for j in range(8):
                mm(pr[:, j * C:(j + 1) * C], aup[:, d, j * 128:(j + 1) * 128], ads[:, :], True, True,
                   [B["ads"], B["lora"]], [bPR[j // 4]])
            actf(aT[:], pr3, AF.Sigmoid, bPR, [B["aT"]])
            actf(sq[:], kkn[:], AF.Square, [B["kkn"]], [B["sq"]])
            yield
            sq2 = sq[:].rearrange("p j c -> p (j c)")
            for hf in range(2):
                mm(pr[:, hf * 512:(hf + 1) * 512], BO, sq2[:, hf * 512:(hf + 1) * 512], True, True,
                   [B["sq"], b_cst], [bPR[hf]])
            actf(sq[:], pr3, AF.Ln, bPR + [B["sq"]], [B["sq"]], bias=1e-24)
            actf(sq[:], sq[:], AF.Exp, [B["sq"]], [B["sq"]], scale=-0.5)
            tt(dve, kmod[:], aT[:], ka_bc, ALU.mult, [B["aT"], b_par], [B["kmod"]])
            tt(dve, kmod[:], kmod[:], omka_bc, ALU.add, [B["kmod"], b_par], [B["kmod"]])
            tt(pool, kmod[:], kmod[:], ks, ALU.mult, [B["kmod"], B["sh24_1"]], [B["kmod"]])
            tt(pool, kkn[:], kkn[:], sq[:], ALU.mult, [B["kkn"], B["sq"]], [B["kkn"]])
            tt(pool, bb[:], kkn[:], aT[:], ALU.mult, [B["kkn"], B["aT"]], [B["bb"]])
            tt(pool, prod[:], rs, kmod[:], ALU.mult, [B["sh24_0"], B["kmod"]], [B["prod"]])
            tt(pool, prod[:], prod[:], rk_bc, ALU.mult, [B["prod"], b_par], [B["prod"]])
            yield
            for j in range(8):
                mm(pr[:, 2 * j:2 * j + 2], prod[:, j, :], HS, True, True, [B["prod"], b_cst], [bPR[0]])
            cp(act, cc[:], pr[:, 0:16], [bPR[0]], [B["cc"]])
            tt(dve, QR[:, :, 0, :], kkn[:], eX[:], ALU.mult, [B["kkn"], B["eX"]], [bQR])
            tt(dve, QR[:, :, 1, :], rs, eL[:], ALU.mult, [B["sh24_0"], beL], [bQR], acc=True)
            tt(pool, QR[:, :, 2, :], kkn[:], eX[:], ALU.mult, [B["kkn"], B["eX"]], [bQR], acc=True)
            tt(pool, KhT[:], kmod[:], eN[:], ALU.mult, [B["kmod"], B["eN"]], [bKhT])
            tt(pool, BhT[:], bb[:], eN[:], ALU.mult, [B["bb"], B["eN"]], [bBhT])
            yield
            for j in range(8):
                trp(pr[:, j * 128:(j + 1) * 128], vs[:, j, :], identf, [B["sh24_2"], b_cst], [bPR[j // 4]])
            cp(act, Vb[:], pr[:], bPR, [bVb])
            tt(dve, bon[:].rearrange("p (h v) -> p h v", h=16), pr[:].rearrange("p (h v) -> p h v", h=16),
               cc[:].unsqueeze(2).to_broadcast([128, 16, 64]), ALU.mult, bPR + [B["cc"]], [B["bon"]])
            st(ysc[2 + d, g0:g0 + C, :], bon[:], [B["bon"]], [b_ysc])
            yield
            for j in range(8):
                mm(pr[:, j * 128:(j + 1) * 128], KhT[:, j, :], identb[:], True, True, [bKhT, b_cst], [bPR[j // 4]])
            cp(act, Kh[:], pr[:], bPR, [bKh])
            yield
            for j in range(8):
                mm(pr[:, j * 128:(j + 1) * 128], BhT[:, j, :], identb[:], True, True, [bBhT, b_cst], [bPR[j // 4]])
            S.op(act, lambda: A.mul(out=Bhn[:], in_=pr[:], mul=-1.0), bPR, [bBhn])

        def heads(u):
            si, ch, d, up, g0, sd = u.si, u.ch, u.d, u.up, u.g0, u.sd
            eL, QR, KhT, BhT, Vb, Kh, Bhn = eL2[up], QR2[up], KhT2[up], BhT2[up], Vb2[up], Kh2[up], Bhn2[up]
            beL, bQR, bKhT, bBhT, bVb, bKh, bBhn = (B[f"{n}{up}"] for n in ("eL", "QR", "KhT", "BhT", "Vb", "Kh", "Bhn"))
            bH, bHq = B[f"H{sd}"], B[f"Hq{sd}"]
            MT = cst[:, (CO_MTF if d == 0 else CO_MTB):(CO_MTF if d == 0 else CO_MTB) + C4]
            M3 = cst[:, (CO_M3F if d == 0 else CO_M3B):(CO_M3F if d == 0 else CO_M3B) + C]
            last = C - 1 if d == 0 else 0

            def head(h, gi):
                j, par = h // 2, h % 2
                sl = slice(par * 64, par * 64 + 64)
                hs = slice(h * 64, (h + 1) * 64)
                bS = B[f"pS{gi}"]
                pb = pSX[gi]
                Mt = M12[gi]
                qr01 = QR[sl, j, 0:2, :].rearrange("p a c -> p (a c)")
                qr12 = QR[sl, j, 1:3, :].rearrange("p a c -> p (a c)")
                mm(pb[:, 0:C2], KhT[sl, j, :], qr01, True, True, [bKhT, bQR], [bS])
                mm(pb[:, C2:C4], BhT[sl, j, :], qr12, True, True, [bBhT, bQR], [bS])
                tt(dve, Mt[:, 0:C4], pb[:, :], MT, ALU.mult, [bS, b_cst], [bM[gi]])
                yield
                mm(pb[:, 0:C], QR[sl, j, 0, :], BhT[sl, j, :], True, True, [bQR, bBhT], [bS])
                mm(pb[:, 256:320], QR[sl, j, 0, :], Hq[sl, sd, j, :], True, False, [bQR, bHq], [bS])
                mm(pb[:, 256:320], Mt[:, 0:C], Vb[:, hs], False, True, [bM[gi], bVb], [bS])
                tt(dve, Mt[:, C4:C4 + C], pb[:, 0:C], M3, ALU.mult, [bS, b_cst], [bM[gi]], acc=True)
                cp(act, Mt[:, C4 + C:C4 + C + 64], pb[:, 256:320], [bS], [bM[gi]], acc=True)
                yield
                Tt, o, bT = Mt, 3 * C, bM[gi]
                for lv in range(7):
                    XT_, X_, U_ = Tt[:, o:o + C], Tt[:, o + C:o + C2], Tt[:, o + C2:o + C2 + 64]
                    if lv < 5:
                        mm(pb[:, C:C2 + 64], XT_, Tt[:, o + C:o + C2 + 64], True, True, [bT], [bS])
                        mm(pb[:, 0:C], X_, XT_, True, True, [bT], [bS])
                    elif lv == 5:
                        mm(pb[:, C2:C2 + 64], XT_, U_, True, True, [bT], [bS])
                        mm(pb[:, 0:C], X_, XT_, True, True, [bT], [bS])
                    else:
                        mm(pb[:, C2:C2 + 64], XT_, U_, True, True, [bT], [bS])
                    if lv < 6:
                        nt, bn = XU[gi][lv], bXU[gi][lv]
                        ev = act
                        cp(ev, nt[:, 0:(C2 if lv < 5 else C)], pb[:, 0:(C2 if lv < 5 else C)], [bS], [bn])
                        tt(dve, nt[:, C2:C2 + 64], pb[:, C2:C2 + 64], U_, ALU.add, [bS, bT], [bn], acc=True)
                        Tt, o, bT = nt, 0, bn
                    else:
                        tt(dve, Uall[:, hs], pb[:, C2:C2 + 64], U_, ALU.add, [bS, bT], [B["Uall"]], acc=True)
                    yield
                yo = (h % 8) * 64
                mm(pY[:, yo:yo + 64], QR[sl, j, 1, :], Hq[sl, sd, j, :], True, False, [bQR, bHq], [B["pY"]])
                mm(pY[:, yo:yo + 64], Mt[:, C:C2], Vb[:, hs], False, False, [bM[gi], bVb], [B["pY"]])
                mm(pY[:, yo:yo + 64], Mt[:, C2:3 * C], Uall[:, hs], False, True, [bM[gi], B["Uall"]], [B["pY"]])

            for q in range(16 // NG):
                gens = [head(q * NG + gi, gi) for gi in range(NG)]
                while gens:
                    for gen in list(gens):
                        try:
                            next(gen)
                        except StopIteration:
                            gens.remove(gen)
                    yield
                for j in range(q * NG // 2, (q + 1) * NG // 2):
                    if j % 4 == 0 and j > 0 or False:
                        pass
                    jj = j % 4
                    mm(pH[:, jj, :], Kh[:, j * 128:(j + 1) * 128], Vb[:, j * 128:(j + 1) * 128], True, False,
                       [bKh, bVb], [B["pH"]])
                    mm(pH[:, jj, :], Bhn[:, j * 128:(j + 1) * 128], Uall[:, j * 128:(j + 1) * 128], False, True,
                       [bBhn, B["Uall"]], [B["pH"]])
                    if jj == 3:
                        hb = j // 4
                        cp(act, Ysb[:, hb * 512:(hb + 1) * 512], pY[:], [B["pY"]], [B["Ysb"]], acc=(hb == 1))
                        j0 = j - 3
                        for hp in range(2):
                            psl = slice(hp * 64, hp * 64 + 64)
                            tt(dve, tmpH[psl, :, :], H[psl, sd, j0:j0 + 4, :], pH[psl, :, hp * 64:hp * 64 + 64],
                               ALU.add, [bH, B["pH"]], [B["tmpH"]], acc=(hp == 1))
                        for hp in range(2):
                            psl = slice(hp * 64, hp * 64 + 64)
                            tt(pool, H[psl, sd, j0:j0 + 4, :], tmpH[psl, :, :],
                               eL[psl, j0:j0 + 4, last:last + 1].to_broadcast([64, 4, 64]), ALU.mult,
                               [B["tmpH"], beL], [bH], acc=True)
                yield
            cp(act, Hq[:, sd, :, :], H[:, sd, :, :], [bH], [bHq])
            st(ysc[d, g0:g0 + C, :], Ysb[:], [B["Ysb"]], [b_ysc])

        nchs = [T // C for (_, T) in cfg.seqs]
        units = []
        for i in range(max(nchs)):
            for si in range(len(cfg.seqs)):
                if i < nchs[si]:
                    units.append((si, i, 0))
                    units.append((si, nchs[si] - 1 - i, 1))

        def drain(gen):
            for _ in gen:
                pass

        prev = None
        for ui, (si, ch, d) in enumerate(units):
            u = mk_unit(si, ch, d, ui % 2)
            pg = prep(u)
            if prev is None:
                drain(pg)
            else:
                hg = heads(prev)
                alive = [hg, pg]
                while alive:
                    for gen in list(alive):
                        try:
                            next(gen)
                        except StopIteration:
                            alive.remove(gen)
            prev = u
        drain(heads(prev))
        S.barrier()


def phase3(nc, cfg, S, l, E):
    g = lambda k: E[k]
    T_, P_, ld, st, tt, ts, actf, cp, mm, trp = (g(k) for k in
                                                   "T_ P_ ld st tt ts actf cp mm trp".split())
    pe, act, dve, pool = S.pe, S.act, S.dve, S.pool
    pcol, pder, b_par, identb, b_cst, cst = (g(k) for k in "pcol pder b_par identb b_cst cst".split())
    brow_d = g("brow_d")
    pT, b_pT, ysc, b_ysc, qkv, b_qkv = (g(k) for k in "pT b_pT ysc b_ysc qkv b_qkv".split())
    wpa_d, wpb_d, wo_d, rope_d = g("wpa_d"), g("wpb_d"), g("wo_d"), g("rope_d")
    x_src, x_dst, bx_src, bx_dst = g("x_src"), g("x_dst"), g("bx_src"), g("bx_dst")
    V, G, A, PE = nc.vector, nc.gpsimd, nc.scalar, nc.tensor
    identf = cst[:, CO_ID:CO_ID + 128]
    MPREV = cst[:, CO_MPREV:CO_MPREV + 128]
    MNEXT = cst[:, CO_MNEXT:CO_MNEXT + 128]
    with ExitStack() as ps:
        S.barrier()
        brow = T_(ps, "brow", [128, NBR])
        bder = T_(ps, "bder", [128, 64 + 16])
        ld(brow[:], brow_d[l, :].partition_broadcast(128), [b_par], acc=True)
        ts(dve, bder[:, 0:64], brow[:, BR_QG:BR_QG + 64], 0.125, ALU.mult, [b_par], [b_par], acc=True)
        actf(bder[:, 64:80], brow[:, BR_SINK:BR_SINK + 16], AF.Exp, [b_par], [b_par], acc=True)
        wts = [T_(ps, nm, [128, 8, D], BF16) for nm in ("wpa", "wpb", "wo")]
        wst = [T_(ps, f"wst3_{i}", [128, D]) for i in range(2)]
        yf, yb, bf_, bb_ = (T_(ps, nm, [128, D]) for nm in ("yf", "yb", "bf", "bb"))
        st16 = T_(ps, "st16", [128, 6, 16])
        gaT = T_(ps, "gaT", [128, 8, 128])
        yagT = T_(ps, "yagT", [128, 8, 128], BF16)
        qt = T_(ps, "qt", [128, D])
        tmpq = T_(ps, "tmpq", [128, D])
        qst = T_(ps, "qst", [128, 2, 16])
        qr = T_(ps, "qr", [128, D], BF16)
        qT = T_(ps, "qT", [64, 16, 128], BF16)
        kvraw = T_(ps, "kvraw", [128, 512])
        tmpk = T_(ps, "tmpk", [128, 256])
        kst = T_(ps, "kst", [128, 2, 4])
        kr = T_(ps, "kr", [128, 256], BF16)
        kT = [T_(ps, f"kT{i}", [64, 4, 128], BF16) for i in range(3)]
        vaug = [T_(ps, f"vaug{i}", [128, 4, 65], BF16) for i in range(3)]
        csq = T_(ps, "csq", [128, 64])
        csk = T_(ps, "csk", [128, 64])
        et = [T_(ps, f"et{i}", [128, 4, 128], BF16) for i in range(3)]
        den = T_(ps, "den", [128, 2, 4])
        og = T_(ps, "og", [128, D])
        gbT = T_(ps, "gbT", [128, 8, 128])
        ogT = T_(ps, "ogT", [128, 8, 128], BF16)
        mgT = T_(ps, "mgT", [128, 16, 128])
        t1 = T_(ps, "t1", [128, 8, 128])
        t2 = T_(ps, "t2", [128, 8, 128])
        mixT = T_(ps, "mixT", [128, 8, 128], BF16)
        xt = T_(ps, "xt3", [128, D])
        xo = T_(ps, "xo", [128, D])
        R0 = P_(ps, "R0", [128, D])
        R1 = P_(ps, "R1", [128, D])
        R2 = P_(ps, "R2", [128, D])
        R3 = P_(ps, "R3", [128, 512])
        R4 = P_(ps, "R4", [128, 512])
        names = ("w wst0 wst1 yf yb bf bb st16 gaT yagT qt tmpq qst qr qT kvraw tmpk kst kr kT0 kT1 kT2 "
                 "vaug0 vaug1 vaug2 csq csk et0 et1 et2 den og gbT ogT mgT t1 t2 mixT xt xo").split()
        B = {n: Buf(n) for n in names}
        for n_ in ("R0a", "R0b", "R1a", "R1b", "R2a", "R2b", "R3", "R4"):
            B[n_] = Buf(n_, True)
        bR0, bR1, bR2 = [B["R0a"], B["R0b"]], [B["R1a"], B["R1b"]], [B["R2a"], B["R2b"]]
        k = 0
        cast_engs = (act, pool, dve)
        for wi, wd_ in enumerate((wpa_d, wpb_d, wo_d)):
            for kc in range(8):
                i = k % 2
                ld(wst[i][:], wd_[l, kc * 128:(kc + 1) * 128, :], [B[f"wst{i}"]])
                cp(cast_engs[k % 3], wts[wi][:, kc, :], wst[i][:], [B[f"wst{i}"]], [B["w"]], acc=True)
                k += 1
        for i in range(3):
            S.op(pool, lambda: G.memset(vaug[i][:, :, 64:65], 1.0), (), [B[f"vaug{i}"]])
        wpa, wpb, wo = wts
        lng_bc = brow[:, BR_LNG:BR_LNG + D]
        lnb_bc = brow[:, BR_LNB:BR_LNB + D]
        qg8 = bder[:, 0:64]
        kg = brow[:, BR_KG:BR_KG + 64]
        esink = bder[:, 64:80]

        def v3(ap, h):
            return ap.rearrange("p (h v) -> p h v", h=h)

        def rope_norm(src3, nh, sq_t, st_t, cs_t, gvec, dst_bf, bsrc, bsq, bst, bcs, bdst):
            sq3 = v3(sq_t, nh)
            actf(sq_t, src3.rearrange("p h v -> p (h v)"), AF.Square, [bsrc], [bsq])
            S.op(dve, lambda: V.tensor_reduce(out=st_t[:, 0, :], in_=sq3, axis=AX.X, op=ALU.add), [bsq], [bst])
            actf(st_t[:, 1, :], st_t[:, 0, :], AF.Sqrt, [bst], [bst], bias=NORM_EPS, scale=1.0 / 64)
            S.op(dve, lambda: V.reciprocal(out=st_t[:, 1, :], in_=st_t[:, 1, :]), [bst], [bst])
            tt(pool, src3, src3, st_t[:, 1, :].unsqueeze(2).to_broadcast([128, nh, 64]), ALU.mult, [bsrc, bst], [bsrc])
            tt(pool, src3, src3, gvec.unsqueeze(1).to_broadcast([128, nh, 64]), ALU.mult, [bsrc, b_par], [bsrc])
            cos_bc = cs_t[:, 0:32].unsqueeze(1).to_broadcast([128, nh, 32])
            sin_bc = cs_t[:, 32:64].unsqueeze(1).to_broadcast([128, nh, 32])
            x1, x2 = src3[:, :, 0:32], src3[:, :, 32:64]
            s1_, s2_ = sq3[:, :, 0:32], sq3[:, :, 32:64]
            tt(pool, s1_, x1, cos_bc, ALU.mult, [bsrc, bcs], [bsq])
            tt(pool, s2_, x2, sin_bc, ALU.mult, [bsrc, bcs], [bsq])
            tt(pool, dst_bf[:, :, 0:32], s1_, s2_, ALU.subtract, [bsq], [bdst])
            tt(pool, s1_, x2, cos_bc, ALU.mult, [bsrc, bcs], [bsq])
            tt(pool, s2_, x1, sin_bc, ALU.mult, [bsrc, bcs], [bsq])
            tt(pool, dst_bf[:, :, 32:64], s1_, s2_, ALU.add, [bsq], [bdst])

        def proc_kv(s0, m):
            slot = m % 3
            gm = s0 + m * 128
            ld(kvraw[:], qkv[gm:gm + 128, 1024:1536], [B["kvraw"]], R=[b_qkv])
            ld(csk[:], rope_d[m * 128:(m + 1) * 128, :], [B["csk"]])
            cp(act, vaug[slot][:, :, 0:64], v3(kvraw[:, 256:512], 4), [B["kvraw"]], [B[f"vaug{slot}"]])
            rope_norm(v3(kvraw[:, 0:256], 4), 4, tmpk[:], kst, csk, kg, v3(kr[:], 4),
                      B["kvraw"], B["tmpk"], B["kst"], B["csk"], B["kr"])
            for gk in range(4):
                mm(R3[0:64, gk * 128:(gk + 1) * 128], kr[:, gk * 64:(gk + 1) * 64], identb[:], True, True,
                   [B["kr"], b_cst], [B["R3"]])
            cp(dve, kT[slot][:], R3[0:64, :].rearrange("p (g t) -> p g t", g=4), [B["R3"]], [B[f"kT{slot}"]])

        def tile3(si, n):
            s0, T = cfg.seqs[si]
            NB = T // 128
            g0 = s0 + n * 128
            ld(yf[:], ysc[0, g0:g0 + 128, :], [B["yf"]], R=[b_ysc])
            ld(yb[:], ysc[1, g0:g0 + 128, :], [B["yb"]], R=[b_ysc])
            ld(bf_[:], ysc[2, g0:g0 + 128, :], [B["bf"]], R=[b_ysc])
            ld(bb_[:], ysc[3, g0:g0 + 128, :], [B["bb"]], R=[b_ysc])
            ld(gaT[:], pT[3328:4352, g0:g0 + 128].rearrange("(j p) t -> p j t", p=128), [B["gaT"]], R=[b_pT])
            ld(qt[:], qkv[g0:g0 + 128, 0:1024], [B["qt"]], R=[b_qkv])
            ld(csq[:], rope_d[n * 128:(n + 1) * 128, :], [B["csq"]])
            ld(gbT[:], pT[5888:6912, g0:g0 + 128].rearrange("(j p) t -> p j t", p=128), [B["gbT"]], R=[b_pT])
            ld(mgT[:, 0:8, :], pT[6912:7936, g0:g0 + 128].rearrange("(j p) t -> p j t", p=128), [B["mgT"]], R=[b_pT])
            ld(mgT[:, 8:16, :], pT[7936:8960, g0:g0 + 128].rearrange("(j p) t -> p j t", p=128), [B["mgT"]], R=[b_pT],
               acc=True)
            ld(xt[:], x_src(l, g0, 128), [B["xt"]], R=[bx_src(l)])
            def genA():
                tt(pool, yf[:], yf[:], yb[:], ALU.add, [B["yf"], B["yb"]], [B["yf"]])
                tt(pool, bf_[:], bf_[:], bb_[:], ALU.add, [B["bf"], B["bb"]], [B["bf"]])
                tt(pool, bf_[:], bf_[:], lnb_bc, ALU.add, [B["bf"], b_par], [B["bf"]])
                yield
                S.op(dve, lambda: V.tensor_reduce(out=st16[:, 0, :], in_=v3(yf[:], 16), axis=AX.X, op=ALU.add),
                     [B["yf"]], [B["st16"]])
                actf(yb[:], yf[:], AF.Square, [B["yf"]], [B["yb"]])
                S.op(dve, lambda: V.tensor_reduce(out=st16[:, 1, :], in_=v3(yb[:], 16), axis=AX.X, op=ALU.add),
                     [B["yb"]], [B["st16"]])
                ts(dve, st16[:, 2, :], st16[:, 0, :], 1.0 / 64, ALU.mult, [B["st16"]], [B["st16"]])
                tt(dve, st16[:, 3, :], st16[:, 2, :], st16[:, 2, :], ALU.mult, [B["st16"]], [B["st16"]])
                S.op(dve, lambda: V.scalar_tensor_tensor(out=st16[:, 4, :], in0=st16[:, 1, :], scalar=1.0 / 64,
                                                         in1=st16[:, 3, :], op0=ALU.mult, op1=ALU.subtract),
                     [B["st16"]], [B["st16"]])
                actf(st16[:, 4, :], st16[:, 4, :], AF.Sqrt, [B["st16"]], [B["st16"]], bias=GN_EPS)
                S.op(dve, lambda: V.reciprocal(out=st16[:, 5, :], in_=st16[:, 4, :]), [B["st16"]], [B["st16"]])
                yield
                tt(pool, v3(yb[:], 16), v3(yf[:], 16), st16[:, 2, :].unsqueeze(2).to_broadcast([128, 16, 64]),
                   ALU.subtract, [B["yf"], B["st16"]], [B["yb"]])
                tt(pool, v3(yb[:], 16), v3(yb[:], 16), st16[:, 5, :].unsqueeze(2).to_broadcast([128, 16, 64]),
                   ALU.mult, [B["yb"], B["st16"]], [B["yb"]])
                tt(pool, yb[:], yb[:], lng_bc, ALU.mult, [B["yb"], b_par], [B["yb"]])
                tt(pool, yb[:], yb[:], bf_[:], ALU.add, [B["yb"], B["bf"]], [B["yb"]])
                yield
                for kc in range(8):
                    trp(R0[:, kc * 128:(kc + 1) * 128], yb[:, kc * 128:(kc + 1) * 128], identf, [B["yb"], b_cst],
                        [bR0[kc // 4]])
                actf(gaT[:], gaT[:], AF.Silu, [B["gaT"]], [B["gaT"]])
                tt(dve, yagT[:], R0[:].rearrange("p (j t) -> p j t", j=8), gaT[:], ALU.mult, bR0 + [B["gaT"]], [B["yagT"]])
                yield
                for db in range(8):
                    for kc in range(8):
                        mm(R1[:, db * 128:(db + 1) * 128], wpa[:, kc, db * 128:(db + 1) * 128], yagT[:, kc, :],
                           kc == 0, kc == 7, [B["w"], B["yagT"]], [bR1[db // 4]])
                    if db % 2 == 1:
                        yield

            def genB():
                if n == 0:
                    proc_kv(s0, 0)
                    yield
                if n + 1 < NB:
                    proc_kv(s0, n + 1)
                    yield
                rope_norm(v3(qt[:], 16), 16, tmpq[:], qst, csq, qg8, v3(qr[:], 16),
                          B["qt"], B["tmpq"], B["qst"], B["csq"], B["qr"])
                for rnd in range(2):
                    for hh in range(8):
                        h = rnd * 8 + hh
                        mm(R2[0:64, hh * 128:(hh + 1) * 128], qr[:, h * 64:(h + 1) * 64], identb[:], True, True,
                           [B["qr"], b_cst], [bR2[hh // 4]])
                    cp(act, qT[:, rnd * 8:(rnd + 1) * 8, :], R2[0:64, :].rearrange("p (h t) -> p h t", h=8), bR2,
                       [B["qT"]], acc=(rnd == 1))
                    yield
                cbs = [m for m in (n - 1, n, n + 1) if 0 <= m < NB]
                for gq in range(4):
                    for ci, m in enumerate(cbs):
                        slot = m % 3
                        mm(R3[:, :], kT[slot][:, gq, :], qT[:, 4 * gq:4 * gq + 4, :].rearrange("p h t -> p (h t)"),
                           True, True, [B[f"kT{slot}"], B["qT"]], [B["R3"]])
                        actf(et[ci][:].rearrange("p h t -> p (h t)"), R3[:, :], AF.Exp, [B["R3"]], [B[f"et{ci}"]])
                        if m != n:
                            msk = MPREV if m < n else MNEXT
                            tt(pool, et[ci][:], et[ci][:], msk.unsqueeze(1).to_broadcast([128, 4, 128]), ALU.mult,
                               [B[f"et{ci}"], b_cst], [B[f"et{ci}"]])
                    yield
                    for i4 in range(4):
                        for ci, m in enumerate(cbs):
                            slot = m % 3
                            mm(R4[:, i4 * 65:(i4 + 1) * 65], et[ci][:, i4, :], vaug[slot][:, gq, :],
                               ci == 0, ci == len(cbs) - 1, [B[f"et{ci}"], B[f"vaug{slot}"]], [B["R4"]])
                    r4v = R4[:, 0:260].rearrange("p (h v) -> p h v", h=4)
                    tt(dve, den[:, 0, :], r4v[:, :, 64], esink[:, 4 * gq:4 * gq + 4], ALU.add, [B["R4"], b_par], [B["den"]])
                    S.op(dve, lambda: V.reciprocal(out=den[:, 1, :], in_=den[:, 0, :]), [B["den"]], [B["den"]])
                    tt(dve, v3(og[:], 16)[:, 4 * gq:4 * gq + 4, :], r4v[:, :, 0:64],
                       den[:, 1, :].unsqueeze(2).to_broadcast([128, 4, 64]), ALU.mult, [B["R4"], B["den"]], [B["og"]],
                       acc=(gq > 0))
                    yield

            gens = [genA(), genB()]
            while gens:
                for gen_ in list(gens):
                    try:
                        next(gen_)
                    except StopIteration:
                        gens.remove(gen_)
            for kc in range(8):
                trp(R0[:, kc * 128:(kc + 1) * 128], og[:, kc * 128:(kc + 1) * 128], identf, [B["og"], b_cst],
                    [bR0[kc // 4]])
            actf(gbT[:], gbT[:], AF.Silu, [B["gbT"]], [B["gbT"]])
            tt(dve, ogT[:], R0[:].rearrange("p (j t) -> p j t", j=8), gbT[:], ALU.mult, bR0 + [B["gbT"]], [B["ogT"]])
            for db in range(8):
                for kc in range(8):
                    mm(R2[:, db * 128:(db + 1) * 128], wpb[:, kc, db * 128:(db + 1) * 128], ogT[:, kc, :],
                       kc == 0, kc == 7, [B["w"], B["ogT"]], [bR2[db // 4]])
            actf(mgT[:], mgT[:], AF.Sigmoid, [B["mgT"]], [B["mgT"]])
            tt(dve, t1[:], R1[:].rearrange("p (j t) -> p j t", j=8), mgT[:, 0:8, :], ALU.mult, bR1 + [B["mgT"]], [B["t1"]])
            tt(dve, t2[:], R2[:].rearrange("p (j t) -> p j t", j=8), mgT[:, 8:16, :], ALU.mult, bR2 + [B["mgT"]], [B["t2"]])
            tt(pool, mixT[:], t1[:], t2[:], ALU.add, [B["t1"], B["t2"]], [B["mixT"]])
            for hf, (Rb, bRb) in enumerate(((R3, B["R3"]), (R4, B["R4"]))):
                for kc in range(8):
                    mm(Rb[:, :], mixT[:, kc, :], wo[:, kc, hf * 512:(hf + 1) * 512], kc == 0, kc == 7,
                       [B["mixT"], B["w"]], [bRb])
                tt(dve, xo[:, hf * 512:(hf + 1) * 512], Rb[:, :], xt[:, hf * 512:(hf + 1) * 512], ALU.add,
                   [bRb, B["xt"]], [B["xo"]], acc=(hf == 1))
            st(x_dst(l, g0, 128), xo[:], [B["xo"]], [bx_dst(l)])

        S.barrier()
        for si in range(len(cfg.seqs)):
            for n in range(cfg.seqs[si][1] // 128):
                tile3(si, n)
        S.barrier()


def host_params(inp, L):
    f = lambda k: np.asarray(inp[k], np.float32)
    pcol = np.zeros((L, 128, NPC), np.float32)
    mu = f("shift_mu")
    for l in range(L):
        pcol[l, :, PC_NG:PC_NG + 8] = f("norm_g")[l].reshape(8, 128).T
        pcol[l, :, PC_MU:PC_MU + 24] = mu[l, :3072].reshape(24, 128).T
        pcol[l, :64, PC_MUWD] = mu[l, 3072:3136]
        pcol[l, :64, PC_MUWD + 1] = mu[l, 3136:3200]
        pcol[l, :64, PC_MUAD] = mu[l, 3200:3264]
        pcol[l, :64, PC_MUAD + 1] = mu[l, 3264:3328]
        pcol[l, :, PC_KK:PC_KK + 8] = f("k_k")[l].reshape(8, 128).T
        pcol[l, :, PC_KA:PC_KA + 8] = f("k_a")[l].reshape(8, 128).T
        pcol[l, :, PC_RK:PC_RK + 8] = f("r_k")[l].reshape(8, 128).T
    brow = np.concatenate([f("ln_x_g"), f("ln_x_b"), f("q_norm_g"), f("k_norm_g"), f("sink")], axis=1)
    wup = np.concatenate([f("w_lora_up"), f("w0")[:, :, None, :]], axis=2)
    aup = np.concatenate([f("a_lora_up"), f("a0")[:, :, None, :]], axis=2)
    return dict(pcol=pcol, brow=np.ascontiguousarray(brow), wup_aug=np.ascontiguousarray(wup),
                aup_aug=np.ascontiguousarray(aup))


def make_in_maps(inp, cfg, n_cores=8):
    L = cfg.depth
    hp = host_params(inp, L)
    consts = make_consts()
    rope = make_rope(max(cfg.TP, cfg.TS))
    shared = dict(w_in=np.ascontiguousarray(inp["w_in"][:L], dtype=np.float32),
                  w_proj_a=np.ascontiguousarray(inp["w_proj_a"][:L], dtype=np.float32),
                  w_proj_b=np.ascontiguousarray(inp["w_proj_b"][:L], dtype=np.float32),
                  w_out=np.ascontiguousarray(inp["w_out"][:L], dtype=np.float32),
                  consts=consts, rope=rope, **hp)
    xp, xs = np.asarray(inp["x_prompt"]), np.asarray(inp["x_sample"])
    maps = []
    for c in range(n_cores):
        m = dict(shared)
        m["xp"] = np.ascontiguousarray(xp[c % xp.shape[0]], dtype=np.float32)
        m["xs"] = np.ascontiguousarray(xs[c % xs.shape[0]], dtype=np.float32)
        maps.append(m)
    return maps


def kernel(**inputs):
    xp, xs = inputs["x_prompt"], inputs["x_sample"]
    cfg = Cfg(xp.shape[1], xs.shape[1], inputs["w_in"].shape[0])
    nc = build(cfg)
    maps = make_in_maps(inputs, cfg)
    res = run_bass_kernel_spmd(nc, maps, core_ids=list(range(8)))
    y_p = np.stack([res.results[c]["yp"] for c in range(xp.shape[0])], axis=0).astype(np.float32)
    y_s = np.stack([res.results[c]["ys"] for c in range(xs.shape[0])], axis=0).astype(np.float32)
    return (y_p, y_s)
```

```python
import numpy as np
from contextlib import ExitStack
import concourse.bass as bass
import concourse.mybir as mybir
from concourse.bass_utils import run_bass_kernel_spmd

F32 = mybir.dt.float32
BF16 = mybir.dt.bfloat16
AF = mybir.ActivationFunctionType
ALU = mybir.AluOpType
AX = mybir.AxisListType

D = 1024
NIN = 8960
C = 128
NORM_EPS = 1e-6
GN_EPS = 64e-5
NEG_E = -float(np.exp(-0.5))

PC_NG = 0
PC_MU = 8
PC_MUWD = 32
PC_MUAD = 34
PC_KK = 36
PC_KA = 44
PC_RK = 52
NPC = 60
BR_LNG = 0
BR_LNB = 1024
BR_QG = 2048
BR_KG = 2112
BR_SINK = 2176
NBR = 2192
CO_ID = 0
CO_MTF = 128
CO_MTB = CO_MTF + 512
CO_M3F = CO_MTB + 512
CO_M3B = CO_M3F + 128
CO_TRF = CO_M3B + 128
CO_TRB = CO_TRF + 256
CO_BO = CO_TRB + 256
CO_HS = CO_BO + 128
CO_MPREV = CO_HS + 2
CO_MNEXT = CO_MPREV + 128
NCO = CO_MNEXT + 128


def make_consts():
    c = np.zeros((128, NCO), np.float32)
    i = np.arange(128)
    s, t = i[:, None], i[None, :]
    c[:, CO_ID:CO_ID + 128] = (s == t)
    for (o, lt, le) in ((CO_MTF, s < t, s <= t), (CO_MTB, s > t, s >= t)):
        c[:, o:o + 128] = lt
        c[:, o + 128:o + 256] = le
        c[:, o + 256:o + 384] = -1.0 * le
        c[:, o + 384:o + 512] = -1.0 * lt
    c[:, CO_M3F:CO_M3F + 128] = -1.0 * (t < s)
    c[:, CO_M3B:CO_M3B + 128] = -1.0 * (t > s)
    c[:, CO_TRF:CO_TRF + 128] = NEG_E * (s <= t)
    c[:, CO_TRF + 128:CO_TRF + 256] = NEG_E * (s < t)
    c[:, CO_TRB:CO_TRB + 128] = NEG_E * (s >= t)
    c[:, CO_TRB + 128:CO_TRB + 256] = NEG_E * (s > t)
    c[:, CO_BO:CO_BO + 128] = (s // 64 == t // 64)
    c[:, CO_HS] = (i // 64 == 0)
    c[:, CO_HS + 1] = (i // 64 == 1)
    c[:, CO_MPREV:CO_MPREV + 128] = (s >= t)
    c[:, CO_MNEXT:CO_MNEXT + 128] = (s <= t)
    return c


def make_rope(tmax):
    inv = 1.0 / (10000.0 ** (np.arange(0, 64, 2, dtype=np.float32) / 64))
    ang = np.arange(tmax, dtype=np.float32)[:, None] * inv[None, :].astype(np.float32)
    return np.concatenate([np.cos(ang), np.sin(ang)], axis=1).astype(np.float32)


class Buf:
    __slots__ = ("name", "writers", "readers", "base", "excl")

    def __init__(self, name, excl=False):
        self.name = name
        self.excl = excl
        self.writers = {}
        self.readers = {}
        self.base = {}


class Eng:
    def __init__(self, name, h, sem):
        self.name = name
        self.h = h
        self.sem = sem
        self.count = 0
        self.waited = {}


class DmaQ:
    def __init__(self, eng, sems):
        self.eng = eng
        self.sems = sems
        self.n = 0


class Sched:
    def __init__(self, nc, es, n_dma_sems=8):
        self.nc = nc
        mk = lambda nm: es.enter_context(nc.semaphore(nm))
        self.pe = Eng("pe", nc.tensor, mk("s_pe"))
        self.act = Eng("act", nc.scalar, mk("s_act"))
        self.dve = Eng("dve", nc.vector, mk("s_dve"))
        self.pool = Eng("pool", nc.gpsimd, mk("s_pool"))
        self.sp = Eng("sp", nc.sync, mk("s_sp"))
        self.engs = (self.pe, self.act, self.dve, self.pool, self.sp)
        self.sems = {}
        for e in self.engs:
            self.sems[id(e.sem)] = e.sem
        self.q_ld = DmaQ(self.sp, [mk(f"s_ld{i}") for i in range(n_dma_sems)])
        self.q_st = DmaQ(self.pool, [mk(f"s_st{i}") for i in range(n_dma_sems)])
        self.q_l2 = DmaQ(self.act, [mk(f"s_lb{i}") for i in range(n_dma_sems)])
        self.queues = (self.q_ld, self.q_st, self.q_l2)
        for q in self.queues:
            for s in q.sems:
                self.sems[id(s)] = s
        self.n_inst = 0

    def _wait(self, eng, deps, same_ok=True):
        for sid, val in deps.items():
            if eng.waited.get(sid, 0) >= val:
                continue
            if same_ok and sid == id(eng.sem):
                continue
            eng.h.wait_ge(self.sems[sid], val)
            eng.waited[sid] = val

    @staticmethod
    def _merge(d, o):
        for k, v in o.items():
            if d.get(k, 0) < v:
                d[k] = v

    def _deps(self, reads, writes, acc):
        deps = {}
        for b in reads:
            self._merge(deps, b.writers)
            if b.excl:
                self._merge(deps, b.readers)
        for b in writes:
            self._merge(deps, b.readers)
            if not acc:
                self._merge(deps, b.writers)
            else:
                self._merge(deps, b.base)
        return deps

    def _commit(self, reads, writes, tok, acc, deps=None):
        sid, val = tok
        for b in reads:
            if b.readers.get(sid, 0) < val:
                b.readers[sid] = val
        for b in writes:
            if acc:
                if b.writers.get(sid, 0) < val:
                    b.writers[sid] = val
            else:
                b.writers = {sid: val}
                b.readers = {}
                b.base = dict(deps or {})

    def op(self, eng, fn, reads=(), writes=(), acc=False):
        deps = self._deps(reads, writes, acc)
        self._wait(eng, deps, same_ok=(eng is self.pe))
        inst = fn()
        eng.count += 1
        inst.then_inc(eng.sem, 1)
        self._commit(reads, writes, (id(eng.sem), eng.count), acc, deps)
        self.n_inst += 1
        return inst

    def dma(self, q, out, in_, reads=(), writes=(), acc=False, **kw):
        eng = q.eng
        K = len(q.sems)
        i = q.n
        sem = q.sems[i % K]
        val = 16 * (i // K + 1)
        deps = self._deps(reads, writes, acc)
        if i >= K:
            self._merge(deps, {id(sem): val - 16})
        self._wait(eng, deps, same_ok=False)
        eng.h.dma_start(out=out, in_=in_, **kw).then_inc(sem, 16)
        q.n += 1
        self._commit(reads, writes, (id(sem), val), acc, deps)
        self.n_inst += 1

    def all_tokens(self):
        deps = {}
        for e in self.engs:
            if e.count:
                deps[id(e.sem)] = e.count
        for q in self.queues:
            K = len(q.sems)
            for j, s in enumerate(q.sems):
                n = (q.n - j + K - 1) // K if q.n > j else 0
                if n:
                    deps[id(s)] = 16 * n
        return deps

    def barrier(self):
        deps = self.all_tokens()
        for e in self.engs:
            self._wait(e, deps, same_ok=True)

    def finish(self):
        self._wait(self.sp, self.all_tokens(), same_ok=False)


class Cfg:
    def __init__(self, TP, TS, depth, debug=False):
        self.TP, self.TS, self.depth, self.debug = TP, TS, depth, debug
        self.p2stage = 99
        self.p2_every = 2
        self.TA = TP + TS
        self.seqs = [(0, TP), (TP, TS)]


def build(cfg):
    nc = bass.Bass("TRN2", target_bir_lowering=False)
    L = cfg.depth
    TA = cfg.TA
    dt_in = lambda name, shape: nc.dram_tensor(name, shape, F32, kind="ExternalInput").ap()
    xp = dt_in("xp", [cfg.TP, D])
    xs = dt_in("xs", [cfg.TS, D])
    w_in = dt_in("w_in", [L, D, NIN])
    wpa_d = dt_in("w_proj_a", [L, D, D])
    wpb_d = dt_in("w_proj_b", [L, D, D])
    wo_d = dt_in("w_out", [L, D, D])
    pcol_d = dt_in("pcol", [L, 128, NPC])
    brow_d = dt_in("brow", [L, NBR])
    wup_d = dt_in("wup_aug", [L, 2, 65, D])
    aup_d = dt_in("aup_aug", [L, 2, 65, D])
    consts_d = dt_in("consts", [128, NCO])
    rope_d = dt_in("rope", [max(cfg.TP, cfg.TS), 64])
    yp = nc.dram_tensor("yp", [cfg.TP, D], F32, kind="ExternalOutput").ap()
    ys = nc.dram_tensor("ys", [cfg.TS, D], F32, kind="ExternalOutput").ap()
    dbg_kind = "ExternalOutput" if cfg.debug else "Internal"
    _pt_parts = [(0, 3328, nc.dram_tensor("pT_slab", [3328, TA], F32, kind=dbg_kind).ap()),
                 (3328, 4352, nc.dram_tensor("pT_ga", [1024, TA], F32).ap()),
                 (5888, 6912, nc.dram_tensor("pT_gb", [1024, TA], F32).ap()),
                 (6912, 8960, nc.dram_tensor("pT_mg", [2048, TA], F32).ap())]

    class _PT:
        def __getitem__(self, key):
            rs, cs = key
            for (a, b, ap) in _pt_parts:
                if a <= rs.start and rs.stop <= b:
                    return ap[rs.start - a:rs.stop - a, cs]
            raise KeyError(key)
    pT = _PT()
    qkv = nc.dram_tensor("qkv", [TA, 1536], F32, kind=dbg_kind).ap()
    ysc = nc.dram_tensor("ysc", [4, TA, D], F32, kind=dbg_kind).ap()
    xsc = [nc.dram_tensor(f"xsc{i}", [TA, D], F32).ap() for i in range(2)]

    def x_src(l, g0, n):
        if l == 0:
            return xp[g0:g0 + n, :] if g0 < cfg.TP else xs[g0 - cfg.TP:g0 - cfg.TP + n, :]
        return xsc[(l - 1) % 2][g0:g0 + n, :]

    def x_dst(l, g0, n):
        if l == L - 1:
            return yp[g0:g0 + n, :] if g0 < cfg.TP else ys[g0 - cfg.TP:g0 - cfg.TP + n, :]
        return xsc[l % 2][g0:g0 + n, :]

    with ExitStack() as es:
        S = Sched(nc, es)
        pe, act, dve, pool = S.pe, S.act, S.dve, S.pool
        V, G, A, PE = nc.vector, nc.gpsimd, nc.scalar, nc.tensor

        def EH(e):
            return {"dve": V, "pool": G, "act": A}[e.name]

        def tt(e, out, in0, in1, op, R, W, acc=False):
            h = EH(e)
            S.op(e, lambda: h.tensor_tensor(out=out, in0=in0, in1=in1, op=op), R, W, acc)

        def ts(e, out, in0, s1, op0, R, W, s2=None, op1=None, acc=False):
            h = EH(e)
            if op1 is None:
                S.op(e, lambda: h.tensor_scalar(out=out, in0=in0, scalar1=s1, scalar2=None, op0=op0), R, W, acc)
            else:
                S.op(e, lambda: h.tensor_scalar(out=out, in0=in0, scalar1=s1, scalar2=s2, op0=op0, op1=op1), R, W, acc)

        def actf(out, in_, func, R, W, bias=0.0, scale=1.0, acc=False):
            S.op(act, lambda: A.activation(out=out, in_=in_, func=func, bias=bias, scale=scale), R, W, acc)

        def cp(e, out, in_, R, W, acc=False):
            if e is act:
                S.op(act, lambda: A.copy(out=out, in_=in_), R, W, acc)
            else:
                h = EH(e)
                S.op(e, lambda: h.tensor_copy(out=out, in_=in_), R, W, acc)

        def mm(out, lhsT, rhs, start, stop, R, W, acc=False, skip=False):
            if skip:
                S.op(pe, lambda: PE.matmul(out, lhsT, rhs, start=start, stop=stop, skip_group_check=True), R, W, acc)
            else:
                S.op(pe, lambda: PE.matmul(out, lhsT, rhs, start=start, stop=stop), R, W, acc)

        def trp(out, in_, ident, R, W):
            S.op(pe, lambda: PE.transpose(out, in_, ident), R, W)

        def ld(out, in_, W, R=(), q=None, **kw):
            S.dma(q or S.q_ld, out, in_, reads=R, writes=W, **kw)

        def st(out, in_, R, W, acc=True):
            S.dma(S.q_st, out, in_, reads=R, writes=W, acc=acc)

        uid = [0]

        def T_(st_, name, shape, dt=F32):
            uid[0] += 1
            return st_.enter_context(nc.sbuf_tensor(f"{name}_{uid[0]}", shape, dt))

        def P_(st_, name, shape, dt=F32):
            uid[0] += 1
            return st_.enter_context(nc.psum_tensor(f"{name}_{uid[0]}", shape, dt))
        cst = T_(es, "cst", [128, NCO])
        identb = T_(es, "identb", [128, 128], BF16)
        b_cst = Buf("cst")
        ld(cst[:], consts_d[:, :], [b_cst])
        cp(pool, identb[:], cst[:, CO_ID:CO_ID + 128], [b_cst], [b_cst])
        identf = cst[:, CO_ID:CO_ID + 128]
        b_pT, b_qkv, b_ysc = Buf("pT"), Buf("qkv"), Buf("ysc")
        b_x = [Buf("x_in"), Buf("xsc0"), Buf("xsc1"), Buf("y_out")]

        def bx_src(l):
            return b_x[0] if l == 0 else b_x[1 + (l - 1) % 2]

        def bx_dst(l):
            return b_x[3] if l == L - 1 else b_x[1 + l % 2]

        for l in range(L):
            with ExitStack() as ls:
                pcol = T_(ls, "pcol", [128, NPC])
                pder = T_(ls, "pder", [128, 64])
                b_par = Buf("par")
                S.barrier()
                ld(pcol[:], pcol_d[l, :, :], [b_par])
                ts(dve, pder[:, 0:24], pcol[:, PC_MU:PC_MU + 24], 0.5, ALU.mult, [b_par], [b_par])
                ts(dve, pder[:, 24:48], pcol[:, PC_MU:PC_MU + 24], -1.0, ALU.mult, [b_par], [b_par], 1.0, ALU.add)
                ts(dve, pder[:, 48:56], pcol[:, PC_KA:PC_KA + 8], -1.0, ALU.mult, [b_par], [b_par], 1.0, ALU.add)
                ts(dve, pder[:, 56:60], pcol[:, PC_MUWD:PC_MUWD + 4], 0.5, ALU.mult, [b_par], [b_par])
                ts(dve, pder[:, 60:64], pcol[:, PC_MUWD:PC_MUWD + 4], -1.0, ALU.mult, [b_par], [b_par], 1.0, ALU.add)

                phase1(nc, cfg, S, l, locals())
                if cfg.debug == "p1":
                    break
                phase2(nc, cfg, S, l, locals())
                if cfg.debug == "p2":
                    break
                phase3(nc, cfg, S, l, locals())
        S.barrier()
        S.finish()
    return nc


FM_BLOCKS = list(range(0, 34)) + list(range(46, 70))
TM_COL0 = 4352


def phase1(nc, cfg, S, l, E):
    g = lambda k: E[k]
    T_, P_, ld, st, tt, ts, actf, cp, mm, trp = (g(k) for k in
                                                   "T_ P_ ld st tt ts actf cp mm trp".split())
    pe, act, dve, pool = S.pe, S.act, S.dve, S.pool
    pcol, b_par, identb, b_cst = g("pcol"), g("b_par"), g("identb"), g("b_cst")
    w_in, pT, qkv, b_pT, b_qkv = g("w_in"), g("pT"), g("qkv"), g("b_pT"), g("b_qkv")
    x_src, bx_src = g("x_src"), g("bx_src")
    V, G, A, PE = nc.vector, nc.gpsimd, nc.scalar, nc.tensor
    TW = 512
    with ExitStack() as ps:
        S.barrier()
        wbf = T_(ps, "wbf", [128, 8, NIN], BF16)
        wst = [T_(ps, f"wst{i}", [128, 896]) for i in range(2)]
        hT2 = [T_(ps, f"hT{i}", [128, 8, TW], BF16) for i in range(2)]
        xt = [T_(ps, f"xt{i}", [128, D]) for i in range(2)]
        xsq = T_(ps, "xsq", [128, D])
        xn = [T_(ps, f"xn{i}", [128, D], BF16) for i in range(2)]
        ss = [T_(ps, f"ss{i}", [128, 2]) for i in range(2)]
        stg = [T_(ps, f"stg{i}", [128, TW]) for i in range(4)]
        ptr = [P_(ps, f"ptr{i}", [128, 8, 128], BF16) for i in range(2)]
        pout = [P_(ps, f"pout{i}", [128, TW]) for i in range(4)]
        b_w = Buf("wbf")
        b_wst = [Buf("wst0"), Buf("wst1")]
        b_hT2 = [Buf("hT0"), Buf("hT1")]
        b_xt = [Buf("xt0"), Buf("xt1")]
        b_xsq = Buf("xsq")
        b_xn = [Buf("xn0"), Buf("xn1")]
        b_ss = [Buf("ss0"), Buf("ss1")]
        b_stg = [Buf(f"stg{i}") for i in range(4)]
        b_ptr = [Buf("ptr0", True), Buf("ptr1", True)]
        b_pout = [Buf(f"pout{i}", True) for i in range(4)]
        k = 0
        cast_engs = (act, pool, dve)
        for kc in range(8):
            for cc in range(10):
                i = k % 2
                ld(wst[i][:], w_in[l, kc * 128:(kc + 1) * 128, cc * 896:(cc + 1) * 896], [b_wst[i]])
                cp(cast_engs[k % 3], wbf[:, kc, cc * 896:(cc + 1) * 896], wst[i][:], [b_wst[i]], [b_w], acc=True)
                k += 1
        gcol = pcol[:, PC_NG:PC_NG + 8]
        n_tiles = cfg.TA // TW
        sub = 0
        oi = 0
        for ti in range(n_tiles):
            g0 = ti * TW
            hT, b_hT = hT2[ti % 2], b_hT2[ti % 2]
            for s4 in range(TW // 128):
                i = sub % 2
                sub += 1
                gs = g0 + s4 * 128
                ld(xt[i][:], x_src(l, gs, 128), [b_xt[i]], R=[bx_src(l)])
                actf(xsq[:], xt[i][:], AF.Square, [b_xt[i]], [b_xsq])
                S.op(dve, lambda: V.reduce_sum(out=ss[i][:, 0:1], in_=xsq[:], axis=AX.X), [b_xsq], [b_ss[i]])
                actf(ss[i][:, 1:2], ss[i][:, 0:1], AF.Sqrt, [b_ss[i]], [b_ss[i]], bias=NORM_EPS, scale=1.0 / D)
                S.op(dve, lambda: V.reciprocal(out=ss[i][:, 1:2], in_=ss[i][:, 1:2]), [b_ss[i]], [b_ss[i]])
                ts(dve, xn[i][:], xt[i][:], ss[i][:, 1:2], ALU.mult, [b_xt[i], b_ss[i]], [b_xn[i]])
                for kc in range(8):
                    trp(ptr[i][:, kc, :], xn[i][:, kc * 128:(kc + 1) * 128], identb[:], [b_xn[i], b_cst], [b_ptr[i]])
                tt(dve, hT[:, :, s4 * 128:(s4 + 1) * 128], ptr[i][:],
                   gcol.unsqueeze(2).to_broadcast([128, 8, 128]), ALU.mult, [b_ptr[i], b_par], [b_hT])
            for fb in FM_BLOCKS:
                o = oi % 4
                oi += 1
                for kc in range(8):
                    mm(pout[o][:], wbf[:, kc, fb * 128:(fb + 1) * 128], hT[:, kc, :], kc == 0, kc == 7,
                       [b_w, b_hT], [b_pout[o]])
                cp(act if o % 2 == 0 else dve, stg[o][:], pout[o][:], [b_pout[o]], [b_stg[o]])
                st(pT[fb * 128:(fb + 1) * 128, g0:g0 + TW], stg[o][:], [b_stg[o]], [b_pT])
            for s4 in range(TW // 128):
                for cg in range(3):
                    o = oi % 4
                    oi += 1
                    for kc in range(8):
                        mm(pout[o][:], hT[:, kc, s4 * 128:(s4 + 1) * 128],
                           wbf[:, kc, TM_COL0 + cg * 512:TM_COL0 + (cg + 1) * 512], kc == 0, kc == 7,
                           [b_w, b_hT], [b_pout[o]])
                    cp(act if o % 2 == 0 else dve, stg[o][:], pout[o][:], [b_pout[o]], [b_stg[o]])
                    st(qkv[g0 + s4 * 128:g0 + (s4 + 1) * 128, cg * 512:(cg + 1) * 512], stg[o][:],
                       [b_stg[o]], [b_qkv])
        S.barrier()


def phase2(nc, cfg, S, l, E):
    g = lambda k: E[k]
    T_, P_, ld, st, tt, ts, actf, cp, mm, trp = (g(k) for k in
                                                   "T_ P_ ld st tt ts actf cp mm trp".split())
    pe, act, dve, pool = S.pe, S.act, S.dve, S.pool
    pcol, pder, b_par, identb, b_cst, cst = (g(k) for k in "pcol pder b_par identb b_cst cst".split())
    pT, b_pT, ysc, b_ysc = g("pT"), g("b_pT"), g("ysc"), g("b_ysc")
    wup_d, aup_d = g("wup_d"), g("aup_d")
    V, G, A, PE = nc.vector, nc.gpsimd, nc.scalar, nc.tensor
    identf = cst[:, CO_ID:CO_ID + 128]
    BO = cst[:, CO_BO:CO_BO + 128]
    HS = cst[:, CO_HS:CO_HS + 2]
    C2, C4 = 2 * C, 4 * C
    NG = 4
    with ExitStack() as ps:
        S.barrier()
        wup = T_(ps, "wup", [65, 2, D])
        aup = T_(ps, "aup", [65, 2, D])
        H = T_(ps, "H", [128, 4, 8, 64])
        Hq = T_(ps, "Hq", [128, 4, 8, 64], BF16)
        slab = T_(ps, "slab", [128, 24, C + 2])
        wdt = T_(ps, "wdt", [64, C + 2])
        adt = T_(ps, "adt", [64, C + 2])
        wtmp = T_(ps, "wtmp", [64, C])
        atmp = T_(ps, "atmp", [64, C])
        wds = T_(ps, "wds", [65, C])
        ads = T_(ps, "ads", [65, C])
        sg = T_(ps, "sg", [128, D])
        eN = T_(ps, "eN", [128, 8, C])
        eX = T_(ps, "eX", [128, 8, C])
        aT = T_(ps, "aT", [128, 8, C])
        tmp24 = T_(ps, "tmp24", [128, 24, C])
        sh24 = T_(ps, "sh24", [128, 24, C])
        kkn = T_(ps, "kkn", [128, 8, C])
        sq = T_(ps, "sq", [128, 8, C])
        bb = T_(ps, "bb", [128, 8, C])
        kmod = T_(ps, "kmod", [128, 8, C])
        prod = T_(ps, "prod", [128, 8, C])
        cc = T_(ps, "cc", [128, 16])
        bon = T_(ps, "bon", [128, D])
        eL2 = [T_(ps, f"eL{i}", [128, 8, C]) for i in range(2)]
        QR2 = [T_(ps, f"QR{i}", [128, 8, 3, C], BF16) for i in range(2)]
        KhT2 = [T_(ps, f"KhT{i}", [128, 8, C], BF16) for i in range(2)]
        BhT2 = [T_(ps, f"BhT{i}", [128, 8, C], BF16) for i in range(2)]
        Vb2 = [T_(ps, f"Vb{i}", [128, D], BF16) for i in range(2)]
        Kh2 = [T_(ps, f"Kh{i}", [128, D], BF16) for i in range(2)]
        Bhn2 = [T_(ps, f"Bhn{i}", [128, D], BF16) for i in range(2)]
        M12 = [T_(ps, f"M12_{i}", [128, C4 + C + 64], BF16) for i in range(NG)]
        XU = [[T_(ps, f"XU_{i}_{lv}", [128, C2 + 64], BF16) for lv in range(6)] for i in range(NG)]
        Uall = T_(ps, "Uall", [128, D], BF16)
        Ysb = T_(ps, "Ysb", [128, D])
        tmpH = T_(ps, "tmpH", [128, 4, 64])
        pr = P_(ps, "pr", [128, D])
        pSX = [P_(ps, f"pSX{i}", [128, 512]) for i in range(NG)]
        pY = P_(ps, "pY", [128, 512])
        pH = P_(ps, "pH", [128, 4, 128])
        B = {n: Buf(n) for n in ("lora H0 H1 H2 H3 Hq0 Hq1 Hq2 Hq3 slab wdt adt wtmp atmp wds ads sg eN eX aT "
                                 "tmp24_0 tmp24_1 tmp24_2 sh24_0 sh24_1 sh24_2 kkn sq bb kmod prod cc bon Uall Ysb tmpH "
                                 "eL0 eL1 QR0 QR1 KhT0 KhT1 BhT0 BhT1 Vb0 Vb1 Kh0 Kh1 Bhn0 Bhn1").split()}
        for n_ in ["pr0", "pr1", "pY", "pH"] + [f"pS{i}" for i in range(NG)]:
            B[n_] = Buf(n_, True)
        bM = [Buf(f"M12_{i}") for i in range(NG)]
        bXU = [[Buf(f"XU{i}{lv}") for lv in range(6)] for i in range(NG)]
        bPR = [B["pr0"], B["pr1"]]
        for d in range(2):
            ld(wup[:, d, :], wup_d[l, d, :, :], [B["lora"]], acc=True)
            ld(aup[:, d, :], aup_d[l, d, :, :], [B["lora"]], acc=True)
        for sd in range(4):
            S.op(pool, lambda: G.memset(H[:, sd, :, :], 0.0), (), [B[f"H{sd}"]])
            S.op(pool, lambda: G.memset(Hq[:, sd, :, :], 0.0), (), [B[f"Hq{sd}"]])
        S.op(pool, lambda: G.memset(wds[64:65, :], 1.0), (), [B["wds"]])
        S.op(pool, lambda: G.memset(ads[64:65, :], 1.0), (), [B["ads"]])

        hmu_bc = pder[:, 0:24].unsqueeze(2).to_broadcast([128, 24, C])
        omu_bc = pder[:, 24:48].unsqueeze(2).to_broadcast([128, 24, C])
        omka_bc = pder[:, 48:56].unsqueeze(2).to_broadcast([128, 8, C])
        kk_bc = pcol[:, PC_KK:PC_KK + 8].unsqueeze(2).to_broadcast([128, 8, C])
        ka_bc = pcol[:, PC_KA:PC_KA + 8].unsqueeze(2).to_broadcast([128, 8, C])
        rk_bc = pcol[:, PC_RK:PC_RK + 8].unsqueeze(2).to_broadcast([128, 8, C])

        class U_:
            pass

        def mk_unit(si, ch, d, up):
            u = U_()
            u.si, u.ch, u.d, u.up = si, ch, d, up
            u.s0, T = cfg.seqs[si]
            u.nch = T // C
            u.g0 = u.s0 + ch * C
            u.sd = si * 2 + d
            return u

        def prep(u):
            si, ch, d, up, g0, sd = u.si, u.ch, u.d, u.up, u.g0, u.sd
            eL, QR, KhT, BhT, Vb, Kh, Bhn = eL2[up], QR2[up], KhT2[up], BhT2[up], Vb2[up], Kh2[up], Bhn2[up]
            beL, bQR, bKhT, bBhT, bVb, bKh, bBhn = (B[f"{n}{up}"] for n in ("eL", "QR", "KhT", "BhT", "Vb", "Kh", "Bhn"))
            TRo = CO_TRF if d == 0 else CO_TRB
            TRinc = cst[:, TRo:TRo + C]
            TRexc = cst[:, TRo + C:TRo + C2]
            first, lastc = (ch == 0), (ch == u.nch - 1)
            lo = 1 if first else 0
            hi = C + 1 if lastc else C + 2
            if first:
                S.op(pool, lambda: G.memset(slab[:, :, 0:1], 0.0), (), [B["slab"]])
                S.op(pool, lambda: G.memset(wdt[:, 0:1], 0.0), (), [B["wdt"]])
                S.op(pool, lambda: G.memset(adt[:, 0:1], 0.0), (), [B["adt"]])
            if lastc:
                S.op(pool, lambda: G.memset(slab[:, :, C + 1:C + 2], 0.0), (), [B["slab"]], acc=not first)
                S.op(pool, lambda: G.memset(wdt[:, C + 1:C + 2], 0.0), (), [B["wdt"]], acc=not first)
                S.op(pool, lambda: G.memset(adt[:, C + 1:C + 2], 0.0), (), [B["adt"]], acc=not first)
            c0 = g0 - 1 + lo
            c1 = g0 - 1 + hi
            ld(wdt[:, lo:hi], pT[3072 + d * 64:3072 + (d + 1) * 64, c0:c1], [B["wdt"]], R=[b_pT], acc=(first or lastc))
            ld(adt[:, lo:hi], pT[3200 + d * 64:3200 + (d + 1) * 64, c0:c1], [B["adt"]], R=[b_pT], acc=(first or lastc))
            for jb in (2, 3, 0, 1, 4, 5):
                src = pT[jb * 512:(jb + 1) * 512, c0:c1].rearrange("(j p) t -> p j t", p=128)
                ld(slab[:, jb * 4:(jb + 1) * 4, lo:hi], src, [B["slab"]], R=[b_pT], acc=(jb != 2 or first or lastc))
            yield
            for (src_t, tmp_t, dst_t, bs, bt, bd, co) in ((wdt, wtmp, wds, "wdt", "wtmp", "wds", d),
                                                          (adt, atmp, ads, "adt", "atmp", "ads", 2 + d)):
                tt(dve, tmp_t[:], src_t[:, 0:C], src_t[:, 2:C + 2], ALU.add, [B[bs]], [B[bt]])
                ts(dve, tmp_t[:], tmp_t[:], pder[0:64, 56 + co:57 + co], ALU.mult, [B[bt], b_par], [B[bt]])
                S.op(dve, lambda: V.scalar_tensor_tensor(out=dst_t[0:64, :], in0=src_t[:, 1:C + 1],
                                                         scalar=pder[0:64, 60 + co:61 + co], in1=tmp_t[:],
                                                         op0=ALU.mult, op1=ALU.add),
                     [B[bs], B[bt], b_par], [B[bd]])
            actf(wds[0:64, :], wds[0:64, :], AF.Tanh, [B["wds"]], [B["wds"]])
            yield

            def shift_third(th, e1, e2):
                t8 = slice(th * 8, th * 8 + 8)
                bt_, bs_ = B[f"tmp24_{th}"], B[f"sh24_{th}"]
                hm = pder[:, th * 8:th * 8 + 8].unsqueeze(2).to_broadcast([128, 8, C])
                om = pder[:, 24 + th * 8:24 + th * 8 + 8].unsqueeze(2).to_broadcast([128, 8, C])
                tt(e1, tmp24[:, t8, :], slab[:, t8, 0:C], slab[:, t8, 2:C + 2], ALU.add, [B["slab"]], [bt_])
                tt(e1, tmp24[:, t8, :], tmp24[:, t8, :], hm, ALU.mult, [bt_, b_par], [bt_])
                tt(e2, sh24[:, t8, :], slab[:, t8, 1:C + 1], om, ALU.mult, [B["slab"], b_par], [bs_])
                tt(e2, sh24[:, t8, :], sh24[:, t8, :], tmp24[:, t8, :], ALU.add, [bs_, bt_], [bs_])
            rs, ks, vs = sh24[:, 0:8, :], sh24[:, 8:16, :], sh24[:, 16:24, :]
            pr3 = pr[:].rearrange("p (j c) -> p j c", j=8)
            shift_third(1, pool, dve)
            yield
            for hf in range(2):
                mm(pr[:, hf * 512:(hf + 1) * 512], wds[:, :], wup[:, d, hf * 512:(hf + 1) * 512], True, True,
                   [B["wds"], B["lora"]], [bPR[hf]])
            actf(sg[:], pr[:], AF.Sigmoid, bPR, [B["sg"]])
            yield
            tt(pool, kkn[:], ks, kk_bc, ALU.mult, [B["sh24_1"], b_par], [B["kkn"]])
            shift_third(0, dve, pool)
            yield
            for j in range(8):
                mm(pr[:, j * C:(j + 1) * C], aup[:, d, j * 128:(j + 1) * 128], ads[:, :], True, True,
                   [B["ads"], B["lora"]], [bPR[j // 4]])
            actf(aT[:], pr3, AF.Sigmoid, bPR, [B["aT"]])
            actf(sq[:], kkn[:], AF.Square, [B["kkn"]], [B["sq"]])
            yield
            for j in range(8):
                mm(pr[:, j * C:(j + 1) * C], sg[:, j * 128:(j + 1) * 128], TRinc, True, True,
                   [B["sg"], b_cst], [bPR[j // 4]])
            actf(eL[:], pr3, AF.Exp, bPR, [beL])
            actf(eN[:], pr3, AF.Exp, bPR, [B["eN"]], scale=-1.0)
            yield
            sq2 = sq[:].rearrange("p j c -> p (j c)")
            for hf in range(2):
                mm(pr[:, hf * 512:(hf + 1) * 512], BO, sq2[:, hf * 512:(hf + 1) * 512], True, True,
                   [B["sq"], b_cst], [bPR[hf]])
            actf(sq[:], pr3, AF.Ln, bPR + [B["sq"]], [B["sq"]], bias=1e-24)
            actf(sq[:], sq[:], AF.Exp, [B["sq"]], [B["sq"]], scale=-0.5)
            tt(dve, kmod[:], aT[:], ka_bc, ALU.mult, [B["aT"], b_par], [B["kmod"]])
            tt(dve, kmod[:], kmod[:], omka_bc, ALU.add, [B["kmod"], b_par], [B["kmod"]])
            yield
            for j in range(8):
                mm(pr[:, j * C:(j + 1) * C], sg[:, j * 128:(j + 1) * 128], TRexc, True, True,
                   [B["sg"], b_cst], [bPR[j // 4]])
            actf(eX[:], pr3, AF.Exp, bPR, [B["eX"]])
            shift_third(2, pool, dve)
            yield
            tt(pool, kmod[:], kmod[:], ks, ALU.mult, [B["kmod"], B["sh24_1"]], [B["kmod"]])
            tt(pool, kkn[:], kkn[:], sq[:], ALU.mult, [B["kkn"], B["sq"]], [B["kkn"]])
            yield
            tt(pool, bb[:], kkn[:], aT[:], ALU.mult, [B["kkn"], B["aT"]], [B["bb"]])
            tt(dve, prod[:], rs, kmod[:], ALU.mult, [B["sh24_0"], B["kmod"]], [B["prod"]])
            tt(dve, QR[:, :, 0, :], kkn[:], eX[:], ALU.mult, [B["kkn"], B["eX"]], [bQR])
            yield
            tt(pool, prod[:], prod[:], rk_bc, ALU.mult, [B["prod"], b_par], [B["prod"]])
            tt(dve, QR[:, :, 1, :], rs, eL[:], ALU.mult, [B["sh24_0"], beL], [bQR], acc=True)
            S.op(act, lambda: A.copy(out=QR[:, :, 2, :], in_=QR[:, :, 0, :]), [bQR], [bQR], acc=True)
            yield
            tt(pool, KhT[:], kmod[:], eN[:], ALU.mult, [B["kmod"], B["eN"]], [bKhT])
            for j in range(8):
                mm(pr[:, 2 * j:2 * j + 2], prod[:, j, :], HS, True, True, [B["prod"], b_cst], [bPR[0]])
            cp(act, cc[:], pr[:, 0:16], [bPR[0]], [B["cc"]])
            yield
            tt(pool, BhT[:], bb[:], eN[:], ALU.mult, [B["bb"], B["eN"]], [bBhT])
            for j in range(8):
                trp(pr[:, j * 128:(j + 1) * 128], vs[:, j, :], identf, [B["sh24_2"], b_cst], [bPR[j // 4]])
            cp(act, Vb[:], pr[:], bPR, [bVb])
            tt(dve, bon[:].rearrange("p (h v) -> p h v", h=16), pr[:].rearrange("p (h v) -> p h v", h=16),
               cc[:].unsqueeze(2).to_broadcast([128, 16, 64]), ALU.mult, bPR + [B["cc"]], [B["bon"]])
            st(ysc[2 + d, g0:g0 + C, :], bon[:], [B["bon"]], [b_ysc])
            yield
            for j in range(8):
                mm(pr[:, j * 128:(j + 1) * 128], KhT[:, j, :], identb[:], True, True, [bKhT, b_cst], [bPR[j // 4]])
            cp(act, Kh[:], pr[:], bPR, [bKh])
            yield
            for j in range(8):
                mm(pr[:, j * 128:(j + 1) * 128], BhT[:, j, :], identb[:], True, True, [bBhT, b_cst], [bPR[j // 4]])
            S.op(act, lambda: A.mul(out=Bhn[:], in_=pr[:], mul=-1.0), bPR, [bBhn])

        def heads(u):
            if cfg.p2stage < 50:
                return
            si, ch, d, up, g0, sd = u.si, u.ch, u.d, u.up, u.g0, u.sd
            eL, QR, KhT, BhT, Vb, Kh, Bhn = eL2[up], QR2[up], KhT2[up], BhT2[up], Vb2[up], Kh2[up], Bhn2[up]
            beL, bQR, bKhT, bBhT, bVb, bKh, bBhn = (B[f"{n}{up}"] for n in ("eL", "QR", "KhT", "BhT", "Vb", "Kh", "Bhn"))
            bH, bHq = B[f"H{sd}"], B[f"Hq{sd}"]
            MT = cst[:, (CO_MTF if d == 0 else CO_MTB):(CO_MTF if d == 0 else CO_MTB) + C4]
            M3 = cst[:, (CO_M3F if d == 0 else CO_M3B):(CO_M3F if d == 0 else CO_M3B) + C]
            last = C - 1 if d == 0 else 0

            def head(h, gi):
                j, par = h // 2, h % 2
                sl = slice(par * 64, par * 64 + 64)
                hs = slice(h * 64, (h + 1) * 64)
                bS = B[f"pS{gi}"]
                pb = pSX[gi]
                Mt = M12[gi]
                qr01 = QR[sl, j, 0:2, :].rearrange("p a c -> p (a c)")
                qr12 = QR[sl, j, 1:3, :].rearrange("p a c -> p (a c)")
                mm(pb[:, 0:C2], KhT[sl, j, :], qr01, True, True, [bKhT, bQR], [bS])
                mm(pb[:, C2:C4], BhT[sl, j, :], qr12, True, True, [bBhT, bQR], [bS])
                tt(dve, Mt[:, 0:C4], pb[:, :], MT, ALU.mult, [bS, b_cst], [bM[gi]])
                yield
                mm(pb[:, 0:C], QR[sl, j, 0, :], BhT[sl, j, :], True, True, [bQR, bBhT], [bS])
                mm(pb[:, 256:320], QR[sl, j, 0, :], Hq[sl, sd, j, :], True, False, [bQR, bHq], [bS])
                mm(pb[:, 256:320], Mt[:, 0:C], Vb[:, hs], False, True, [bM[gi], bVb], [bS])
                tt(dve, Mt[:, C4:C4 + C], pb[:, 0:C], M3, ALU.mult, [bS, b_cst], [bM[gi]], acc=True)
                cp(act, Mt[:, C4 + C:C4 + C + 64], pb[:, 256:320], [bS], [bM[gi]], acc=True)
                yield
                Tt, o, bT = Mt, 3 * C, bM[gi]
                for lv in range(7):
                    XT_, X_, U_ = Tt[:, o:o + C], Tt[:, o + C:o + C2], Tt[:, o + C2:o + C2 + 64]
                    if lv < 5:
                        mm(pb[:, C:C2 + 64], XT_, Tt[:, o + C:o + C2 + 64], True, True, [bT], [bS])
                        mm(pb[:, 0:C], X_, XT_, True, True, [bT], [bS])
                    elif lv == 5:
                        mm(pb[:, C2:C2 + 64], XT_, U_, True, True, [bT], [bS])
                        mm(pb[:, 0:C], X_, XT_, True, True, [bT], [bS])
                    else:
                        mm(pb[:, C2:C2 + 64], XT_, U_, True, True, [bT], [bS])
                    if lv < 6:
                        nt, bn = XU[gi][lv], bXU[gi][lv]
                        ev = act
                        cp(ev, nt[:, 0:(C2 if lv < 5 else C)], pb[:, 0:(C2 if lv < 5 else C)], [bS], [bn])
                        tt(dve, nt[:, C2:C2 + 64], pb[:, C2:C2 + 64], U_, ALU.add, [bS, bT], [bn], acc=True)
                        Tt, o, bT = nt, 0, bn
                    else:
                        tt(dve, Uall[:, hs], pb[:, C2:C2 + 64], U_, ALU.add, [bS, bT], [B["Uall"]], acc=True)
                    yield
                yo = (h % 8) * 64
                mm(pY[:, yo:yo + 64], QR[sl, j, 1, :], Hq[sl, sd, j, :], True, False, [bQR, bHq], [B["pY"]])
                mm(pY[:, yo:yo + 64], Mt[:, C:C2], Vb[:, hs], False, False, [bM[gi], bVb], [B["pY"]])
                mm(pY[:, yo:yo + 64], Mt[:, C2:3 * C], Uall[:, hs], False, True, [bM[gi], B["Uall"]], [B["pY"]])

            for q in range(16 // NG):
                gens = [head(q * NG + gi, gi) for gi in range(NG)]
                while gens:
                    for gen in list(gens):
                        try:
                            next(gen)
                        except StopIteration:
                            gens.remove(gen)
                    yield
                for j in range(q * NG // 2, (q + 1) * NG // 2):
                    if j % 4 == 0 and j > 0 or False:
                        pass
                    jj = j % 4
                    mm(pH[:, jj, :], Kh[:, j * 128:(j + 1) * 128], Vb[:, j * 128:(j + 1) * 128], True, False,
                       [bKh, bVb], [B["pH"]])
                    mm(pH[:, jj, :], Bhn[:, j * 128:(j + 1) * 128], Uall[:, j * 128:(j + 1) * 128], False, True,
                       [bBhn, B["Uall"]], [B["pH"]])
                    if jj == 3:
                        hb = j // 4
                        cp(act, Ysb[:, hb * 512:(hb + 1) * 512], pY[:], [B["pY"]], [B["Ysb"]], acc=(hb == 1))
                        j0 = j - 3
                        for hp in range(2):
                            psl = slice(hp * 64, hp * 64 + 64)
                            tt(dve, tmpH[psl, :, :], H[psl, sd, j0:j0 + 4, :], pH[psl, :, hp * 64:hp * 64 + 64],
                               ALU.add, [bH, B["pH"]], [B["tmpH"]], acc=(hp == 1))
                        for hp in range(2):
                            psl = slice(hp * 64, hp * 64 + 64)
                            tt(pool, H[psl, sd, j0:j0 + 4, :], tmpH[psl, :, :],
                               eL[psl, j0:j0 + 4, last:last + 1].to_broadcast([64, 4, 64]), ALU.mult,
                               [B["tmpH"], beL], [bH], acc=True)
                yield
            cp(act, Hq[:, sd, :, :], H[:, sd, :, :], [bH], [bHq])
            st(ysc[d, g0:g0 + C, :], Ysb[:], [B["Ysb"]], [b_ysc])

        nchs = [T // C for (_, T) in cfg.seqs]
        units = []
        for i in range(max(nchs)):
            for si in range(len(cfg.seqs)):
                if i < nchs[si]:
                    units.append((si, i, 0))
                    units.append((si, nchs[si] - 1 - i, 1))

        def drain(gen):
            for _ in gen:
                pass

        prev = None
        for ui, (si, ch, d) in enumerate(units):
            u = mk_unit(si, ch, d, ui % 2)
            pg = prep(u)
            if prev is None:
                drain(pg)
            else:
                hg = heads(prev)
                r = 0
                pg_alive = True
                for _ in hg:
                    r += 1
                    if pg_alive and r % cfg.p2_every == 0:
                        try:
                            next(pg)
                        except StopIteration:
                            pg_alive = False
                if pg_alive:
                    drain(pg)
            prev = u
        drain(heads(prev))
        S.barrier()


def phase3(nc, cfg, S, l, E):
    g = lambda k: E[k]
    T_, P_, ld, st, tt, ts, actf, cp, mm, trp = (g(k) for k in
                                                   "T_ P_ ld st tt ts actf cp mm trp".split())
    pe, act, dve, pool = S.pe, S.act, S.dve, S.pool
    pcol, pder, b_par, identb, b_cst, cst = (g(k) for k in "pcol pder b_par identb b_cst cst".split())
    brow_d = g("brow_d")
    pT, b_pT, ysc, b_ysc, qkv, b_qkv = (g(k) for k in "pT b_pT ysc b_ysc qkv b_qkv".split())
    wpa_d, wpb_d, wo_d, rope_d = g("wpa_d"), g("wpb_d"), g("wo_d"), g("rope_d")
    x_src, x_dst, bx_src, bx_dst = g("x_src"), g("x_dst"), g("bx_src"), g("bx_dst")
    V, G, A, PE = nc.vector, nc.gpsimd, nc.scalar, nc.tensor
    identf = cst[:, CO_ID:CO_ID + 128]
    MPREV = cst[:, CO_MPREV:CO_MPREV + 128]
    MNEXT = cst[:, CO_MNEXT:CO_MNEXT + 128]
    with ExitStack() as ps:
        S.barrier()
        brow = T_(ps, "brow", [128, NBR])
        bder = T_(ps, "bder", [128, 64 + 16])
        ld(brow[:], brow_d[l, :].partition_broadcast(128), [b_par], acc=True)
        ts(dve, bder[:, 0:64], brow[:, BR_QG:BR_QG + 64], 0.125, ALU.mult, [b_par], [b_par], acc=True)
        actf(bder[:, 64:80], brow[:, BR_SINK:BR_SINK + 16], AF.Exp, [b_par], [b_par], acc=True)
        wts = [T_(ps, nm, [128, 8, D], BF16) for nm in ("wpa", "wpb", "wo")]
        wst = [T_(ps, f"wst3_{i}", [128, D]) for i in range(2)]
        yf, yb, bf_, bb_ = (T_(ps, nm, [128, D]) for nm in ("yf", "yb", "bf", "bb"))
        st16 = T_(ps, "st16", [128, 6, 16])
        gaT = T_(ps, "gaT", [128, 8, 128])
        yagT = T_(ps, "yagT", [128, 8, 128], BF16)
        qt = T_(ps, "qt", [128, D])
        tmpq = T_(ps, "tmpq", [128, D])
        qst = T_(ps, "qst", [128, 2, 16])
        qr = T_(ps, "qr", [128, D], BF16)
        qT = T_(ps, "qT", [64, 16, 128], BF16)
        kvraw = T_(ps, "kvraw", [128, 512])
        tmpk = T_(ps, "tmpk", [128, 256])
        kst = T_(ps, "kst", [128, 2, 4])
        kr = T_(ps, "kr", [128, 256], BF16)
        kT = [T_(ps, f"kT{i}", [64, 4, 128], BF16) for i in range(3)]
        vaug = [T_(ps, f"vaug{i}", [128, 4, 65], BF16) for i in range(3)]
        csq = T_(ps, "csq", [128, 64])
        csk = T_(ps, "csk", [128, 64])
        et = [T_(ps, f"et{i}", [128, 4, 128], BF16) for i in range(3)]
        den = T_(ps, "den", [128, 2, 4])
        og = T_(ps, "og", [128, D])
        gbT = T_(ps, "gbT", [128, 8, 128])
        ogT = T_(ps, "ogT", [128, 8, 128], BF16)
        mgT = T_(ps, "mgT", [128, 16, 128])
        t1 = T_(ps, "t1", [128, 8, 128])
        t2 = T_(ps, "t2", [128, 8, 128])
        mixT = T_(ps, "mixT", [128, 8, 128], BF16)
        xt = T_(ps, "xt3", [128, D])
        xo = T_(ps, "xo", [128, D])
        R0 = P_(ps, "R0", [128, D])
        R1 = P_(ps, "R1", [128, D])
        R2 = P_(ps, "R2", [128, D])
        R3 = P_(ps, "R3", [128, 512])
        R4 = P_(ps, "R4", [128, 512])
        names = ("w wst0 wst1 yf yb bf bb st16 gaT yagT qt tmpq qst qr qT kvraw tmpk kst kr kT0 kT1 kT2 "
                 "vaug0 vaug1 vaug2 csq csk et0 et1 et2 den og gbT ogT mgT t1 t2 mixT xt xo").split()
        B = {n: Buf(n) for n in names}
        for n_ in ("R0a", "R0b", "R1a", "R1b", "R2a", "R2b", "R3", "R4"):
            B[n_] = Buf(n_, True)
        bR0, bR1, bR2 = [B["R0a"], B["R0b"]], [B["R1a"], B["R1b"]], [B["R2a"], B["R2b"]]
        k = 0
        cast_engs = (act, pool, dve)
        for wi, wd_ in enumerate((wpa_d, wpb_d, wo_d)):
            for kc in range(8):
                i = k % 2
                ld(wst[i][:], wd_[l, kc * 128:(kc + 1) * 128, :], [B[f"wst{i}"]])
                cp(cast_engs[k % 3], wts[wi][:, kc, :], wst[i][:], [B[f"wst{i}"]], [B["w"]], acc=True)
                k += 1
        for i in range(3):
            S.op(pool, lambda: G.memset(vaug[i][:, :, 64:65], 1.0), (), [B[f"vaug{i}"]])
        wpa, wpb, wo = wts
        lng_bc = brow[:, BR_LNG:BR_LNG + D]
        lnb_bc = brow[:, BR_LNB:BR_LNB + D]
        qg8 = bder[:, 0:64]
        kg = brow[:, BR_KG:BR_KG + 64]
        esink = bder[:, 64:80]

        def v3(ap, h):
            return ap.rearrange("p (h v) -> p h v", h=h)

        def rope_norm(src3, nh, sq_t, st_t, cs_t, gvec, dst_bf, bsrc, bsq, bst, bcs, bdst):
            sq3 = v3(sq_t, nh)
            actf(sq_t, src3.rearrange("p h v -> p (h v)"), AF.Square, [bsrc], [bsq])
            S.op(dve, lambda: V.tensor_reduce(out=st_t[:, 0, :], in_=sq3, axis=AX.X, op=ALU.add), [bsq], [bst])
            actf(st_t[:, 1, :], st_t[:, 0, :], AF.Sqrt, [bst], [bst], bias=NORM_EPS, scale=1.0 / 64)
            S.op(dve, lambda: V.reciprocal(out=st_t[:, 1, :], in_=st_t[:, 1, :]), [bst], [bst])
            tt(pool, src3, src3, st_t[:, 1, :].unsqueeze(2).to_broadcast([128, nh, 64]), ALU.mult, [bsrc, bst], [bsrc])
            tt(pool, src3, src3, gvec.unsqueeze(1).to_broadcast([128, nh, 64]), ALU.mult, [bsrc, b_par], [bsrc])
            cos_bc = cs_t[:, 0:32].unsqueeze(1).to_broadcast([128, nh, 32])
            sin_bc = cs_t[:, 32:64].unsqueeze(1).to_broadcast([128, nh, 32])
            x1, x2 = src3[:, :, 0:32], src3[:, :, 32:64]
            s1_, s2_ = sq3[:, :, 0:32], sq3[:, :, 32:64]
            tt(pool, s1_, x1, cos_bc, ALU.mult, [bsrc, bcs], [bsq])
            tt(pool, s2_, x2, sin_bc, ALU.mult, [bsrc, bcs], [bsq])
            tt(pool, dst_bf[:, :, 0:32], s1_, s2_, ALU.subtract, [bsq], [bdst])
            tt(pool, s1_, x2, cos_bc, ALU.mult, [bsrc, bcs], [bsq])
            tt(pool, s2_, x1, sin_bc, ALU.mult, [bsrc, bcs], [bsq])
            tt(pool, dst_bf[:, :, 32:64], s1_, s2_, ALU.add, [bsq], [bdst])

        def proc_kv(s0, m):
            slot = m % 3
            gm = s0 + m * 128
            ld(kvraw[:], qkv[gm:gm + 128, 1024:1536], [B["kvraw"]], R=[b_qkv])
            ld(csk[:], rope_d[m * 128:(m + 1) * 128, :], [B["csk"]])
            cp(act, vaug[slot][:, :, 0:64], v3(kvraw[:, 256:512], 4), [B["kvraw"]], [B[f"vaug{slot}"]])
            rope_norm(v3(kvraw[:, 0:256], 4), 4, tmpk[:], kst, csk, kg, v3(kr[:], 4),
                      B["kvraw"], B["tmpk"], B["kst"], B["csk"], B["kr"])
            for gk in range(4):
                mm(R3[0:64, gk * 128:(gk + 1) * 128], kr[:, gk * 64:(gk + 1) * 64], identb[:], True, True,
                   [B["kr"], b_cst], [B["R3"]])
            cp(dve, kT[slot][:], R3[0:64, :].rearrange("p (g t) -> p g t", g=4), [B["R3"]], [B[f"kT{slot}"]])

        def tile3(si, n):
            s0, T = cfg.seqs[si]
            NB = T // 128
            g0 = s0 + n * 128
            ld(yf[:], ysc[0, g0:g0 + 128, :], [B["yf"]], R=[b_ysc])
            ld(yb[:], ysc[1, g0:g0 + 128, :], [B["yb"]], R=[b_ysc])
            ld(bf_[:], ysc[2, g0:g0 + 128, :], [B["bf"]], R=[b_ysc])
            ld(bb_[:], ysc[3, g0:g0 + 128, :], [B["bb"]], R=[b_ysc])
            ld(gaT[:], pT[3328:4352, g0:g0 + 128].rearrange("(j p) t -> p j t", p=128), [B["gaT"]], R=[b_pT])
            ld(qt[:], qkv[g0:g0 + 128, 0:1024], [B["qt"]], R=[b_qkv])
            ld(csq[:], rope_d[n * 128:(n + 1) * 128, :], [B["csq"]])
            ld(gbT[:], pT[5888:6912, g0:g0 + 128].rearrange("(j p) t -> p j t", p=128), [B["gbT"]], R=[b_pT])
            ld(mgT[:, 0:8, :], pT[6912:7936, g0:g0 + 128].rearrange("(j p) t -> p j t", p=128), [B["mgT"]], R=[b_pT])
            ld(mgT[:, 8:16, :], pT[7936:8960, g0:g0 + 128].rearrange("(j p) t -> p j t", p=128), [B["mgT"]], R=[b_pT],
               acc=True)
            ld(xt[:], x_src(l, g0, 128), [B["xt"]], R=[bx_src(l)])
            def genA():
                tt(pool, yf[:], yf[:], yb[:], ALU.add, [B["yf"], B["yb"]], [B["yf"]])
                tt(pool, bf_[:], bf_[:], bb_[:], ALU.add, [B["bf"], B["bb"]], [B["bf"]])
                tt(pool, bf_[:], bf_[:], lnb_bc, ALU.add, [B["bf"], b_par], [B["bf"]])
                yield
                S.op(dve, lambda: V.tensor_reduce(out=st16[:, 0, :], in_=v3(yf[:], 16), axis=AX.X, op=ALU.add),
                     [B["yf"]], [B["st16"]])
                actf(yb[:], yf[:], AF.Square, [B["yf"]], [B["yb"]])
                S.op(dve, lambda: V.tensor_reduce(out=st16[:, 1, :], in_=v3(yb[:], 16), axis=AX.X, op=ALU.add),
                     [B["yb"]], [B["st16"]])
                ts(dve, st16[:, 2, :], st16[:, 0, :], 1.0 / 64, ALU.mult, [B["st16"]], [B["st16"]])
                tt(dve, st16[:, 3, :], st16[:, 2, :], st16[:, 2, :], ALU.mult, [B["st16"]], [B["st16"]])
                S.op(dve, lambda: V.scalar_tensor_tensor(out=st16[:, 4, :], in0=st16[:, 1, :], scalar=1.0 / 64,
                                                         in1=st16[:, 3, :], op0=ALU.mult, op1=ALU.subtract),
                     [B["st16"]], [B["st16"]])
                actf(st16[:, 4, :], st16[:, 4, :], AF.Sqrt, [B["st16"]], [B["st16"]], bias=GN_EPS)
                S.op(dve, lambda: V.reciprocal(out=st16[:, 5, :], in_=st16[:, 4, :]), [B["st16"]], [B["st16"]])
                yield
                tt(pool, v3(yb[:], 16), v3(yf[:], 16), st16[:, 2, :].unsqueeze(2).to_broadcast([128, 16, 64]),
                   ALU.subtract, [B["yf"], B["st16"]], [B["yb"]])
                tt(pool, v3(yb[:], 16), v3(yb[:], 16), st16[:, 5, :].unsqueeze(2).to_broadcast([128, 16, 64]),
                   ALU.mult, [B["yb"], B["st16"]], [B["yb"]])
                tt(pool, yb[:], yb[:], lng_bc, ALU.mult, [B["yb"], b_par], [B["yb"]])
                tt(pool, yb[:], yb[:], bf_[:], ALU.add, [B["yb"], B["bf"]], [B["yb"]])
                yield
                for kc in range(8):
                    trp(R0[:, kc * 128:(kc + 1) * 128], yb[:, kc * 128:(kc + 1) * 128], identf, [B["yb"], b_cst],
                        [bR0[kc // 4]])
                actf(gaT[:], gaT[:], AF.Silu, [B["gaT"]], [B["gaT"]])
                tt(dve, yagT[:], R0[:].rearrange("p (j t) -> p j t", j=8), gaT[:], ALU.mult, bR0 + [B["gaT"]], [B["yagT"]])
                yield
                for db in range(8):
                    for kc in range(8):
                        mm(R1[:, db * 128:(db + 1) * 128], wpa[:, kc, db * 128:(db + 1) * 128], yagT[:, kc, :],
                           kc == 0, kc == 7, [B["w"], B["yagT"]], [bR1[db // 4]])
                    if db % 2 == 1:
                        yield

            def genB():
                if n == 0:
                    proc_kv(s0, 0)
                    yield
                if n + 1 < NB:
                    proc_kv(s0, n + 1)
                    yield
                rope_norm(v3(qt[:], 16), 16, tmpq[:], qst, csq, qg8, v3(qr[:], 16),
                          B["qt"], B["tmpq"], B["qst"], B["csq"], B["qr"])
                for rnd in range(2):
                    for hh in range(8):
                        h = rnd * 8 + hh
                        mm(R2[0:64, hh * 128:(hh + 1) * 128], qr[:, h * 64:(h + 1) * 64], identb[:], True, True,
                           [B["qr"], b_cst], [bR2[hh // 4]])
                    cp(act, qT[:, rnd * 8:(rnd + 1) * 8, :], R2[0:64, :].rearrange("p (h t) -> p h t", h=8), bR2,
                       [B["qT"]], acc=(rnd == 1))
                    yield
                cbs = [m for m in (n - 1, n, n + 1) if 0 <= m < NB]
                for gq in range(4):
                    for ci, m in enumerate(cbs):
                        slot = m % 3
                        mm(R3[:, :], kT[slot][:, gq, :], qT[:, 4 * gq:4 * gq + 4, :].rearrange("p h t -> p (h t)"),
                           True, True, [B[f"kT{slot}"], B["qT"]], [B["R3"]])
                        actf(et[ci][:].rearrange("p h t -> p (h t)"), R3[:, :], AF.Exp, [B["R3"]], [B[f"et{ci}"]])
                        if m != n:
                            msk = MPREV if m < n else MNEXT
                            tt(pool, et[ci][:], et[ci][:], msk.unsqueeze(1).to_broadcast([128, 4, 128]), ALU.mult,
                               [B[f"et{ci}"], b_cst], [B[f"et{ci}"]])
                    yield
                    for i4 in range(4):
                        for ci, m in enumerate(cbs):
                            slot = m % 3
                            mm(R4[:, i4 * 65:(i4 + 1) * 65], et[ci][:, i4, :], vaug[slot][:, gq, :],
                               ci == 0, ci == len(cbs) - 1, [B[f"et{ci}"], B[f"vaug{slot}"]], [B["R4"]])
                    r4v = R4[:, 0:260].rearrange("p (h v) -> p h v", h=4)
                    tt(dve, den[:, 0, :], r4v[:, :, 64], esink[:, 4 * gq:4 * gq + 4], ALU.add, [B["R4"], b_par], [B["den"]])
                    S.op(dve, lambda: V.reciprocal(out=den[:, 1, :], in_=den[:, 0, :]), [B["den"]], [B["den"]])
                    tt(dve, v3(og[:], 16)[:, 4 * gq:4 * gq + 4, :], r4v[:, :, 0:64],
                       den[:, 1, :].unsqueeze(2).to_broadcast([128, 4, 64]), ALU.mult, [B["R4"], B["den"]], [B["og"]],
                       acc=(gq > 0))
                    yield

            gens = [genA(), genB()]
            while gens:
                for gen_ in list(gens):
                    try:
                        next(gen_)
                    except StopIteration:
                        gens.remove(gen_)
            for kc in range(8):
                trp(R0[:, kc * 128:(kc + 1) * 128], og[:, kc * 128:(kc + 1) * 128], identf, [B["og"], b_cst],
                    [bR0[kc // 4]])
            actf(gbT[:], gbT[:], AF.Silu, [B["gbT"]], [B["gbT"]])
            tt(dve, ogT[:], R0[:].rearrange("p (j t) -> p j t", j=8), gbT[:], ALU.mult, bR0 + [B["gbT"]], [B["ogT"]])
            for db in range(8):
                for kc in range(8):
                    mm(R2[:, db * 128:(db + 1) * 128], wpb[:, kc, db * 128:(db + 1) * 128], ogT[:, kc, :],
                       kc == 0, kc == 7, [B["w"], B["ogT"]], [bR2[db // 4]])
            actf(mgT[:], mgT[:], AF.Sigmoid, [B["mgT"]], [B["mgT"]])
            tt(dve, t1[:], R1[:].rearrange("p (j t) -> p j t", j=8), mgT[:, 0:8, :], ALU.mult, bR1 + [B["mgT"]], [B["t1"]])
            tt(dve, t2[:], R2[:].rearrange("p (j t) -> p j t", j=8), mgT[:, 8:16, :], ALU.mult, bR2 + [B["mgT"]], [B["t2"]])
            tt(pool, mixT[:], t1[:], t2[:], ALU.add, [B["t1"], B["t2"]], [B["mixT"]])
            for hf, (Rb, bRb) in enumerate(((R3, B["R3"]), (R4, B["R4"]))):
                for kc in range(8):
                    mm(Rb[:, :], mixT[:, kc, :], wo[:, kc, hf * 512:(hf + 1) * 512], kc == 0, kc == 7,
                       [B["mixT"], B["w"]], [bRb])
                tt(dve, xo[:, hf * 512:(hf + 1) * 512], Rb[:, :], xt[:, hf * 512:(hf + 1) * 512], ALU.add,
                   [bRb, B["xt"]], [B["xo"]], acc=(hf == 1))
            st(x_dst(l, g0, 128), xo[:], [B["xo"]], [bx_dst(l)])

        S.barrier()
        for si in range(len(cfg.seqs)):
            for n in range(cfg.seqs[si][1] // 128):
                tile3(si, n)
        S.barrier()


def host_params(inp, L):
    f = lambda k: np.asarray(inp[k], np.float32)
    pcol = np.zeros((L, 128, NPC), np.float32)
    mu = f("shift_mu")
    for l in range(L):
        pcol[l, :, PC_NG:PC_NG + 8] = f("norm_g")[l].reshape(8, 128).T
        pcol[l, :, PC_MU:PC_MU + 24] = mu[l, :3072].reshape(24, 128).T
        pcol[l, :64, PC_MUWD] = mu[l, 3072:3136]
        pcol[l, :64, PC_MUWD + 1] = mu[l, 3136:3200]
        pcol[l, :64, PC_MUAD] = mu[l, 3200:3264]
        pcol[l, :64, PC_MUAD + 1] = mu[l, 3264:3328]
        pcol[l, :, PC_KK:PC_KK + 8] = f("k_k")[l].reshape(8, 128).T
        pcol[l, :, PC_KA:PC_KA + 8] = f("k_a")[l].reshape(8, 128).T
        pcol[l, :, PC_RK:PC_RK + 8] = f("r_k")[l].reshape(8, 128).T
    brow = np.concatenate([f("ln_x_g"), f("ln_x_b"), f("q_norm_g"), f("k_norm_g"), f("sink")], axis=1)
    wup = np.concatenate([f("w_lora_up"), f("w0")[:, :, None, :]], axis=2)
    aup = np.concatenate([f("a_lora_up"), f("a0")[:, :, None, :]], axis=2)
    return dict(pcol=pcol, brow=np.ascontiguousarray(brow), wup_aug=np.ascontiguousarray(wup),
                aup_aug=np.ascontiguousarray(aup))


def make_in_maps(inp, cfg, n_cores=8):
    L = cfg.depth
    hp = host_params(inp, L)
    consts = make_consts()
    rope = make_rope(max(cfg.TP, cfg.TS))
    shared = dict(w_in=np.ascontiguousarray(inp["w_in"][:L], dtype=np.float32),
                  w_proj_a=np.ascontiguousarray(inp["w_proj_a"][:L], dtype=np.float32),
                  w_proj_b=np.ascontiguousarray(inp["w_proj_b"][:L], dtype=np.float32),
                  w_out=np.ascontiguousarray(inp["w_out"][:L], dtype=np.float32),
                  consts=consts, rope=rope, **hp)
    xp, xs = np.asarray(inp["x_prompt"]), np.asarray(inp["x_sample"])
    maps = []
    for c in range(n_cores):
        m = dict(shared)
        m["xp"] = np.ascontiguousarray(xp[c % xp.shape[0]], dtype=np.float32)
        m["xs"] = np.ascontiguousarray(xs[c % xs.shape[0]], dtype=np.float32)
        maps.append(m)
    return maps


def kernel(**inputs):
    xp, xs = inputs["x_prompt"], inputs["x_sample"]
    cfg = Cfg(xp.shape[1], xs.shape[1], inputs["w_in"].shape[0])
    nc = build(cfg)
    maps = make_in_maps(inputs, cfg)
    res = run_bass_kernel_spmd(nc, maps, core_ids=list(range(8)))
    y_p = np.stack([res.results[c]["yp"] for c in range(xp.shape[0])], axis=0).astype(np.float32)
    y_s = np.stack([res.results[c]["ys"] for c in range(xs.shape[0])], axis=0).astype(np.float32)
    return (y_p, y_s)
```

```python
import numpy as np
from contextlib import ExitStack
import concourse.bass as bass
import concourse.mybir as mybir
from concourse.bass_utils import run_bass_kernel_spmd

F32 = mybir.dt.float32
BF16 = mybir.dt.bfloat16
AF = mybir.ActivationFunctionType
ALU = mybir.AluOpType
AX = mybir.AxisListType

D = 1024
NIN = 8960
C = 128
NORM_EPS = 1e-6
GN_EPS = 64e-5
NEG_E = -float(np.exp(-0.5))

PC_NG = 0
PC_MU = 8
PC_MUWD = 32
PC_MUAD = 34
PC_KK = 36
PC_KA = 44
PC_RK = 52
NPC = 60
BR_LNG = 0
BR_LNB = 1024
BR_QG = 2048
BR_KG = 2112
BR_SINK = 2176
NBR = 2192
CO_ID = 0
CO_MTF = 128
CO_MTB = CO_MTF + 512
CO_M3F = CO_MTB + 512
CO_M3B = CO_M3F + 128
CO_TRF = CO_M3B + 128
CO_TRB = CO_TRF + 256
CO_BO = CO_TRB + 256
CO_HS = CO_BO + 128
CO_MPREV = CO_HS + 2
CO_MNEXT = CO_MPREV + 128
NCO = CO_MNEXT + 128


def make_consts():
    c = np.zeros((128, NCO), np.float32)
    i = np.arange(128)
    s, t = i[:, None], i[None, :]
    c[:, CO_ID:CO_ID + 128] = (s == t)
    for (o, lt, le) in ((CO_MTF, s < t, s <= t), (CO_MTB, s > t, s >= t)):
        c[:, o:o + 128] = lt
        c[:, o + 128:o + 256] = le
        c[:, o + 256:o + 384] = -1.0 * le
        c[:, o + 384:o + 512] = -1.0 * lt
    c[:, CO_M3F:CO_M3F + 128] = -1.0 * (t < s)
    c[:, CO_M3B:CO_M3B + 128] = -1.0 * (t > s)
    c[:, CO_TRF:CO_TRF + 128] = NEG_E * (s <= t)
    c[:, CO_TRF + 128:CO_TRF + 256] = NEG_E * (s < t)
    c[:, CO_TRB:CO_TRB + 128] = NEG_E * (s >= t)
    c[:, CO_TRB + 128:CO_TRB + 256] = NEG_E * (s > t)
    c[:, CO_BO:CO_BO + 128] = (s // 64 == t // 64)
    c[:, CO_HS] = (i // 64 == 0)
    c[:, CO_HS + 1] = (i // 64 == 1)
    c[:, CO_MPREV:CO_MPREV + 128] = (s >= t)
    c[:, CO_MNEXT:CO_MNEXT + 128] = (s <= t)
    return c


def make_rope(tmax):
    inv = 1.0 / (10000.0 ** (np.arange(0, 64, 2, dtype=np.float32) / 64))
    ang = np.arange(tmax, dtype=np.float32)[:, None] * inv[None, :].astype(np.float32)
    return np.concatenate([np.cos(ang), np.sin(ang)], axis=1).astype(np.float32)


class Buf:
    __slots__ = ("name", "writers", "readers", "base", "excl")

    def __init__(self, name, excl=False):
        self.name = name
        self.excl = excl
        self.writers = {}
        self.readers = {}
        self.base = {}


class Eng:
    def __init__(self, name, h, sem):
        self.name = name
        self.h = h
        self.sem = sem
        self.count = 0
        self.waited = {}


class DmaQ:
    def __init__(self, eng, sems):
        self.eng = eng
        self.sems = sems
        self.n = 0


class Sched:
    def __init__(self, nc, es, n_dma_sems=8):
        self.nc = nc
        mk = lambda nm: es.enter_context(nc.semaphore(nm))
        self.pe = Eng("pe", nc.tensor, mk("s_pe"))
        self.act = Eng("act", nc.scalar, mk("s_act"))
        self.dve = Eng("dve", nc.vector, mk("s_dve"))
        self.pool = Eng("pool", nc.gpsimd, mk("s_pool"))
        self.sp = Eng("sp", nc.sync, mk("s_sp"))
        self.engs = (self.pe, self.act, self.dve, self.pool, self.sp)
        self.sems = {}
        for e in self.engs:
            self.sems[id(e.sem)] = e.sem
        self.q_ld = DmaQ(self.sp, [mk(f"s_ld{i}") for i in range(n_dma_sems)])
        self.q_st = DmaQ(self.pool, [mk(f"s_st{i}") for i in range(n_dma_sems)])
        self.q_l2 = DmaQ(self.act, [mk(f"s_lb{i}") for i in range(n_dma_sems)])
        self.queues = (self.q_ld, self.q_st, self.q_l2)
        for q in self.queues:
            for s in q.sems:
                self.sems[id(s)] = s
        self.n_inst = 0

    def _wait(self, eng, deps, same_ok=True):
        for sid, val in deps.items():
            if eng.waited.get(sid, 0) >= val:
                continue
            if same_ok and sid == id(eng.sem):
                continue
            eng.h.wait_ge(self.sems[sid], val)
            eng.waited[sid] = val

    @staticmethod
    def _merge(d, o):
        for k, v in o.items():
            if d.get(k, 0) < v:
                d[k] = v

    def _deps(self, reads, writes, acc):
        deps = {}
        for b in reads:
            self._merge(deps, b.writers)
            if b.excl:
                self._merge(deps, b.readers)
        for b in writes:
            self._merge(deps, b.readers)
            if not acc:
                self._merge(deps, b.writers)
            else:
                self._merge(deps, b.base)
        return deps

    def _commit(self, reads, writes, tok, acc, deps=None):
        sid, val = tok
        for b in reads:
            if b.readers.get(sid, 0) < val:
                b.readers[sid] = val
        for b in writes:
            if acc:
                if b.writers.get(sid, 0) < val:
                    b.writers[sid] = val
            else:
                b.writers = {sid: val}
                b.readers = {}
                b.base = dict(deps or {})

    def op(self, eng, fn, reads=(), writes=(), acc=False):
        deps = self._deps(reads, writes, acc)
        self._wait(eng, deps, same_ok=(eng is self.pe))
        inst = fn()
        eng.count += 1
        inst.then_inc(eng.sem, 1)
        self._commit(reads, writes, (id(eng.sem), eng.count), acc, deps)
        self.n_inst += 1
        return inst

    def dma(self, q, out, in_, reads=(), writes=(), acc=False, **kw):
        eng = q.eng
        K = len(q.sems)
        i = q.n
        sem = q.sems[i % K]
        val = 16 * (i // K + 1)
        deps = self._deps(reads, writes, acc)
        if i >= K:
            self._merge(deps, {id(sem): val - 16})
        self._wait(eng, deps, same_ok=False)
        eng.h.dma_start(out=out, in_=in_, **kw).then_inc(sem, 16)
        q.n += 1
        self._commit(reads, writes, (id(sem), val), acc, deps)
        self.n_inst += 1

    def all_tokens(self):
        deps = {}
        for e in self.engs:
            if e.count:
                deps[id(e.sem)] = e.count
        for q in self.queues:
            K = len(q.sems)
            for j, s in enumerate(q.sems):
                n = (q.n - j + K - 1) // K if q.n > j else 0
                if n:
                    deps[id(s)] = 16 * n
        return deps

    def barrier(self):
        deps = self.all_tokens()
        for e in self.engs:
            self._wait(e, deps, same_ok=True)

    def finish(self):
        self._wait(self.sp, self.all_tokens(), same_ok=False)


class Cfg:
    def __init__(self, TP, TS, depth, debug=False):
        self.TP, self.TS, self.depth, self.debug = TP, TS, depth, debug
        self.p2stage = 99
        self.p2_every = 2
        self.TA = TP + TS
        self.seqs = [(0, TP), (TP, TS)]


def build(cfg):
    nc = bass.Bass("TRN2", target_bir_lowering=False)
    L = cfg.depth
    TA = cfg.TA
    dt_in = lambda name, shape: nc.dram_tensor(name, shape, F32, kind="ExternalInput").ap()
    xp = dt_in("xp", [cfg.TP, D])
    xs = dt_in("xs", [cfg.TS, D])
    w_in = dt_in("w_in", [L, D, NIN])
    wpa_d = dt_in("w_proj_a", [L, D, D])
    wpb_d = dt_in("w_proj_b", [L, D, D])
    wo_d = dt_in("w_out", [L, D, D])
    pcol_d = dt_in("pcol", [L, 128, NPC])
    brow_d = dt_in("brow", [L, NBR])
    wup_d = dt_in("wup_aug", [L, 2, 65, D])
    aup_d = dt_in("aup_aug", [L, 2, 65, D])
    consts_d = dt_in("consts", [128, NCO])
    rope_d = dt_in("rope", [max(cfg.TP, cfg.TS), 64])
    yp = nc.dram_tensor("yp", [cfg.TP, D], F32, kind="ExternalOutput").ap()
    ys = nc.dram_tensor("ys", [cfg.TS, D], F32, kind="ExternalOutput").ap()
    dbg_kind = "ExternalOutput" if cfg.debug else "Internal"
    _pt_parts = [(0, 3328, nc.dram_tensor("pT_slab", [3328, TA], F32, kind=dbg_kind).ap()),
                 (3328, 4352, nc.dram_tensor("pT_ga", [1024, TA], F32).ap()),
                 (5888, 6912, nc.dram_tensor("pT_gb", [1024, TA], F32).ap()),
                 (6912, 8960, nc.dram_tensor("pT_mg", [2048, TA], F32).ap())]

    class _PT:
        def __getitem__(self, key):
            rs, cs = key
            for (a, b, ap) in _pt_parts:
                if a <= rs.start and rs.stop <= b:
                    return ap[rs.start - a:rs.stop - a, cs]
            raise KeyError(key)
    pT = _PT()
    qkv = nc.dram_tensor("qkv", [TA, 1536], F32, kind=dbg_kind).ap()
    ysc = nc.dram_tensor("ysc", [4, TA, D], F32, kind=dbg_kind).ap()
    xsc = [nc.dram_tensor(f"xsc{i}", [TA, D], F32).ap() for i in range(2)]

    def x_src(l, g0, n):
        if l == 0:
            return xp[g0:g0 + n, :] if g0 < cfg.TP else xs[g0 - cfg.TP:g0 - cfg.TP + n, :]
        return xsc[(l - 1) % 2][g0:g0 + n, :]

    def x_dst(l, g0, n):
        if l == L - 1:
            return yp[g0:g0 + n, :] if g0 < cfg.TP else ys[g0 - cfg.TP:g0 - cfg.TP + n, :]
        return xsc[l % 2][g0:g0 + n, :]

    with ExitStack() as es:
        S = Sched(nc, es)
        pe, act, dve, pool = S.pe, S.act, S.dve, S.pool
        V, G, A, PE = nc.vector, nc.gpsimd, nc.scalar, nc.tensor

        def EH(e):
            return {"dve": V, "pool": G, "act": A}[e.name]

        def tt(e, out, in0, in1, op, R, W, acc=False):
            h = EH(e)
            S.op(e, lambda: h.tensor_tensor(out=out, in0=in0, in1=in1, op=op), R, W, acc)

        def ts(e, out, in0, s1, op0, R, W, s2=None, op1=None, acc=False):
            h = EH(e)
            if op1 is None:
                S.op(e, lambda: h.tensor_scalar(out=out, in0=in0, scalar1=s1, scalar2=None, op0=op0), R, W, acc)
            else:
                S.op(e, lambda: h.tensor_scalar(out=out, in0=in0, scalar1=s1, scalar2=s2, op0=op0, op1=op1), R, W, acc)

        def actf(out, in_, func, R, W, bias=0.0, scale=1.0, acc=False):
            S.op(act, lambda: A.activation(out=out, in_=in_, func=func, bias=bias, scale=scale), R, W, acc)

        def cp(e, out, in_, R, W, acc=False):
            if e is act:
                S.op(act, lambda: A.copy(out=out, in_=in_), R, W, acc)
            else:
                h = EH(e)
                S.op(e, lambda: h.tensor_copy(out=out, in_=in_), R, W, acc)

        def mm(out, lhsT, rhs, start, stop, R, W, acc=False, skip=False):
            if skip:
                S.op(pe, lambda: PE.matmul(out, lhsT, rhs, start=start, stop=stop, skip_group_check=True), R, W, acc)
            else:
                S.op(pe, lambda: PE.matmul(out, lhsT, rhs, start=start, stop=stop), R, W, acc)

        def trp(out, in_, ident, R, W):
            S.op(pe, lambda: PE.transpose(out, in_, ident), R, W)

        def ld(out, in_, W, R=(), q=None, **kw):
            S.dma(q or S.q_ld, out, in_, reads=R, writes=W, **kw)

        def st(out, in_, R, W, acc=True):
            S.dma(S.q_st, out, in_, reads=R, writes=W, acc=acc)

        uid = [0]

        def T_(st_, name, shape, dt=F32):
            uid[0] += 1
            return st_.enter_context(nc.sbuf_tensor(f"{name}_{uid[0]}", shape, dt))

        def P_(st_, name, shape, dt=F32):
            uid[0] += 1
            return st_.enter_context(nc.psum_tensor(f"{name}_{uid[0]}", shape, dt))
        cst = T_(es, "cst", [128, NCO])
        identb = T_(es, "identb", [128, 128], BF16)
        b_cst = Buf("cst")
        ld(cst[:], consts_d[:, :], [b_cst])
        cp(pool, identb[:], cst[:, CO_ID:CO_ID + 128], [b_cst], [b_cst])
        identf = cst[:, CO_ID:CO_ID + 128]
        b_pT, b_qkv, b_ysc = Buf("pT"), Buf("qkv"), Buf("ysc")
        b_x = [Buf("x_in"), Buf("xsc0"), Buf("xsc1"), Buf("y_out")]

        def bx_src(l):
            return b_x[0] if l == 0 else b_x[1 + (l - 1) % 2]

        def bx_dst(l):
            return b_x[3] if l == L - 1 else b_x[1 + l % 2]

        for l in range(L):
            with ExitStack() as ls:
                pcol = T_(ls, "pcol", [128, NPC])
                pder = T_(ls, "pder", [128, 64])
                b_par = Buf("par")
                S.barrier()
                ld(pcol[:], pcol_d[l, :, :], [b_par])
                ts(dve, pder[:, 0:24], pcol[:, PC_MU:PC_MU + 24], 0.5, ALU.mult, [b_par], [b_par])
                ts(dve, pder[:, 24:48], pcol[:, PC_MU:PC_MU + 24], -1.0, ALU.mult, [b_par], [b_par], 1.0, ALU.add)
                ts(dve, pder[:, 48:56], pcol[:, PC_KA:PC_KA + 8], -1.0, ALU.mult, [b_par], [b_par], 1.0, ALU.add)
                ts(dve, pder[:, 56:60], pcol[:, PC_MUWD:PC_MUWD + 4], 0.5, ALU.mult, [b_par], [b_par])
                ts(dve, pder[:, 60:64], pcol[:, PC_MUWD:PC_MUWD + 4], -1.0, ALU.mult, [b_par], [b_par], 1.0, ALU.add)

                phase1(nc, cfg, S, l, locals())
                if cfg.debug == "p1":
                    break
                phase2(nc, cfg, S, l, locals())
                if cfg.debug == "p2":
                    break
                phase3(nc, cfg, S, l, locals())
        S.barrier()
        S.finish()
    return nc


FM_BLOCKS = list(range(0, 34)) + list(range(46, 70))
TM_COL0 = 4352


def phase1(nc, cfg, S, l, E):
    g = lambda k: E[k]
    T_, P_, ld, st, tt, ts, actf, cp, mm, trp = (g(k) for k in
                                                   "T_ P_ ld st tt ts actf cp mm trp".split())
    pe, act, dve, pool = S.pe, S.act, S.dve, S.pool
    pcol, b_par, identb, b_cst = g("pcol"), g("b_par"), g("identb"), g("b_cst")
    w_in, pT, qkv, b_pT, b_qkv = g("w_in"), g("pT"), g("qkv"), g("b_pT"), g("b_qkv")
    x_src, bx_src = g("x_src"), g("bx_src")
    V, G, A, PE = nc.vector, nc.gpsimd, nc.scalar, nc.tensor
    TW = 512
    with ExitStack() as ps:
        S.barrier()
        wbf = T_(ps, "wbf", [128, 8, NIN], BF16)
        wst = [T_(ps, f"wst{i}", [128, 896]) for i in range(2)]
        hT2 = [T_(ps, f"hT{i}", [128, 8, TW], BF16) for i in range(2)]
        xt = [T_(ps, f"xt{i}", [128, D]) for i in range(2)]
        xsq = T_(ps, "xsq", [128, D])
        xn = [T_(ps, f"xn{i}", [128, D], BF16) for i in range(2)]
        ss = [T_(ps, f"ss{i}", [128, 2]) for i in range(2)]
        stg = [T_(ps, f"stg{i}", [128, TW]) for i in range(4)]
        ptr = [P_(ps, f"ptr{i}", [128, 8, 128], BF16) for i in range(2)]
        pout = [P_(ps, f"pout{i}", [128, TW]) for i in range(4)]
        b_w = Buf("wbf")
        b_wst = [Buf("wst0"), Buf("wst1")]
        b_hT2 = [Buf("hT0"), Buf("hT1")]
        b_xt = [Buf("xt0"), Buf("xt1")]
        b_xsq = Buf("xsq")
        b_xn = [Buf("xn0"), Buf("xn1")]
        b_ss = [Buf("ss0"), Buf("ss1")]
        b_stg = [Buf(f"stg{i}") for i in range(4)]
        b_ptr = [Buf("ptr0", True), Buf("ptr1", True)]
        b_pout = [Buf(f"pout{i}", True) for i in range(4)]
        k = 0
        cast_engs = (act, pool, dve)
        for kc in range(8):
            for cc in range(10):
                i = k % 2
                ld(wst[i][:], w_in[l, kc * 128:(kc + 1) * 128, cc * 896:(cc + 1) * 896], [b_wst[i]])
                cp(cast_engs[k % 3], wbf[:, kc, cc * 896:(cc + 1) * 896], wst[i][:], [b_wst[i]], [b_w], acc=True)
                k += 1
        gcol = pcol[:, PC_NG:PC_NG + 8]
        n_tiles = cfg.TA // TW
        sub = 0
        oi = 0
        for ti in range(n_tiles):
            g0 = ti * TW
            hT, b_hT = hT2[ti % 2], b_hT2[ti % 2]
            for s4 in range(TW // 128):
                i = sub % 2
                sub += 1
                gs = g0 + s4 * 128
                ld(xt[i][:], x_src(l, gs, 128), [b_xt[i]], R=[bx_src(l)])
                actf(xsq[:], xt[i][:], AF.Square, [b_xt[i]], [b_xsq])
                S.op(dve, lambda: V.reduce_sum(out=ss[i][:, 0:1], in_=xsq[:], axis=AX.X), [b_xsq], [b_ss[i]])
                actf(ss[i][:, 1:2], ss[i][:, 0:1], AF.Sqrt, [b_ss[i]], [b_ss[i]], bias=NORM_EPS, scale=1.0 / D)
                S.op(dve, lambda: V.reciprocal(out=ss[i][:, 1:2], in_=ss[i][:, 1:2]), [b_ss[i]], [b_ss[i]])
                ts(dve, xn[i][:], xt[i][:], ss[i][:, 1:2], ALU.mult, [b_xt[i], b_ss[i]], [b_xn[i]])
                for kc in range(8):
                    trp(ptr[i][:, kc, :], xn[i][:, kc * 128:(kc + 1) * 128], identb[:], [b_xn[i], b_cst], [b_ptr[i]])
                tt(dve, hT[:, :, s4 * 128:(s4 + 1) * 128], ptr[i][:],
                   gcol.unsqueeze(2).to_broadcast([128, 8, 128]), ALU.mult, [b_ptr[i], b_par], [b_hT])
            for fb in FM_BLOCKS:
                o = oi % 4
                oi += 1
                for kc in range(8):
                    mm(pout[o][:], wbf[:, kc, fb * 128:(fb + 1) * 128], hT[:, kc, :], kc == 0, kc == 7,
                       [b_w, b_hT], [b_pout[o]])
                cp(act if o % 2 == 0 else dve, stg[o][:], pout[o][:], [b_pout[o]], [b_stg[o]])
                st(pT[fb * 128:(fb + 1) * 128, g0:g0 + TW], stg[o][:], [b_stg[o]], [b_pT])
            for s4 in range(TW // 128):
                for cg in range(3):
                    o = oi % 4
                    oi += 1
                    for kc in range(8):
                        mm(pout[o][:], hT[:, kc, s4 * 128:(s4 + 1) * 128],
                           wbf[:, kc, TM_COL0 + cg * 512:TM_COL0 + (cg + 1) * 512], kc == 0, kc == 7,
                           [b_w, b_hT], [b_pout[o]])
                    cp(act if o % 2 == 0 else dve, stg[o][:], pout[o][:], [b_pout[o]], [b_stg[o]])
                    st(qkv[g0 + s4 * 128:g0 + (s4 + 1) * 128, cg * 512:(cg + 1) * 512], stg[o][:],
                       [b_stg[o]], [b_qkv])
        S.barrier()


def phase2(nc, cfg, S, l, E):
    g = lambda k: E[k]
    T_, P_, ld, st, tt, ts, actf, cp, mm, trp = (g(k) for k in
                                                   "T_ P_ ld st tt ts actf cp mm trp".split())
    pe, act, dve, pool = S.pe, S.act, S.dve, S.pool
    pcol, pder, b_par, identb, b_cst, cst = (g(k) for k in "pcol pder b_par identb b_cst cst".split())
    pT, b_pT, ysc, b_ysc = g("pT"), g("b_pT"), g("ysc"), g("b_ysc")
    wup_d, aup_d = g("wup_d"), g("aup_d")
    V, G, A, PE = nc.vector, nc.gpsimd, nc.scalar, nc.tensor
    identf = cst[:, CO_ID:CO_ID + 128]
    BO = cst[:, CO_BO:CO_BO + 128]
    HS = cst[:, CO_HS:CO_HS + 2]
    C2, C4 = 2 * C, 4 * C
    NG = 4
    with ExitStack() as ps:
        S.barrier()
        wup = T_(ps, "wup", [65, 2, D])
        aup = T_(ps, "aup", [65, 2, D])
        H = T_(ps, "H", [128, 4, 8, 64])
        Hq = T_(ps, "Hq", [128, 4, 8, 64], BF16)
        slab = T_(ps, "slab", [128, 24, C + 2])
        wdt = T_(ps, "wdt", [64, C + 2])
        adt = T_(ps, "adt", [64, C + 2])
        wtmp = T_(ps, "wtmp", [64, C])
        atmp = T_(ps, "atmp", [64, C])
        wds = T_(ps, "wds", [65, C])
        ads = T_(ps, "ads", [65, C])
        sg = T_(ps, "sg", [128, D])
        eN = T_(ps, "eN", [128, 8, C])
        eX = T_(ps, "eX", [128, 8, C])
        aT = T_(ps, "aT", [128, 8, C])
        tmp24 = T_(ps, "tmp24", [128, 24, C])
        sh24 = T_(ps, "sh24", [128, 24, C])
        kkn = T_(ps, "kkn", [128, 8, C])
        sq = T_(ps, "sq", [128, 8, C])
        bb = T_(ps, "bb", [128, 8, C])
        kmod = T_(ps, "kmod", [128, 8, C])
        prod = T_(ps, "prod", [128, 8, C])
        cc = T_(ps, "cc", [128, 16])
        bon = T_(ps, "bon", [128, D])
        eL2 = [T_(ps, f"eL{i}", [128, 8, C]) for i in range(2)]
        QR2 = [T_(ps, f"QR{i}", [128, 8, 3, C], BF16) for i in range(2)]
        KhT2 = [T_(ps, f"KhT{i}", [128, 8, C], BF16) for i in range(2)]
        BhT2 = [T_(ps, f"BhT{i}", [128, 8, C], BF16) for i in range(2)]
        Vb2 = [T_(ps, f"Vb{i}", [128, D], BF16) for i in range(2)]
        Kh2 = [T_(ps, f"Kh{i}", [128, D], BF16) for i in range(2)]
        Bhn2 = [T_(ps, f"Bhn{i}", [128, D], BF16) for i in range(2)]
        M12 = [T_(ps, f"M12_{i}", [128, C4 + C + 64], BF16) for i in range(NG)]
        XU = [[T_(ps, f"XU_{i}_{lv}", [128, C2 + 64], BF16) for lv in range(6)] for i in range(NG)]
        Uall = T_(ps, "Uall", [128, D], BF16)
        Ysb = T_(ps, "Ysb", [128, D])
        tmpH = T_(ps, "tmpH", [128, 4, 64])
        pr = P_(ps, "pr", [128, D])
        pSX = [P_(ps, f"pSX{i}", [128, 512]) for i in range(NG)]
        pY = P_(ps, "pY", [128, 512])
        pH = P_(ps, "pH", [128, 4, 128])
        B = {n: Buf(n) for n in ("lora H0 H1 H2 H3 Hq0 Hq1 Hq2 Hq3 slab wdt adt wtmp atmp wds ads sg eN eX aT "
                                 "tmp24_0 tmp24_1 tmp24_2 sh24_0 sh24_1 sh24_2 kkn sq bb kmod prod cc bon Uall Ysb tmpH "
                                 "eL0 eL1 QR0 QR1 KhT0 KhT1 BhT0 BhT1 Vb0 Vb1 Kh0 Kh1 Bhn0 Bhn1").split()}
        for n_ in ["pr0", "pr1", "pY", "pH"] + [f"pS{i}" for i in range(NG)]:
            B[n_] = Buf(n_, True)
        bM = [Buf(f"M12_{i}") for i in range(NG)]
        bXU = [[Buf(f"XU{i}{lv}") for lv in range(6)] for i in range(NG)]
        bPR = [B["pr0"], B["pr1"]]
        for d in range(2):
            ld(wup[:, d, :], wup_d[l, d, :, :], [B["lora"]], acc=True)
            ld(aup[:, d, :], aup_d[l, d, :, :], [B["lora"]], acc=True)
        for sd in range(4):
            S.op(pool, lambda: G.memset(H[:, sd, :, :], 0.0), (), [B[f"H{sd}"]])
            S.op(pool, lambda: G.memset(Hq[:, sd, :, :], 0.0), (), [B[f"Hq{sd}"]])
        S.op(pool, lambda: G.memset(wds[64:65, :], 1.0), (), [B["wds"]])
        S.op(pool, lambda: G.memset(ads[64:65, :], 1.0), (), [B["ads"]])

        hmu_bc = pder[:, 0:24].unsqueeze(2).to_broadcast([128, 24, C])
        omu_bc = pder[:, 24:48].unsqueeze(2).to_broadcast([128, 24, C])
        omka_bc = pder[:, 48:56].unsqueeze(2).to_broadcast([128, 8, C])
        kk_bc = pcol[:, PC_KK:PC_KK + 8].unsqueeze(2).to_broadcast([128, 8, C])
        ka_bc = pcol[:, PC_KA:PC_KA + 8].unsqueeze(2).to_broadcast([128, 8, C])
        rk_bc = pcol[:, PC_RK:PC_RK + 8].unsqueeze(2).to_broadcast([128, 8, C])

        class U_:
            pass

        def mk_unit(si, ch, d, up):
            u = U_()
            u.si, u.ch, u.d, u.up = si, ch, d, up
            u.s0, T = cfg.seqs[si]
            u.nch = T // C
            u.g0 = u.s0 + ch * C
            u.sd = si * 2 + d
            return u

        def prep(u):
            si, ch, d, up, g0, sd = u.si, u.ch, u.d, u.up, u.g0, u.sd
            eL, QR, KhT, BhT, Vb, Kh, Bhn = eL2[up], QR2[up], KhT2[up], BhT2[up], Vb2[up], Kh2[up], Bhn2[up]
            beL, bQR, bKhT, bBhT, bVb, bKh, bBhn = (B[f"{n}{up}"] for n in ("eL", "QR", "KhT", "BhT", "Vb", "Kh", "Bhn"))
            TRo = CO_TRF if d == 0 else CO_TRB
            TRinc = cst[:, TRo:TRo + C]
            TRexc = cst[:, TRo + C:TRo + C2]
            first, lastc = (ch == 0), (ch == u.nch - 1)
            lo = 1 if first else 0
            hi = C + 1 if lastc else C + 2
            if first:
                S.op(pool, lambda: G.memset(slab[:, :, 0:1], 0.0), (), [B["slab"]])
                S.op(pool, lambda: G.memset(wdt[:, 0:1], 0.0), (), [B["wdt"]])
                S.op(pool, lambda: G.memset(adt[:, 0:1], 0.0), (), [B["adt"]])
            if lastc:
                S.op(pool, lambda: G.memset(slab[:, :, C + 1:C + 2], 0.0), (), [B["slab"]], acc=not first)
                S.op(pool, lambda: G.memset(wdt[:, C + 1:C + 2], 0.0), (), [B["wdt"]], acc=not first)
                S.op(pool, lambda: G.memset(adt[:, C + 1:C + 2], 0.0), (), [B["adt"]], acc=not first)
            c0 = g0 - 1 + lo
            c1 = g0 - 1 + hi
            ld(wdt[:, lo:hi], pT[3072 + d * 64:3072 + (d + 1) * 64, c0:c1], [B["wdt"]], R=[b_pT], acc=(first or lastc))
            ld(adt[:, lo:hi], pT[3200 + d * 64:3200 + (d + 1) * 64, c0:c1], [B["adt"]], R=[b_pT], acc=(first or lastc))
            for jb in (2, 3, 0, 1, 4, 5):
                src = pT[jb * 512:(jb + 1) * 512, c0:c1].rearrange("(j p) t -> p j t", p=128)
                ld(slab[:, jb * 4:(jb + 1) * 4, lo:hi], src, [B["slab"]], R=[b_pT], acc=(jb != 2 or first or lastc))
            yield
            for (src_t, tmp_t, dst_t, bs, bt, bd, co) in ((wdt, wtmp, wds, "wdt", "wtmp", "wds", d),
                                                          (adt, atmp, ads, "adt", "atmp", "ads", 2 + d)):
                tt(dve, tmp_t[:], src_t[:, 0:C], src_t[:, 2:C + 2], ALU.add, [B[bs]], [B[bt]])
                ts(dve, tmp_t[:], tmp_t[:], pder[0:64, 56 + co:57 + co], ALU.mult, [B[bt], b_par], [B[bt]])
                S.op(dve, lambda: V.scalar_tensor_tensor(out=dst_t[0:64, :], in0=src_t[:, 1:C + 1],
                                                         scalar=pder[0:64, 60 + co:61 + co], in1=tmp_t[:],
                                                         op0=ALU.mult, op1=ALU.add),
                     [B[bs], B[bt], b_par], [B[bd]])
            actf(wds[0:64, :], wds[0:64, :], AF.Tanh, [B["wds"]], [B["wds"]])
            yield

            def shift_third(th, e1, e2):
                t8 = slice(th * 8, th * 8 + 8)
                bt_, bs_ = B[f"tmp24_{th}"], B[f"sh24_{th}"]
                hm = pder[:, th * 8:th * 8 + 8].unsqueeze(2).to_broadcast([128, 8, C])
                om = pder[:, 24 + th * 8:24 + th * 8 + 8].unsqueeze(2).to_broadcast([128, 8, C])
                tt(e1, tmp24[:, t8, :], slab[:, t8, 0:C], slab[:, t8, 2:C + 2], ALU.add, [B["slab"]], [bt_])
                tt(e1, tmp24[:, t8, :], tmp24[:, t8, :], hm, ALU.mult, [bt_, b_par], [bt_])
                tt(e2, sh24[:, t8, :], slab[:, t8, 1:C + 1], om, ALU.mult, [B["slab"], b_par], [bs_])
                tt(e2, sh24[:, t8, :], sh24[:, t8, :], tmp24[:, t8, :], ALU.add, [bs_, bt_], [bs_])
            rs, ks, vs = sh24[:, 0:8, :], sh24[:, 8:16, :], sh24[:, 16:24, :]
            pr3 = pr[:].rearrange("p (j c) -> p j c", j=8)
            shift_third(1, pool, dve)
            yield
            for hf in range(2):
                mm(pr[:, hf * 512:(hf + 1) * 512], wds[:, :], wup[:, d, hf * 512:(hf + 1) * 512], True, True,
                   [B["wds"], B["lora"]], [bPR[hf]])
            actf(sg[:], pr[:], AF.Sigmoid, bPR, [B["sg"]])
            yield
            tt(pool, kkn[:], ks, kk_bc, ALU.mult, [B["sh24_1"], b_par], [B["kkn"]])
            shift_third(0, dve, pool)
            yield
            for j in range(8):
                mm(pr[:, j * C:(j + 1) * C], aup[:, d, j * 128:(j + 1) * 128], ads[:, :], True, True,
                   [B["ads"], B["lora"]], [bPR[j // 4]])
            actf(aT[:], pr3, AF.Sigmoid, bPR, [B["aT"]])
            actf(sq[:], kkn[:], AF.Square, [B["kkn"]], [B["sq"]])
            yield
            for j in range(8):
                mm(pr[:, j * C:(j + 1) * C], sg[:, j * 128:(j + 1) * 128], TRinc, True, True,
                   [B["sg"], b_cst], [bPR[j // 4]])
            actf(eL[:], pr3, AF.Exp, bPR, [beL])
            actf(eN[:], pr3, AF.Exp, bPR, [B["eN"]], scale=-1.0)
            yield
            sq2 = sq[:].rearrange("p j c -> p (j c)")
            for hf in range(2):
                mm(pr[:, hf * 512:(hf + 1) * 512], BO, sq2[:, hf * 512:(hf + 1) * 512], True, True,
                   [B["sq"], b_cst], [bPR[hf]])
            actf(sq[:], pr3, AF.Ln, bPR + [B["sq"]], [B["sq"]], bias=1e-24)
            actf(sq[:], sq[:], AF.Exp, [B["sq"]], [B["sq"]], scale=-0.5)
            tt(dve, kmod[:], aT[:], ka_bc, ALU.mult, [B["aT"], b_par], [B["kmod"]])
            tt(dve, kmod[:], kmod[:], omka_bc, ALU.add, [B["kmod"], b_par], [B["kmod"]])
            yield
            for j in range(8):
                mm(pr[:, j * C:(j + 1) * C], sg[:, j * 128:(j + 1) * 128], TRexc, True, True,
                   [B["sg"], b_cst], [bPR[j // 4]])
            actf(eX[:], pr3, AF.Exp, bPR, [B["eX"]])
            shift_third(2, pool, dve)
            yield
            tt(pool, kmod[:], kmod[:], ks, ALU.mult, [B["kmod"], B["sh24_1"]], [B["kmod"]])
            tt(pool, kkn[:], kkn[:], sq[:], ALU.mult, [B["kkn"], B["sq"]], [B["kkn"]])
            yield
            tt(pool, bb[:], kkn[:], aT[:], ALU.mult, [B["kkn"], B["aT"]], [B["bb"]])
            tt(dve, prod[:], rs, kmod[:], ALU.mult, [B["sh24_0"], B["kmod"]], [B["prod"]])
            tt(dve, QR[:, :, 0, :], kkn[:], eX[:], ALU.mult, [B["kkn"], B["eX"]], [bQR])
            yield
            tt(pool, prod[:], prod[:], rk_bc, ALU.mult, [B["prod"], b_par], [B["prod"]])
            tt(dve, QR[:, :, 1, :], rs, eL[:], ALU.mult, [B["sh24_0"], beL], [bQR], acc=True)
            S.op(act, lambda: A.copy(out=QR[:, :, 2, :], in_=QR[:, :, 0, :]), [bQR], [bQR], acc=True)
            yield
            tt(pool, KhT[:], kmod[:], eN[:], ALU.mult, [B["kmod"], B["eN"]], [bKhT])
            for j in range(8):
                mm(pr[:, 2 * j:2 * j + 2], prod[:, j, :], HS, True, True, [B["prod"], b_cst], [bPR[0]])
            cp(act, cc[:], pr[:, 0:16], [bPR[0]], [B["cc"]])
            yield
            tt(pool, BhT[:], bb[:], eN[:], ALU.mult, [B["bb"], B["eN"]], [bBhT])
            for j in range(8):
                trp(pr[:, j * 128:(j + 1) * 128], vs[:, j, :], identf, [B["sh24_2"], b_cst], [bPR[j // 4]])
            cp(act, Vb[:], pr[:], bPR, [bVb])
            tt(dve, bon[:].rearrange("p (h v) -> p h v", h=16), pr[:].rearrange("p (h v) -> p h v", h=16),
               cc[:].unsqueeze(2).to_broadcast([128, 16, 64]), ALU.mult, bPR + [B["cc"]], [B["bon"]])
            st(ysc[2 + d, g0:g0 + C, :], bon[:], [B["bon"]], [b_ysc])
            yield
            for j in range(8):
                mm(pr[:, j * 128:(j + 1) * 128], KhT[:, j, :], identb[:], True, True, [bKhT, b_cst], [bPR[j // 4]])
            cp(act, Kh[:], pr[:], bPR, [bKh])
            yield
            for j in range(8):
                mm(pr[:, j * 128:(j + 1) * 128], BhT[:, j, :], identb[:], True, True, [bBhT, b_cst], [bPR[j // 4]])
            S.op(act, lambda: A.mul(out=Bhn[:], in_=pr[:], mul=-1.0), bPR, [bBhn])

        def heads(u):
            if cfg.p2stage < 50:
                return
            si, ch, d, up, g0, sd = u.si, u.ch, u.d, u.up, u.g0, u.sd
            eL, QR, KhT, BhT, Vb, Kh, Bhn = eL2[up], QR2[up], KhT2[up], BhT2[up], Vb2[up], Kh2[up], Bhn2[up]
            beL, bQR, bKhT, bBhT, bVb, bKh, bBhn = (B[f"{n}{up}"] for n in ("eL", "QR", "KhT", "BhT", "Vb", "Kh", "Bhn"))
            bH, bHq = B[f"H{sd}"], B[f"Hq{sd}"]
            MT = cst[:, (CO_MTF if d == 0 else CO_MTB):(CO_MTF if d == 0 else CO_MTB) + C4]
            M3 = cst[:, (CO_M3F if d == 0 else CO_M3B):(CO_M3F if d == 0 else CO_M3B) + C]
            last = C - 1 if d == 0 else 0

            def head(h, gi):
                j, par = h // 2, h % 2
                sl = slice(par * 64, par * 64 + 64)
                hs = slice(h * 64, (h + 1) * 64)
                bS = B[f"pS{gi}"]
                pb = pSX[gi]
                Mt = M12[gi]
                qr01 = QR[sl, j, 0:2, :].rearrange("p a c -> p (a c)")
                qr12 = QR[sl, j, 1:3, :].rearrange("p a c -> p (a c)")
                mm(pb[:, 0:C2], KhT[sl, j, :], qr01, True, True, [bKhT, bQR], [bS])
                mm(pb[:, C2:C4], BhT[sl, j, :], qr12, True, True, [bBhT, bQR], [bS])
                tt(dve, Mt[:, 0:C4], pb[:, :], MT, ALU.mult, [bS, b_cst], [bM[gi]])
                yield
                mm(pb[:, 0:C], QR[sl, j, 0, :], BhT[sl, j, :], True, True, [bQR, bBhT], [bS])
                mm(pb[:, 256:320], QR[sl, j, 0, :], Hq[sl, sd, j, :], True, False, [bQR, bHq], [bS])
                mm(pb[:, 256:320], Mt[:, 0:C], Vb[:, hs], False, True, [bM[gi], bVb], [bS])
                tt(dve, Mt[:, C4:C4 + C], pb[:, 0:C], M3, ALU.mult, [bS, b_cst], [bM[gi]], acc=True)
                cp(act, Mt[:, C4 + C:C4 + C + 64], pb[:, 256:320], [bS], [bM[gi]], acc=True)
                yield
                Tt, o, bT = Mt, 3 * C, bM[gi]
                for lv in range(7):
                    XT_, X_, U_ = Tt[:, o:o + C], Tt[:, o + C:o + C2], Tt[:, o + C2:o + C2 + 64]
                    if lv < 5:
                        mm(pb[:, C:C2 + 64], XT_, Tt[:, o + C:o + C2 + 64], True, True, [bT], [bS])
                        mm(pb[:, 0:C], X_, XT_, True, True, [bT], [bS])
                    elif lv == 5:
                        mm(pb[:, C2:C2 + 64], XT_, U_, True, True, [bT], [bS])
                        mm(pb[:, 0:C], X_, XT_, True, True, [bT], [bS])
                    else:
                        mm(pb[:, C2:C2 + 64], XT_, U_, True, True, [bT], [bS])
                    if lv < 6:
                        nt, bn = XU[gi][lv], bXU[gi][lv]
                        ev = act
                        cp(ev, nt[:, 0:(C2 if lv < 5 else C)], pb[:, 0:(C2 if lv < 5 else C)], [bS], [bn])
                        tt(dve, nt[:, C2:C2 + 64], pb[:, C2:C2 + 64], U_, ALU.add, [bS, bT], [bn], acc=True)
                        Tt, o, bT = nt, 0, bn
                    else:
                        tt(dve, Uall[:, hs], pb[:, C2:C2 + 64], U_, ALU.add, [bS, bT], [B["Uall"]], acc=True)
                    yield
                yo = (h % 8) * 64
                mm(pY[:, yo:yo + 64], QR[sl, j, 1, :], Hq[sl, sd, j, :], True, False, [bQR, bHq], [B["pY"]])
                mm(pY[:, yo:yo + 64], Mt[:, C:C2], Vb[:, hs], False, False, [bM[gi], bVb], [B["pY"]])
                mm(pY[:, yo:yo + 64], Mt[:, C2:3 * C], Uall[:, hs], False, True, [bM[gi], B["Uall"]], [B["pY"]])

            for q in range(16 // NG):
                gens = [head(q * NG + gi, gi) for gi in range(NG)]
                while gens:
                    for gen in list(gens):
                        try:
                            next(gen)
                        except StopIteration:
                            gens.remove(gen)
                    yield
                for j in range(q * NG // 2, (q + 1) * NG // 2):
                    if j % 4 == 0 and j > 0 or False:
                        pass
                    jj = j % 4
                    mm(pH[:, jj, :], Kh[:, j * 128:(j + 1) * 128], Vb[:, j * 128:(j + 1) * 128], True, False,
                       [bKh, bVb], [B["pH"]])
                    mm(pH[:, jj, :], Bhn[:, j * 128:(j + 1) * 128], Uall[:, j * 128:(j + 1) * 128], False, True,
                       [bBhn, B["Uall"]], [B["pH"]])
                    if jj == 3:
                        hb = j // 4
                        cp(act, Ysb[:, hb * 512:(hb + 1) * 512], pY[:], [B["pY"]], [B["Ysb"]], acc=(hb == 1))
                        j0 = j - 3
                        for hp in range(2):
                            psl = slice(hp * 64, hp * 64 + 64)
                            tt(dve, tmpH[psl, :, :], H[psl, sd, j0:j0 + 4, :], pH[psl, :, hp * 64:hp * 64 + 64],
                               ALU.add, [bH, B["pH"]], [B["tmpH"]], acc=(hp == 1))
                        for hp in range(2):
                            psl = slice(hp * 64, hp * 64 + 64)
                            tt(pool, H[psl, sd, j0:j0 + 4, :], tmpH[psl, :, :],
                               eL[psl, j0:j0 + 4, last:last + 1].to_broadcast([64, 4, 64]), ALU.mult,
                               [B["tmpH"], beL], [bH], acc=True)
                yield
            cp(act, Hq[:, sd, :, :], H[:, sd, :, :], [bH], [bHq])
            st(ysc[d, g0:g0 + C, :], Ysb[:], [B["Ysb"]], [b_ysc])

        nchs = [T // C for (_, T) in cfg.seqs]
        units = []
        for i in range(max(nchs)):
            for si in range(len(cfg.seqs)):
                if i < nchs[si]:
                    units.append((si, i, 0))
                    units.append((si, nchs[si] - 1 - i, 1))

        def drain(gen):
            for _ in gen:
                pass

        prev = None
        for ui, (si, ch, d) in enumerate(units):
            u = mk_unit(si, ch, d, ui % 2)
            pg = prep(u)
            if prev is None:
                drain(pg)
            else:
                hg = heads(prev)
                r = 0
                pg_alive = True
                for _ in hg:
                    r += 1
                    if pg_alive and r % cfg.p2_every == 0:
                        try:
                            next(pg)
                        except StopIteration:
                            pg_alive = False
                if pg_alive:
                    drain(pg)
            prev = u
        drain(heads(prev))
        S.barrier()


def phase3(nc, cfg, S, l, E):
    g = lambda k: E[k]
    T_, P_, ld, st, tt, ts, actf, cp, mm, trp = (g(k) for k in
                                                   "T_ P_ ld st tt ts actf cp mm trp".split())
    pe, act, dve, pool = S.pe, S.act, S.dve, S.pool
    pcol, pder, b_par, identb, b_cst, cst = (g(k) for k in "pcol pder b_par identb b_cst cst".split())
    brow_d = g("brow_d")
    pT, b_pT, ysc, b_ysc, qkv, b_qkv = (g(k) for k in "pT b_pT ysc b_ysc qkv b_qkv".split())
    wpa_d, wpb_d, wo_d, rope_d = g("wpa_d"), g("wpb_d"), g("wo_d"), g("rope_d")
    x_src, x_dst, bx_src, bx_dst = g("x_src"), g("x_dst"), g("bx_src"), g("bx_dst")
    V, G, A, PE = nc.vector, nc.gpsimd, nc.scalar, nc.tensor
    identf = cst[:, CO_ID:CO_ID + 128]
    MPREV = cst[:, CO_MPREV:CO_MPREV + 128]
    MNEXT = cst[:, CO_MNEXT:CO_MNEXT + 128]
    with ExitStack() as ps:
        S.barrier()
        brow = T_(ps, "brow", [128, NBR])
        bder = T_(ps, "bder", [128, 64 + 16])
        ld(brow[:], brow_d[l, :].partition_broadcast(128), [b_par], acc=True)
        ts(dve, bder[:, 0:64], brow[:, BR_QG:BR_QG + 64], 0.125, ALU.mult, [b_par], [b_par], acc=True)
        actf(bder[:, 64:80], brow[:, BR_SINK:BR_SINK + 16], AF.Exp, [b_par], [b_par], acc=True)
        wts = [T_(ps, nm, [128, 8, D], BF16) for nm in ("wpa", "wpb", "wo")]
        wst = [T_(ps, f"wst3_{i}", [128, D]) for i in range(2)]
        yf, yb, bf_, bb_ = (T_(ps, nm, [128, D]) for nm in ("yf", "yb", "bf", "bb"))
        st16 = T_(ps, "st16", [128, 6, 16])
        gaT = T_(ps, "gaT", [128, 8, 128])
        yagT = T_(ps, "yagT", [128, 8, 128], BF16)
        qt = T_(ps, "qt", [128, D])
        tmpq = T_(ps, "tmpq", [128, D])
        qst = T_(ps, "qst", [128, 2, 16])
        qr = T_(ps, "qr", [128, D], BF16)
        qT = T_(ps, "qT", [64, 16, 128], BF16)
        kvraw = T_(ps, "kvraw", [128, 512])
        tmpk = T_(ps, "tmpk", [128, 256])
        kst = T_(ps, "kst", [128, 2, 4])
        kr = T_(ps, "kr", [128, 256], BF16)
        kT = [T_(ps, f"kT{i}", [64, 4, 128], BF16) for i in range(3)]
        vaug = [T_(ps, f"vaug{i}", [128, 4, 65], BF16) for i in range(3)]
        csq = T_(ps, "csq", [128, 64])
        csk = T_(ps, "csk", [128, 64])
        et = [T_(ps, f"et{i}", [128, 4, 128], BF16) for i in range(3)]
        den = T_(ps, "den", [128, 2, 4])
        og = T_(ps, "og", [128, D])
        gbT = T_(ps, "gbT", [128, 8, 128])
        ogT = T_(ps, "ogT", [128, 8, 128], BF16)
        mgT = T_(ps, "mgT", [128, 16, 128])
        t1 = T_(ps, "t1", [128, 8, 128])
        t2 = T_(ps, "t2", [128, 8, 128])
        mixT = T_(ps, "mixT", [128, 8, 128], BF16)
        xt = T_(ps, "xt3", [128, D])
        xo = T_(ps, "xo", [128, D])
        R0 = P_(ps, "R0", [128, D])
        R1 = P_(ps, "R1", [128, D])
        R2 = P_(ps, "R2", [128, D])
        R3 = P_(ps, "R3", [128, 512])
        R4 = P_(ps, "R4", [128, 512])
        names = ("w wst0 wst1 yf yb bf bb st16 gaT yagT qt tmpq qst qr qT kvraw tmpk kst kr kT0 kT1 kT2 "
                 "vaug0 vaug1 vaug2 csq csk et0 et1 et2 den og gbT ogT mgT t1 t2 mixT xt xo").split()
        B = {n: Buf(n) for n in names}
        for n_ in ("R0a", "R0b", "R1a", "R1b", "R2a", "R2b", "R3", "R4"):
            B[n_] = Buf(n_, True)
        bR0, bR1, bR2 = [B["R0a"], B["R0b"]], [B["R1a"], B["R1b"]], [B["R2a"], B["R2b"]]
        k = 0
        cast_engs = (act, pool, dve)
        for wi, wd_ in enumerate((wpa_d, wpb_d, wo_d)):
            for kc in range(8):
                i = k % 2
                ld(wst[i][:], wd_[l, kc * 128:(kc + 1) * 128, :], [B[f"wst{i}"]])
                cp(cast_engs[k % 3], wts[wi][:, kc, :], wst[i][:], [B[f"wst{i}"]], [B["w"]], acc=True)
                k += 1
        for i in range(3):
            S.op(pool, lambda: G.memset(vaug[i][:, :, 64:65], 1.0), (), [B[f"vaug{i}"]])
        wpa, wpb, wo = wts
        lng_bc = brow[:, BR_LNG:BR_LNG + D]
        lnb_bc = brow[:, BR_LNB:BR_LNB + D]
        qg8 = bder[:, 0:64]
        kg = brow[:, BR_KG:BR_KG + 64]
        esink = bder[:, 64:80]

        def v3(ap, h):
            return ap.rearrange("p (h v) -> p h v", h=h)

        def rope_norm(src3, nh, sq_t, st_t, cs_t, gvec, dst_bf, bsrc, bsq, bst, bcs, bdst):
            sq3 = v3(sq_t, nh)
            actf(sq_t, src3.rearrange("p h v -> p (h v)"), AF.Square, [bsrc], [bsq])
            S.op(dve, lambda: V.tensor_reduce(out=st_t[:, 0, :], in_=sq3, axis=AX.X, op=ALU.add), [bsq], [bst])
            actf(st_t[:, 1, :], st_t[:, 0, :], AF.Sqrt, [bst], [bst], bias=NORM_EPS, scale=1.0 / 64)
            S.op(dve, lambda: V.reciprocal(out=st_t[:, 1, :], in_=st_t[:, 1, :]), [bst], [bst])
            tt(dve, src3, src3, st_t[:, 1, :].unsqueeze(2).to_broadcast([128, nh, 64]), ALU.mult, [bsrc, bst], [bsrc])
            tt(dve, src3, src3, gvec.unsqueeze(1).to_broadcast([128, nh, 64]), ALU.mult, [bsrc, b_par], [bsrc])
            cos_bc = cs_t[:, 0:32].unsqueeze(1).to_broadcast([128, nh, 32])
            sin_bc = cs_t[:, 32:64].unsqueeze(1).to_broadcast([128, nh, 32])
            x1, x2 = src3[:, :, 0:32], src3[:, :, 32:64]
            s1_, s2_ = sq3[:, :, 0:32], sq3[:, :, 32:64]
            tt(dve, s1_, x1, cos_bc, ALU.mult, [bsrc, bcs], [bsq])
            tt(dve, s2_, x2, sin_bc, ALU.mult, [bsrc, bcs], [bsq])
            tt(dve, dst_bf[:, :, 0:32], s1_, s2_, ALU.subtract, [bsq], [bdst])
            tt(dve, s1_, x2, cos_bc, ALU.mult, [bsrc, bcs], [bsq])
            tt(dve, s2_, x1, sin_bc, ALU.mult, [bsrc, bcs], [bsq])
            tt(dve, dst_bf[:, :, 32:64], s1_, s2_, ALU.add, [bsq], [bdst])

        def proc_kv(s0, m):
            slot = m % 3
            gm = s0 + m * 128
            ld(kvraw[:], qkv[gm:gm + 128, 1024:1536], [B["kvraw"]], R=[b_qkv])
            ld(csk[:], rope_d[m * 128:(m + 1) * 128, :], [B["csk"]])
            cp(act, vaug[slot][:, :, 0:64], v3(kvraw[:, 256:512], 4), [B["kvraw"]], [B[f"vaug{slot}"]])
            rope_norm(v3(kvraw[:, 0:256], 4), 4, tmpk[:], kst, csk, kg, v3(kr[:], 4),
                      B["kvraw"], B["tmpk"], B["kst"], B["csk"], B["kr"])
            for gk in range(4):
                mm(R3[0:64, gk * 128:(gk + 1) * 128], kr[:, gk * 64:(gk + 1) * 64], identb[:], True, True,
                   [B["kr"], b_cst], [B["R3"]])
            cp(dve, kT[slot][:], R3[0:64, :].rearrange("p (g t) -> p g t", g=4), [B["R3"]], [B[f"kT{slot}"]])

        def tile3(si, n):
            s0, T = cfg.seqs[si]
            NB = T // 128
            g0 = s0 + n * 128
            ld(yf[:], ysc[0, g0:g0 + 128, :], [B["yf"]], R=[b_ysc])
            ld(yb[:], ysc[1, g0:g0 + 128, :], [B["yb"]], R=[b_ysc])
            ld(bf_[:], ysc[2, g0:g0 + 128, :], [B["bf"]], R=[b_ysc])
            ld(bb_[:], ysc[3, g0:g0 + 128, :], [B["bb"]], R=[b_ysc])
            ld(gaT[:], pT[3328:4352, g0:g0 + 128].rearrange("(j p) t -> p j t", p=128), [B["gaT"]], R=[b_pT])
            ld(qt[:], qkv[g0:g0 + 128, 0:1024], [B["qt"]], R=[b_qkv])
            ld(csq[:], rope_d[n * 128:(n + 1) * 128, :], [B["csq"]])
            ld(gbT[:], pT[5888:6912, g0:g0 + 128].rearrange("(j p) t -> p j t", p=128), [B["gbT"]], R=[b_pT])
            ld(mgT[:, 0:8, :], pT[6912:7936, g0:g0 + 128].rearrange("(j p) t -> p j t", p=128), [B["mgT"]], R=[b_pT])
            ld(mgT[:, 8:16, :], pT[7936:8960, g0:g0 + 128].rearrange("(j p) t -> p j t", p=128), [B["mgT"]], R=[b_pT],
               acc=True)
            ld(xt[:], x_src(l, g0, 128), [B["xt"]], R=[bx_src(l)])
            def genA():
                tt(pool, yf[:], yf[:], yb[:], ALU.add, [B["yf"], B["yb"]], [B["yf"]])
                tt(pool, bf_[:], bf_[:], bb_[:], ALU.add, [B["bf"], B["bb"]], [B["bf"]])
                tt(pool, bf_[:], bf_[:], lnb_bc, ALU.add, [B["bf"], b_par], [B["bf"]])
                yield
                S.op(dve, lambda: V.tensor_reduce(out=st16[:, 0, :], in_=v3(yf[:], 16), axis=AX.X, op=ALU.add),
                     [B["yf"]], [B["st16"]])
                actf(yb[:], yf[:], AF.Square, [B["yf"]], [B["yb"]])
                S.op(dve, lambda: V.tensor_reduce(out=st16[:, 1, :], in_=v3(yb[:], 16), axis=AX.X, op=ALU.add),
                     [B["yb"]], [B["st16"]])
                ts(dve, st16[:, 2, :], st16[:, 0, :], 1.0 / 64, ALU.mult, [B["st16"]], [B["st16"]])
                tt(dve, st16[:, 3, :], st16[:, 2, :], st16[:, 2, :], ALU.mult, [B["st16"]], [B["st16"]])
                S.op(dve, lambda: V.scalar_tensor_tensor(out=st16[:, 4, :], in0=st16[:, 1, :], scalar=1.0 / 64,
                                                         in1=st16[:, 3, :], op0=ALU.mult, op1=ALU.subtract),
                     [B["st16"]], [B["st16"]])
                actf(st16[:, 4, :], st16[:, 4, :], AF.Sqrt, [B["st16"]], [B["st16"]], bias=GN_EPS)
                S.op(dve, lambda: V.reciprocal(out=st16[:, 5, :], in_=st16[:, 4, :]), [B["st16"]], [B["st16"]])
                yield
                tt(pool, v3(yb[:], 16), v3(yf[:], 16), st16[:, 2, :].unsqueeze(2).to_broadcast([128, 16, 64]),
                   ALU.subtract, [B["yf"], B["st16"]], [B["yb"]])
                tt(pool, v3(yb[:], 16), v3(yb[:], 16), st16[:, 5, :].unsqueeze(2).to_broadcast([128, 16, 64]),
                   ALU.mult, [B["yb"], B["st16"]], [B["yb"]])
                tt(pool, yb[:], yb[:], lng_bc, ALU.mult, [B["yb"], b_par], [B["yb"]])
                tt(pool, yb[:], yb[:], bf_[:], ALU.add, [B["yb"], B["bf"]], [B["yb"]])
                yield
                for kc in range(8):
                    trp(R0[:, kc * 128:(kc + 1) * 128], yb[:, kc * 128:(kc + 1) * 128], identf, [B["yb"], b_cst],
                        [bR0[kc // 4]])
                actf(gaT[:], gaT[:], AF.Silu, [B["gaT"]], [B["gaT"]])
                tt(dve, yagT[:], R0[:].rearrange("p (j t) -> p j t", j=8), gaT[:], ALU.mult, bR0 + [B["gaT"]], [B["yagT"]])
                yield
                for db in range(8):
                    for kc in range(8):
                        mm(R1[:, db * 128:(db + 1) * 128], wpa[:, kc, db * 128:(db + 1) * 128], yagT[:, kc, :],
                           kc == 0, kc == 7, [B["w"], B["yagT"]], [bR1[db // 4]])
                    if db % 2 == 1:
                        yield

            def genB():
                if n == 0:
                    proc_kv(s0, 0)
                    yield
                if n + 1 < NB:
                    proc_kv(s0, n + 1)
                    yield
                rope_norm(v3(qt[:], 16), 16, tmpq[:], qst, csq, qg8, v3(qr[:], 16),
                          B["qt"], B["tmpq"], B["qst"], B["csq"], B["qr"])
                for rnd in range(2):
                    for hh in range(8):
                        h = rnd * 8 + hh
                        mm(R2[0:64, hh * 128:(hh + 1) * 128], qr[:, h * 64:(h + 1) * 64], identb[:], True, True,
                           [B["qr"], b_cst], [bR2[hh // 4]])
                    cp(act, qT[:, rnd * 8:(rnd + 1) * 8, :], R2[0:64, :].rearrange("p (h t) -> p h t", h=8), bR2,
                       [B["qT"]], acc=(rnd == 1))
                    yield
                cbs = [m for m in (n - 1, n, n + 1) if 0 <= m < NB]
                for gq in range(4):
                    for ci, m in enumerate(cbs):
                        slot = m % 3
                        mm(R3[:, :], kT[slot][:, gq, :], qT[:, 4 * gq:4 * gq + 4, :].rearrange("p h t -> p (h t)"),
                           True, True, [B[f"kT{slot}"], B["qT"]], [B["R3"]])
                        actf(et[ci][:].rearrange("p h t -> p (h t)"), R3[:, :], AF.Exp, [B["R3"]], [B[f"et{ci}"]])
                        if m != n:
                            msk = MPREV if m < n else MNEXT
                            tt(pool, et[ci][:], et[ci][:], msk.unsqueeze(1).to_broadcast([128, 4, 128]), ALU.mult,
                               [B[f"et{ci}"], b_cst], [B[f"et{ci}"]])
                    yield
                    for i4 in range(4):
                        for ci, m in enumerate(cbs):
                            slot = m % 3
                            mm(R4[:, i4 * 65:(i4 + 1) * 65], et[ci][:, i4, :], vaug[slot][:, gq, :],
                               ci == 0, ci == len(cbs) - 1, [B[f"et{ci}"], B[f"vaug{slot}"]], [B["R4"]])
                    r4v = R4[:, 0:260].rearrange("p (h v) -> p h v", h=4)
                    tt(dve, den[:, 0, :], r4v[:, :, 64], esink[:, 4 * gq:4 * gq + 4], ALU.add, [B["R4"], b_par], [B["den"]])
                    S.op(dve, lambda: V.reciprocal(out=den[:, 1, :], in_=den[:, 0, :]), [B["den"]], [B["den"]])
                    tt(dve, v3(og[:], 16)[:, 4 * gq:4 * gq + 4, :], r4v[:, :, 0:64],
                       den[:, 1, :].unsqueeze(2).to_broadcast([128, 4, 64]), ALU.mult, [B["R4"], B["den"]], [B["og"]],
                       acc=(gq > 0))
                    yield

            gens = [genA(), genB()]
            while gens:
                for gen_ in list(gens):
                    try:
                        next(gen_)
                    except StopIteration:
                        gens.remove(gen_)
            for kc in range(8):
                trp(R0[:, kc * 128:(kc + 1) * 128], og[:, kc * 128:(kc + 1) * 128], identf, [B["og"], b_cst],
                    [bR0[kc // 4]])
            actf(gbT[:], gbT[:], AF.Silu, [B["gbT"]], [B["gbT"]])
            tt(dve, ogT[:], R0[:].rearrange("p (j t) -> p j t", j=8), gbT[:], ALU.mult, bR0 + [B["gbT"]], [B["ogT"]])
            for db in range(8):
                for kc in range(8):
                    mm(R2[:, db * 128:(db + 1) * 128], wpb[:, kc, db * 128:(db + 1) * 128], ogT[:, kc, :],
                       kc == 0, kc == 7, [B["w"], B["ogT"]], [bR2[db // 4]])
            actf(mgT[:], mgT[:], AF.Sigmoid, [B["mgT"]], [B["mgT"]])
            tt(dve, t1[:], R1[:].rearrange("p (j t) -> p j t", j=8), mgT[:, 0:8, :], ALU.mult, bR1 + [B["mgT"]], [B["t1"]])
            tt(dve, t2[:], R2[:].rearrange("p (j t) -> p j t", j=8), mgT[:, 8:16, :], ALU.mult, bR2 + [B["mgT"]], [B["t2"]])
            tt(dve, mixT[:], t1[:], t2[:], ALU.add, [B["t1"], B["t2"]], [B["mixT"]])
            for hf, (Rb, bRb) in enumerate(((R3, B["R3"]), (R4, B["R4"]))):
                for kc in range(8):
                    mm(Rb[:, :], mixT[:, kc, :], wo[:, kc, hf * 512:(hf + 1) * 512], kc == 0, kc == 7,
                       [B["mixT"], B["w"]], [bRb])
                tt(dve, xo[:, hf * 512:(hf + 1) * 512], Rb[:, :], xt[:, hf * 512:(hf + 1) * 512], ALU.add,
                   [bRb, B["xt"]], [B["xo"]], acc=(hf == 1))
            st(x_dst(l, g0, 128), xo[:], [B["xo"]], [bx_dst(l)])

        S.barrier()
        for si in range(len(cfg.seqs)):
            for n in range(cfg.seqs[si][1] // 128):
                tile3(si, n)
        S.barrier()


def host_params(inp, L):
    f = lambda k: np.asarray(inp[k], np.float32)
    pcol = np.zeros((L, 128, NPC), np.float32)
    mu = f("shift_mu")
    for l in range(L):
        pcol[l, :, PC_NG:PC_NG + 8] = f("norm_g")[l].reshape(8, 128).T
        pcol[l, :, PC_MU:PC_MU + 24] = mu[l, :3072].reshape(24, 128).T
        pcol[l, :64, PC_MUWD] = mu[l, 3072:3136]
        pcol[l, :64, PC_MUWD + 1] = mu[l, 3136:3200]
        pcol[l, :64, PC_MUAD] = mu[l, 3200:3264]
        pcol[l, :64, PC_MUAD + 1] = mu[l, 3264:3328]
        pcol[l, :, PC_KK:PC_KK + 8] = f("k_k")[l].reshape(8, 128).T
        pcol[l, :, PC_KA:PC_KA + 8] = f("k_a")[l].reshape(8, 128).T
        pcol[l, :, PC_RK:PC_RK + 8] = f("r_k")[l].reshape(8, 128).T
    brow = np.concatenate([f("ln_x_g"), f("ln_x_b"), f("q_norm_g"), f("k_norm_g"), f("sink")], axis=1)
    wup = np.concatenate([f("w_lora_up"), f("w0")[:, :, None, :]], axis=2)
    aup = np.concatenate([f("a_lora_up"), f("a0")[:, :, None, :]], axis=2)
    return dict(pcol=pcol, brow=np.ascontiguousarray(brow), wup_aug=np.ascontiguousarray(wup),
                aup_aug=np.ascontiguousarray(aup))


def make_in_maps(inp, cfg, n_cores=8):
    L = cfg.depth
    hp = host_params(inp, L)
    consts = make_consts()
    rope = make_rope(max(cfg.TP, cfg.TS))
    shared = dict(w_in=np.ascontiguousarray(inp["w_in"][:L], dtype=np.float32),
                  w_proj_a=np.ascontiguousarray(inp["w_proj_a"][:L], dtype=np.float32),
                  w_proj_b=np.ascontiguousarray(inp["w_proj_b"][:L], dtype=np.float32),
                  w_out=np.ascontiguousarray(inp["w_out"][:L], dtype=np.float32),
                  consts=consts, rope=rope, **hp)
    xp, xs = np.asarray(inp["x_prompt"]), np.asarray(inp["x_sample"])
    maps = []
    for c in range(n_cores):
        m = dict(shared)
        m["xp"] = np.ascontiguousarray(xp[c % xp.shape[0]], dtype=np.float32)
        m["xs"] = np.ascontiguousarray(xs[c % xs.shape[0]], dtype=np.float32)
        maps.append(m)
    return maps


def kernel(**inputs):
    xp, xs = inputs["x_prompt"], inputs["x_sample"]
    cfg = Cfg(xp.shape[1], xs.shape[1], inputs["w_in"].shape[0])
    nc = build(cfg)
    maps = make_in_maps(inputs, cfg)
    res = run_bass_kernel_spmd(nc, maps, core_ids=list(range(8)))
    y_p = np.stack([res.results[c]["yp"] for c in range(xp.shape[0])], axis=0).astype(np.float32)
    y_s = np.stack([res.results[c]["ys"] for c in range(xs.shape[0])], axis=0).astype(np.float32)
    return (y_p, y_s)
```

```python
import numpy as np
from contextlib import ExitStack
import concourse.bass as bass
import concourse.mybir as mybir
from concourse.bass_utils import run_bass_kernel_spmd

F32 = mybir.dt.float32
BF16 = mybir.dt.bfloat16
AF = mybir.ActivationFunctionType
ALU = mybir.AluOpType
AX = mybir.AxisListType

D = 1024
NIN = 8960
C = 128
NORM_EPS = 1e-6
GN_EPS = 64e-5
NEG_E = -float(np.exp(-0.5))

PC_NG = 0
PC_MU = 8
PC_MUWD = 32
PC_MUAD = 34
PC_KK = 36
PC_KA = 44
PC_RK = 52
NPC = 60
BR_LNG = 0
BR_LNB = 1024
BR_QG = 2048
BR_KG = 2112
BR_SINK = 2176
NBR = 2192
CO_ID = 0
CO_MTF = 128
CO_MTB = CO_MTF + 512
CO_M3F = CO_MTB + 512
CO_M3B = CO_M3F + 128
CO_TRF = CO_M3B + 128
CO_TRB = CO_TRF + 256
CO_BO = CO_TRB + 256
CO_HS = CO_BO + 128
CO_MPREV = CO_HS + 2
CO_MNEXT = CO_MPREV + 128
NCO = CO_MNEXT + 128


def make_consts():
    c = np.zeros((128, NCO), np.float32)
    i = np.arange(128)
    s, t = i[:, None], i[None, :]
    c[:, CO_ID:CO_ID + 128] = (s == t)
    for (o, lt, le) in ((CO_MTF, s < t, s <= t), (CO_MTB, s > t, s >= t)):
        c[:, o:o + 128] = lt
        c[:, o + 128:o + 256] = le
        c[:, o + 256:o + 384] = -1.0 * le
        c[:, o + 384:o + 512] = -1.0 * lt
    c[:, CO_M3F:CO_M3F + 128] = -1.0 * (t < s)
    c[:, CO_M3B:CO_M3B + 128] = -1.0 * (t > s)
    c[:, CO_TRF:CO_TRF + 128] = NEG_E * (s <= t)
    c[:, CO_TRF + 128:CO_TRF + 256] = NEG_E * (s < t)
    c[:, CO_TRB:CO_TRB + 128] = NEG_E * (s >= t)
    c[:, CO_TRB + 128:CO_TRB + 256] = NEG_E * (s > t)
    c[:, CO_BO:CO_BO + 128] = (s // 64 == t // 64)
    c[:, CO_HS] = (i // 64 == 0)
    c[:, CO_HS + 1] = (i // 64 == 1)
    c[:, CO_MPREV:CO_MPREV + 128] = (s >= t)
    c[:, CO_MNEXT:CO_MNEXT + 128] = (s <= t)
    return c


def make_rope(tmax):
    inv = 1.0 / (10000.0 ** (np.arange(0, 64, 2, dtype=np.float32) / 64))
    ang = np.arange(tmax, dtype=np.float32)[:, None] * inv[None, :].astype(np.float32)
    return np.concatenate([np.cos(ang), np.sin(ang)], axis=1).astype(np.float32)


class Buf:
    __slots__ = ("name", "writers", "readers", "base", "excl")

    def __init__(self, name, excl=False):
        self.name = name
        self.excl = excl
        self.writers = {}
        self.readers = {}
        self.base = {}


class Eng:
    def __init__(self, name, h, sem):
        self.name = name
        self.h = h
        self.sem = sem
        self.count = 0
        self.waited = {}


class DmaQ:
    def __init__(self, eng, sems):
        self.eng = eng
        self.sems = sems
        self.n = 0


class Sched:
    def __init__(self, nc, es, n_dma_sems=8):
        self.nc = nc
        mk = lambda nm: es.enter_context(nc.semaphore(nm))
        self.pe = Eng("pe", nc.tensor, mk("s_pe"))
        self.act = Eng("act", nc.scalar, mk("s_act"))
        self.dve = Eng("dve", nc.vector, mk("s_dve"))
        self.pool = Eng("pool", nc.gpsimd, mk("s_pool"))
        self.sp = Eng("sp", nc.sync, mk("s_sp"))
        self.engs = (self.pe, self.act, self.dve, self.pool, self.sp)
        self.sems = {}
        for e in self.engs:
            self.sems[id(e.sem)] = e.sem
        self.q_ld = DmaQ(self.sp, [mk(f"s_ld{i}") for i in range(n_dma_sems)])
        self.q_st = DmaQ(self.pool, [mk(f"s_st{i}") for i in range(n_dma_sems)])
        self.q_l2 = DmaQ(self.act, [mk(f"s_lb{i}") for i in range(n_dma_sems)])
        self.queues = (self.q_ld, self.q_st, self.q_l2)
        for q in self.queues:
            for s in q.sems:
                self.sems[id(s)] = s
        self.n_inst = 0

    def _wait(self, eng, deps, same_ok=True):
        for sid, val in deps.items():
            if eng.waited.get(sid, 0) >= val:
                continue
            if same_ok and sid == id(eng.sem):
                continue
            eng.h.wait_ge(self.sems[sid], val)
            eng.waited[sid] = val

    @staticmethod
    def _merge(d, o):
        for k, v in o.items():
            if d.get(k, 0) < v:
                d[k] = v

    def _deps(self, reads, writes, acc):
        deps = {}
        for b in reads:
            self._merge(deps, b.writers)
            if b.excl:
                self._merge(deps, b.readers)
        for b in writes:
            self._merge(deps, b.readers)
            if not acc:
                self._merge(deps, b.writers)
            else:
                self._merge(deps, b.base)
        return deps

    def _commit(self, reads, writes, tok, acc, deps=None):
        sid, val = tok
        for b in reads:
            if b.readers.get(sid, 0) < val:
                b.readers[sid] = val
        for b in writes:
            if acc:
                if b.writers.get(sid, 0) < val:
                    b.writers[sid] = val
            else:
                b.writers = {sid: val}
                b.readers = {}
                b.base = dict(deps or {})

    def op(self, eng, fn, reads=(), writes=(), acc=False):
        deps = self._deps(reads, writes, acc)
        self._wait(eng, deps, same_ok=(eng is self.pe))
        inst = fn()
        eng.count += 1
        inst.then_inc(eng.sem, 1)
        self._commit(reads, writes, (id(eng.sem), eng.count), acc, deps)
        self.n_inst += 1
        return inst

    def dma(self, q, out, in_, reads=(), writes=(), acc=False, **kw):
        eng = q.eng
        K = len(q.sems)
        i = q.n
        sem = q.sems[i % K]
        val = 16 * (i // K + 1)
        deps = self._deps(reads, writes, acc)
        if i >= K:
            self._merge(deps, {id(sem): val - 16})
        self._wait(eng, deps, same_ok=False)
        eng.h.dma_start(out=out, in_=in_, **kw).then_inc(sem, 16)
        q.n += 1
        self._commit(reads, writes, (id(sem), val), acc, deps)
        self.n_inst += 1

    def all_tokens(self):
        deps = {}
        for e in self.engs:
            if e.count:
                deps[id(e.sem)] = e.count
        for q in self.queues:
            K = len(q.sems)
            for j, s in enumerate(q.sems):
                n = (q.n - j + K - 1) // K if q.n > j else 0
                if n:
                    deps[id(s)] = 16 * n
        return deps

    def barrier(self):
        deps = self.all_tokens()
        for e in self.engs:
            self._wait(e, deps, same_ok=True)

    def finish(self):
        self._wait(self.sp, self.all_tokens(), same_ok=False)


class Cfg:
    def __init__(self, TP, TS, depth, debug=False):
        self.TP, self.TS, self.depth, self.debug = TP, TS, depth, debug
        self.p2stage = 99
        self.p2_every = 2
        self.TA = TP + TS
        self.seqs = [(0, TP), (TP, TS)]


def build(cfg):
    nc = bass.Bass("TRN2", target_bir_lowering=False)
    L = cfg.depth
    TA = cfg.TA
    dt_in = lambda name, shape: nc.dram_tensor(name, shape, F32, kind="ExternalInput").ap()
    xp = dt_in("xp", [cfg.TP, D])
    xs = dt_in("xs", [cfg.TS, D])
    w_in = dt_in("w_in", [L, D, NIN])
    wpa_d = dt_in("w_proj_a", [L, D, D])
    wpb_d = dt_in("w_proj_b", [L, D, D])
    wo_d = dt_in("w_out", [L, D, D])
    pcol_d = dt_in("pcol", [L, 128, NPC])
    brow_d = dt_in("brow", [L, NBR])
    wup_d = dt_in("wup_aug", [L, 2, 65, D])
    aup_d = dt_in("aup_aug", [L, 2, 65, D])
    consts_d = dt_in("consts", [128, NCO])
    rope_d = dt_in("rope", [max(cfg.TP, cfg.TS), 64])
    yp = nc.dram_tensor("yp", [cfg.TP, D], F32, kind="ExternalOutput").ap()
    ys = nc.dram_tensor("ys", [cfg.TS, D], F32, kind="ExternalOutput").ap()
    dbg_kind = "ExternalOutput" if cfg.debug else "Internal"
    _pt_parts = [(0, 3328, nc.dram_tensor("pT_slab", [3328, TA], F32, kind=dbg_kind).ap()),
                 (3328, 4352, nc.dram_tensor("pT_ga", [1024, TA], F32).ap()),
                 (5888, 6912, nc.dram_tensor("pT_gb", [1024, TA], F32).ap()),
                 (6912, 8960, nc.dram_tensor("pT_mg", [2048, TA], F32).ap())]

    class _PT:
        def __getitem__(self, key):
            rs, cs = key
            for (a, b, ap) in _pt_parts:
                if a <= rs.start and rs.stop <= b:
                    return ap[rs.start - a:rs.stop - a, cs]
            raise KeyError(key)
    pT = _PT()
    qkv = nc.dram_tensor("qkv", [TA, 1536], F32, kind=dbg_kind).ap()
    ysc = nc.dram_tensor("ysc", [4, TA, D], F32, kind=dbg_kind).ap()
    xsc = [nc.dram_tensor(f"xsc{i}", [TA, D], F32).ap() for i in range(2)]

    def x_src(l, g0, n):
        if l == 0:
            return xp[g0:g0 + n, :] if g0 < cfg.TP else xs[g0 - cfg.TP:g0 - cfg.TP + n, :]
        return xsc[(l - 1) % 2][g0:g0 + n, :]

    def x_dst(l, g0, n):
        if l == L - 1:
            return yp[g0:g0 + n, :] if g0 < cfg.TP else ys[g0 - cfg.TP:g0 - cfg.TP + n, :]
        return xsc[l % 2][g0:g0 + n, :]

    with ExitStack() as es:
        S = Sched(nc, es)
        pe, act, dve, pool = S.pe, S.act, S.dve, S.pool
        V, G, A, PE = nc.vector, nc.gpsimd, nc.scalar, nc.tensor

        def EH(e):
            return {"dve": V, "pool": G, "act": A}[e.name]

        def tt(e, out, in0, in1, op, R, W, acc=False):
            h = EH(e)
            S.op(e, lambda: h.tensor_tensor(out=out, in0=in0, in1=in1, op=op), R, W, acc)

        def ts(e, out, in0, s1, op0, R, W, s2=None, op1=None, acc=False):
            h = EH(e)
            if op1 is None:
                S.op(e, lambda: h.tensor_scalar(out=out, in0=in0, scalar1=s1, scalar2=None, op0=op0), R, W, acc)
            else:
                S.op(e, lambda: h.tensor_scalar(out=out, in0=in0, scalar1=s1, scalar2=s2, op0=op0, op1=op1), R, W, acc)

        def actf(out, in_, func, R, W, bias=0.0, scale=1.0, acc=False):
            S.op(act, lambda: A.activation(out=out, in_=in_, func=func, bias=bias, scale=scale), R, W, acc)

        def cp(e, out, in_, R, W, acc=False):
            if e is act:
                S.op(act, lambda: A.copy(out=out, in_=in_), R, W, acc)
            else:
                h = EH(e)
                S.op(e, lambda: h.tensor_copy(out=out, in_=in_), R, W, acc)

        def mm(out, lhsT, rhs, start, stop, R, W, acc=False, skip=False):
            if skip:
                S.op(pe, lambda: PE.matmul(out, lhsT, rhs, start=start, stop=stop, skip_group_check=True), R, W, acc)
            else:
                S.op(pe, lambda: PE.matmul(out, lhsT, rhs, start=start, stop=stop), R, W, acc)

        def trp(out, in_, ident, R, W):
            S.op(pe, lambda: PE.transpose(out, in_, ident), R, W)

        def ld(out, in_, W, R=(), q=None, **kw):
            S.dma(q or S.q_ld, out, in_, reads=R, writes=W, **kw)

        def st(out, in_, R, W, acc=True):
            S.dma(S.q_st, out, in_, reads=R, writes=W, acc=acc)

        uid = [0]

        def T_(st_, name, shape, dt=F32):
            uid[0] += 1
            return st_.enter_context(nc.sbuf_tensor(f"{name}_{uid[0]}", shape, dt))

        def P_(st_, name, shape, dt=F32):
            uid[0] += 1
            return st_.enter_context(nc.psum_tensor(f"{name}_{uid[0]}", shape, dt))
        cst = T_(es, "cst", [128, NCO])
        identb = T_(es, "identb", [128, 128], BF16)
        b_cst = Buf("cst")
        ld(cst[:], consts_d[:, :], [b_cst])
        cp(pool, identb[:], cst[:, CO_ID:CO_ID + 128], [b_cst], [b_cst])
        identf = cst[:, CO_ID:CO_ID + 128]
        b_pT, b_qkv, b_ysc = Buf("pT"), Buf("qkv"), Buf("ysc")
        b_x = [Buf("x_in"), Buf("xsc0"), Buf("xsc1"), Buf("y_out")]

        def bx_src(l):
            return b_x[0] if l == 0 else b_x[1 + (l - 1) % 2]

        def bx_dst(l):
            return b_x[3] if l == L - 1 else b_x[1 + l % 2]

        for l in range(L):
            with ExitStack() as ls:
                pcol = T_(ls, "pcol", [128, NPC])
                pder = T_(ls, "pder", [128, 64])
                b_par = Buf("par")
                S.barrier()
                ld(pcol[:], pcol_d[l, :, :], [b_par])
                ts(dve, pder[:, 0:24], pcol[:, PC_MU:PC_MU + 24], 0.5, ALU.mult, [b_par], [b_par])
                ts(dve, pder[:, 24:48], pcol[:, PC_MU:PC_MU + 24], -1.0, ALU.mult, [b_par], [b_par], 1.0, ALU.add)
                ts(dve, pder[:, 48:56], pcol[:, PC_KA:PC_KA + 8], -1.0, ALU.mult, [b_par], [b_par], 1.0, ALU.add)
                ts(dve, pder[:, 56:60], pcol[:, PC_MUWD:PC_MUWD + 4], 0.5, ALU.mult, [b_par], [b_par])
                ts(dve, pder[:, 60:64], pcol[:, PC_MUWD:PC_MUWD + 4], -1.0, ALU.mult, [b_par], [b_par], 1.0, ALU.add)

                phase1(nc, cfg, S, l, locals())
                if cfg.debug == "p1":
                    break
                phase2(nc, cfg, S, l, locals())
                if cfg.debug == "p2":
                    break
                phase3(nc, cfg, S, l, locals())
        S.barrier()
        S.finish()
    return nc


FM_BLOCKS = list(range(0, 34)) + list(range(46, 70))
TM_COL0 = 4352


def phase1(nc, cfg, S, l, E):
    g = lambda k: E[k]
    T_, P_, ld, st, tt, ts, actf, cp, mm, trp = (g(k) for k in
                                                   "T_ P_ ld st tt ts actf cp mm trp".split())
    pe, act, dve, pool = S.pe, S.act, S.dve, S.pool
    pcol, b_par, identb, b_cst = g("pcol"), g("b_par"), g("identb"), g("b_cst")
    w_in, pT, qkv, b_pT, b_qkv = g("w_in"), g("pT"), g("qkv"), g("b_pT"), g("b_qkv")
    x_src, bx_src = g("x_src"), g("bx_src")
    V, G, A, PE = nc.vector, nc.gpsimd, nc.scalar, nc.tensor
    TW = 512
    with ExitStack() as ps:
        S.barrier()
        wbf = T_(ps, "wbf", [128, 8, NIN], BF16)
        wst = [T_(ps, f"wst{i}", [128, 896]) for i in range(2)]
        hT2 = [T_(ps, f"hT{i}", [128, 8, TW], BF16) for i in range(2)]
        xt = [T_(ps, f"xt{i}", [128, D]) for i in range(2)]
        xsq = T_(ps, "xsq", [128, D])
        xn = [T_(ps, f"xn{i}", [128, D], BF16) for i in range(2)]
        ss = [T_(ps, f"ss{i}", [128, 2]) for i in range(2)]
        stg = [T_(ps, f"stg{i}", [128, TW]) for i in range(4)]
        ptr = [P_(ps, f"ptr{i}", [128, 8, 128], BF16) for i in range(2)]
        pout = [P_(ps, f"pout{i}", [128, TW]) for i in range(4)]
        b_w = Buf("wbf")
        b_wst = [Buf("wst0"), Buf("wst1")]
        b_hT2 = [Buf("hT0"), Buf("hT1")]
        b_xt = [Buf("xt0"), Buf("xt1")]
        b_xsq = Buf("xsq")
        b_xn = [Buf("xn0"), Buf("xn1")]
        b_ss = [Buf("ss0"), Buf("ss1")]
        b_stg = [Buf(f"stg{i}") for i in range(4)]
        b_ptr = [Buf("ptr0", True), Buf("ptr1", True)]
        b_pout = [Buf(f"pout{i}", True) for i in range(4)]
        k = 0
        cast_engs = (act, pool, dve)
        for kc in range(8):
            for cc in range(10):
                i = k % 2
                ld(wst[i][:], w_in[l, kc * 128:(kc + 1) * 128, cc * 896:(cc + 1) * 896], [b_wst[i]])
                cp(cast_engs[k % 3], wbf[:, kc, cc * 896:(cc + 1) * 896], wst[i][:], [b_wst[i]], [b_w], acc=True)
                k += 1
        gcol = pcol[:, PC_NG:PC_NG + 8]
        n_tiles = cfg.TA // TW
        sub = 0
        oi = 0
        for ti in range(n_tiles):
            g0 = ti * TW
            hT, b_hT = hT2[ti % 2], b_hT2[ti % 2]
            for s4 in range(TW // 128):
                i = sub % 2
                sub += 1
                gs = g0 + s4 * 128
                ld(xt[i][:], x_src(l, gs, 128), [b_xt[i]], R=[bx_src(l)])
                actf(xsq[:], xt[i][:], AF.Square, [b_xt[i]], [b_xsq])
                S.op(dve, lambda: V.reduce_sum(out=ss[i][:, 0:1], in_=xsq[:], axis=AX.X), [b_xsq], [b_ss[i]])
                actf(ss[i][:, 1:2], ss[i][:, 0:1], AF.Sqrt, [b_ss[i]], [b_ss[i]], bias=NORM_EPS, scale=1.0 / D)
                S.op(dve, lambda: V.reciprocal(out=ss[i][:, 1:2], in_=ss[i][:, 1:2]), [b_ss[i]], [b_ss[i]])
                ts(dve, xn[i][:], xt[i][:], ss[i][:, 1:2], ALU.mult, [b_xt[i], b_ss[i]], [b_xn[i]])
                for kc in range(8):
                    trp(ptr[i][:, kc, :], xn[i][:, kc * 128:(kc + 1) * 128], identb[:], [b_xn[i], b_cst], [b_ptr[i]])
                tt(dve, hT[:, :, s4 * 128:(s4 + 1) * 128], ptr[i][:],
                   gcol.unsqueeze(2).to_broadcast([128, 8, 128]), ALU.mult, [b_ptr[i], b_par], [b_hT])
            for fb in FM_BLOCKS:
                o = oi % 4
                oi += 1
                for kc in range(8):
                    mm(pout[o][:], wbf[:, kc, fb * 128:(fb + 1) * 128], hT[:, kc, :], kc == 0, kc == 7,
                       [b_w, b_hT], [b_pout[o]])
                cp(act if o % 2 == 0 else dve, stg[o][:], pout[o][:], [b_pout[o]], [b_stg[o]])
                st(pT[fb * 128:(fb + 1) * 128, g0:g0 + TW], stg[o][:], [b_stg[o]], [b_pT])
            for s4 in range(TW // 128):
                for cg in range(3):
                    o = oi % 4
                    oi += 1
                    for kc in range(8):
                        mm(pout[o][:], hT[:, kc, s4 * 128:(s4 + 1) * 128],
                           wbf[:, kc, TM_COL0 + cg * 512:TM_COL0 + (cg + 1) * 512], kc == 0, kc == 7,
                           [b_w, b_hT], [b_pout[o]])
                    cp(act if o % 2 == 0 else dve, stg[o][:], pout[o][:], [b_pout[o]], [b_stg[o]])
                    st(qkv[g0 + s4 * 128:g0 + (s4 + 1) * 128, cg * 512:(cg + 1) * 512], stg[o][:],
                       [b_stg[o]], [b_qkv])
        S.barrier()


def phase2(nc, cfg, S, l, E):
    g = lambda k: E[k]
    T_, P_, ld, st, tt, ts, actf, cp, mm, trp = (g(k) for k in
                                                   "T_ P_ ld st tt ts actf cp mm trp".split())
    pe, act, dve, pool = S.pe, S.act, S.dve, S.pool
    pcol, pder, b_par, identb, b_cst, cst = (g(k) for k in "pcol pder b_par identb b_cst cst".split())
    pT, b_pT, ysc, b_ysc = g("pT"), g("b_pT"), g("ysc"), g("b_ysc")
    wup_d, aup_d = g("wup_d"), g("aup_d")
    V, G, A, PE = nc.vector, nc.gpsimd, nc.scalar, nc.tensor
    identf = cst[:, CO_ID:CO_ID + 128]
    BO = cst[:, CO_BO:CO_BO + 128]
    HS = cst[:, CO_HS:CO_HS + 2]
    C2, C4 = 2 * C, 4 * C
    NG = 4
    with ExitStack() as ps:
        S.barrier()
        wup = T_(ps, "wup", [65, 2, D])
        aup = T_(ps, "aup", [65, 2, D])
        H = T_(ps, "H", [128, 4, 8, 64])
        Hq = T_(ps, "Hq", [128, 4, 8, 64], BF16)
        slab = T_(ps, "slab", [128, 24, C + 2])
        wdt = T_(ps, "wdt", [64, C + 2])
        adt = T_(ps, "adt", [64, C + 2])
        wtmp = T_(ps, "wtmp", [64, C])
        atmp = T_(ps, "atmp", [64, C])
        wds = T_(ps, "wds", [65, C])
        ads = T_(ps, "ads", [65, C])
        sg = T_(ps, "sg", [128, D])
        eN = T_(ps, "eN", [128, 8, C])
        eX = T_(ps, "eX", [128, 8, C])
        aT = T_(ps, "aT", [128, 8, C])
        tmp24 = T_(ps, "tmp24", [128, 24, C])
        sh24 = T_(ps, "sh24", [128, 24, C])
        kkn = T_(ps, "kkn", [128, 8, C])
        sq = T_(ps, "sq", [128, 8, C])
        bb = T_(ps, "bb", [128, 8, C])
        kmod = T_(ps, "kmod", [128, 8, C])
        prod = T_(ps, "prod", [128, 8, C])
        cc = T_(ps, "cc", [128, 16])
        bon = T_(ps, "bon", [128, D])
        eL2 = [T_(ps, f"eL{i}", [128, 8, C]) for i in range(2)]
        QR2 = [T_(ps, f"QR{i}", [128, 8, 3, C], BF16) for i in range(2)]
        KhT2 = [T_(ps, f"KhT{i}", [128, 8, C], BF16) for i in range(2)]
        BhT2 = [T_(ps, f"BhT{i}", [128, 8, C], BF16) for i in range(2)]
        Vb2 = [T_(ps, f"Vb{i}", [128, D], BF16) for i in range(2)]
        Kh2 = [T_(ps, f"Kh{i}", [128, D], BF16) for i in range(2)]
        Bhn2 = [T_(ps, f"Bhn{i}", [128, D], BF16) for i in range(2)]
        M12 = [T_(ps, f"M12_{i}", [128, C4 + C + 64], BF16) for i in range(NG)]
        XU = [[T_(ps, f"XU_{i}_{lv}", [128, C2 + 64], BF16) for lv in range(6)] for i in range(NG)]
        Uall = T_(ps, "Uall", [128, D], BF16)
        Ysb = T_(ps, "Ysb", [128, D])
        tmpH = T_(ps, "tmpH", [128, 4, 64])
        pr = P_(ps, "pr", [128, D])
        pSX = [P_(ps, f"pSX{i}", [128, 512]) for i in range(NG)]
        pY = P_(ps, "pY", [128, 512])
        pH = P_(ps, "pH", [128, 4, 128])
        B = {n: Buf(n) for n in ("lora H0 H1 H2 H3 Hq0 Hq1 Hq2 Hq3 slab wdt adt wtmp atmp wds ads sg eN eX aT "
                                 "tmp24_0 tmp24_1 tmp24_2 sh24_0 sh24_1 sh24_2 kkn sq bb kmod prod cc bon Uall Ysb tmpH "
                                 "eL0 eL1 QR0 QR1 KhT0 KhT1 BhT0 BhT1 Vb0 Vb1 Kh0 Kh1 Bhn0 Bhn1").split()}
        for n_ in ["pr0", "pr1", "pY", "pH"] + [f"pS{i}" for i in range(NG)]:
            B[n_] = Buf(n_, True)
        bM = [Buf(f"M12_{i}") for i in range(NG)]
        bXU = [[Buf(f"XU{i}{lv}") for lv in range(6)] for i in range(NG)]
        bPR = [B["pr0"], B["pr1"]]
        for d in range(2):
            ld(wup[:, d, :], wup_d[l, d, :, :], [B["lora"]], acc=True)
            ld(aup[:, d, :], aup_d[l, d, :, :], [B["lora"]], acc=True)
        for sd in range(4):
            S.op(pool, lambda: G.memset(H[:, sd, :, :], 0.0), (), [B[f"H{sd}"]])
            S.op(pool, lambda: G.memset(Hq[:, sd, :, :], 0.0), (), [B[f"Hq{sd}"]])
        S.op(pool, lambda: G.memset(wds[64:65, :], 1.0), (), [B["wds"]])
        S.op(pool, lambda: G.memset(ads[64:65, :], 1.0), (), [B["ads"]])

        hmu_bc = pder[:, 0:24].unsqueeze(2).to_broadcast([128, 24, C])
        omu_bc = pder[:, 24:48].unsqueeze(2).to_broadcast([128, 24, C])
        omka_bc = pder[:, 48:56].unsqueeze(2).to_broadcast([128, 8, C])
        kk_bc = pcol[:, PC_KK:PC_KK + 8].unsqueeze(2).to_broadcast([128, 8, C])
        ka_bc = pcol[:, PC_KA:PC_KA + 8].unsqueeze(2).to_broadcast([128, 8, C])
        rk_bc = pcol[:, PC_RK:PC_RK + 8].unsqueeze(2).to_broadcast([128, 8, C])

        class U_:
            pass

        def mk_unit(si, ch, d, up):
            u = U_()
            u.si, u.ch, u.d, u.up = si, ch, d, up
            u.s0, T = cfg.seqs[si]
            u.nch = T // C
            u.g0 = u.s0 + ch * C
            u.sd = si * 2 + d
            return u

        def prep(u):
            si, ch, d, up, g0, sd = u.si, u.ch, u.d, u.up, u.g0, u.sd
            eL, QR, KhT, BhT, Vb, Kh, Bhn = eL2[up], QR2[up], KhT2[up], BhT2[up], Vb2[up], Kh2[up], Bhn2[up]
            beL, bQR, bKhT, bBhT, bVb, bKh, bBhn = (B[f"{n}{up}"] for n in ("eL", "QR", "KhT", "BhT", "Vb", "Kh", "Bhn"))
            TRo = CO_TRF if d == 0 else CO_TRB
            TRinc = cst[:, TRo:TRo + C]
            TRexc = cst[:, TRo + C:TRo + C2]
            first, lastc = (ch == 0), (ch == u.nch - 1)
            lo = 1 if first else 0
            hi = C + 1 if lastc else C + 2
            if first:
                S.op(pool, lambda: G.memset(slab[:, :, 0:1], 0.0), (), [B["slab"]])
                S.op(pool, lambda: G.memset(wdt[:, 0:1], 0.0), (), [B["wdt"]])
                S.op(pool, lambda: G.memset(adt[:, 0:1], 0.0), (), [B["adt"]])
            if lastc:
                S.op(pool, lambda: G.memset(slab[:, :, C + 1:C + 2], 0.0), (), [B["slab"]], acc=not first)
                S.op(pool, lambda: G.memset(wdt[:, C + 1:C + 2], 0.0), (), [B["wdt"]], acc=not first)
                S.op(pool, lambda: G.memset(adt[:, C + 1:C + 2], 0.0), (), [B["adt"]], acc=not first)
            c0 = g0 - 1 + lo
            c1 = g0 - 1 + hi
            ld(wdt[:, lo:hi], pT[3072 + d * 64:3072 + (d + 1) * 64, c0:c1], [B["wdt"]], R=[b_pT], acc=(first or lastc))
            ld(adt[:, lo:hi], pT[3200 + d * 64:3200 + (d + 1) * 64, c0:c1], [B["adt"]], R=[b_pT], acc=(first or lastc))
            for jb in (2, 3, 0, 1, 4, 5):
                src = pT[jb * 512:(jb + 1) * 512, c0:c1].rearrange("(j p) t -> p j t", p=128)
                ld(slab[:, jb * 4:(jb + 1) * 4, lo:hi], src, [B["slab"]], R=[b_pT], acc=(jb != 2 or first or lastc))
            yield
            for (src_t, tmp_t, dst_t, bs, bt, bd, co) in ((wdt, wtmp, wds, "wdt", "wtmp", "wds", d),
                                                          (adt, atmp, ads, "adt", "atmp", "ads", 2 + d)):
                tt(dve, tmp_t[:], src_t[:, 0:C], src_t[:, 2:C + 2], ALU.add, [B[bs]], [B[bt]])
                ts(dve, tmp_t[:], tmp_t[:], pder[0:64, 56 + co:57 + co], ALU.mult, [B[bt], b_par], [B[bt]])
                S.op(dve, lambda: V.scalar_tensor_tensor(out=dst_t[0:64, :], in0=src_t[:, 1:C + 1],
                                                         scalar=pder[0:64, 60 + co:61 + co], in1=tmp_t[:],
                                                         op0=ALU.mult, op1=ALU.add),
                     [B[bs], B[bt], b_par], [B[bd]])
            actf(wds[0:64, :], wds[0:64, :], AF.Tanh, [B["wds"]], [B["wds"]])
            yield

            def shift_third(th, e1, e2):
                t8 = slice(th * 8, th * 8 + 8)
                bt_, bs_ = B[f"tmp24_{th}"], B[f"sh24_{th}"]
                hm = pder[:, th * 8:th * 8 + 8].unsqueeze(2).to_broadcast([128, 8, C])
                om = pder[:, 24 + th * 8:24 + th * 8 + 8].unsqueeze(2).to_broadcast([128, 8, C])
                tt(e1, tmp24[:, t8, :], slab[:, t8, 0:C], slab[:, t8, 2:C + 2], ALU.add, [B["slab"]], [bt_])
                tt(e1, tmp24[:, t8, :], tmp24[:, t8, :], hm, ALU.mult, [bt_, b_par], [bt_])
                tt(e2, sh24[:, t8, :], slab[:, t8, 1:C + 1], om, ALU.mult, [B["slab"], b_par], [bs_])
                tt(e2, sh24[:, t8, :], sh24[:, t8, :], tmp24[:, t8, :], ALU.add, [bs_, bt_], [bs_])
            rs, ks, vs = sh24[:, 0:8, :], sh24[:, 8:16, :], sh24[:, 16:24, :]
            pr3 = pr[:].rearrange("p (j c) -> p j c", j=8)
            shift_third(1, pool, dve)
            yield
            for hf in range(2):
                mm(pr[:, hf * 512:(hf + 1) * 512], wds[:, :], wup[:, d, hf * 512:(hf + 1) * 512], True, True,
                   [B["wds"], B["lora"]], [bPR[hf]])
            actf(sg[:], pr[:], AF.Sigmoid, bPR, [B["sg"]])
            yield
            tt(pool, kkn[:], ks, kk_bc, ALU.mult, [B["sh24_1"], b_par], [B["kkn"]])
            shift_third(0, dve, pool)
            yield
            for j in range(8):
                mm(pr[:, j * C:(j + 1) * C], aup[:, d, j * 128:(j + 1) * 128], ads[:, :], True, True,
                   [B["ads"], B["lora"]], [bPR[j // 4]])
            actf(aT[:], pr3, AF.Sigmoid, bPR, [B["aT"]])
            actf(sq[:], kkn[:], AF.Square, [B["kkn"]], [B["sq"]])
            yield
            for j in range(8):
                mm(pr[:, j * C:(j + 1) * C], sg[:, j * 128:(j + 1) * 128], TRinc, True, True,
                   [B["sg"], b_cst], [bPR[j // 4]])
            actf(eL[:], pr3, AF.Exp, bPR, [beL])
            actf(eN[:], pr3, AF.Exp, bPR, [B["eN"]], scale=-1.0)
            yield
            sq2 = sq[:].rearrange("p j c -> p (j c)")
            for hf in range(2):
                mm(pr[:, hf * 512:(hf + 1) * 512], BO, sq2[:, hf * 512:(hf + 1) * 512], True, True,
                   [B["sq"], b_cst], [bPR[hf]])
            actf(sq[:], pr3, AF.Ln, bPR + [B["sq"]], [B["sq"]], bias=1e-24)
            actf(sq[:], sq[:], AF.Exp, [B["sq"]], [B["sq"]], scale=-0.5)
            tt(dve, kmod[:], aT[:], ka_bc, ALU.mult, [B["aT"], b_par], [B["kmod"]])
            tt(dve, kmod[:], kmod[:], omka_bc, ALU.add, [B["kmod"], b_par], [B["kmod"]])
            yield
            for j in range(8):
                mm(pr[:, j * C:(j + 1) * C], sg[:, j * 128:(j + 1) * 128], TRexc, True, True,
                   [B["sg"], b_cst], [bPR[j // 4]])
            actf(eX[:], pr3, AF.Exp, bPR, [B["eX"]])
            shift_third(2, pool, dve)
            yield
            tt(pool, kmod[:], kmod[:], ks, ALU.mult, [B["kmod"], B["sh24_1"]], [B["kmod"]])
            tt(pool, kkn[:], kkn[:], sq[:], ALU.mult, [B["kkn"], B["sq"]], [B["kkn"]])
            yield
            tt(pool, bb[:], kkn[:], aT[:], ALU.mult, [B["kkn"], B["aT"]], [B["bb"]])
            tt(dve, prod[:], rs, kmod[:], ALU.mult, [B["sh24_0"], B["kmod"]], [B["prod"]])
            tt(dve, QR[:, :, 0, :], kkn[:], eX[:], ALU.mult, [B["kkn"], B["eX"]], [bQR])
            yield
            tt(pool, prod[:], prod[:], rk_bc, ALU.mult, [B["prod"], b_par], [B["prod"]])
            tt(dve, QR[:, :, 1, :], rs, eL[:], ALU.mult, [B["sh24_0"], beL], [bQR], acc=True)
            S.op(act, lambda: A.copy(out=QR[:, :, 2, :], in_=QR[:, :, 0, :]), [bQR], [bQR], acc=True)
            yield
            tt(pool, KhT[:], kmod[:], eN[:], ALU.mult, [B["kmod"], B["eN"]], [bKhT])
            for j in range(8):
                mm(pr[:, 2 * j:2 * j + 2], prod[:, j, :], HS, True, True, [B["prod"], b_cst], [bPR[0]])
            cp(act, cc[:], pr[:, 0:16], [bPR[0]], [B["cc"]])
            yield
            tt(pool, BhT[:], bb[:], eN[:], ALU.mult, [B["bb"], B["eN"]], [bBhT])
            for j in range(8):
                trp(pr[:, j * 128:(j + 1) * 128], vs[:, j, :], identf, [B["sh24_2"], b_cst], [bPR[j // 4]])
            cp(act, Vb[:], pr[:], bPR, [bVb])
            tt(dve, bon[:].rearrange("p (h v) -> p h v", h=16), pr[:].rearrange("p (h v) -> p h v", h=16),
               cc[:].unsqueeze(2).to_broadcast([128, 16, 64]), ALU.mult, bPR + [B["cc"]], [B["bon"]])
            st(ysc[2 + d, g0:g0 + C, :], bon[:], [B["bon"]], [b_ysc])
            yield
            for j in range(8):
                mm(pr[:, j * 128:(j + 1) * 128], KhT[:, j, :], identb[:], True, True, [bKhT, b_cst], [bPR[j // 4]])
            cp(act, Kh[:], pr[:], bPR, [bKh])
            yield
            for j in range(8):
                mm(pr[:, j * 128:(j + 1) * 128], BhT[:, j, :], identb[:], True, True, [bBhT, b_cst], [bPR[j // 4]])
            S.op(act, lambda: A.mul(out=Bhn[:], in_=pr[:], mul=-1.0), bPR, [bBhn])

        def heads(u):
            if cfg.p2stage < 50:
                return
            si, ch, d, up, g0, sd = u.si, u.ch, u.d, u.up, u.g0, u.sd
            eL, QR, KhT, BhT, Vb, Kh, Bhn = eL2[up], QR2[up], KhT2[up], BhT2[up], Vb2[up], Kh2[up], Bhn2[up]
            beL, bQR, bKhT, bBhT, bVb, bKh, bBhn = (B[f"{n}{up}"] for n in ("eL", "QR", "KhT", "BhT", "Vb", "Kh", "Bhn"))
            bH, bHq = B[f"H{sd}"], B[f"Hq{sd}"]
            MT = cst[:, (CO_MTF if d == 0 else CO_MTB):(CO_MTF if d == 0 else CO_MTB) + C4]
            M3 = cst[:, (CO_M3F if d == 0 else CO_M3B):(CO_M3F if d == 0 else CO_M3B) + C]
            last = C - 1 if d == 0 else 0

            def head(h, gi):
                j, par = h // 2, h % 2
                sl = slice(par * 64, par * 64 + 64)
                hs = slice(h * 64, (h + 1) * 64)
                bS = B[f"pS{gi}"]
                pb = pSX[gi]
                Mt = M12[gi]
                qr01 = QR[sl, j, 0:2, :].rearrange("p a c -> p (a c)")
                qr12 = QR[sl, j, 1:3, :].rearrange("p a c -> p (a c)")
                mm(pb[:, 0:C2], KhT[sl, j, :], qr01, True, True, [bKhT, bQR], [bS])
                mm(pb[:, C2:C4], BhT[sl, j, :], qr12, True, True, [bBhT, bQR], [bS])
                tt(dve, Mt[:, 0:C4], pb[:, :], MT, ALU.mult, [bS, b_cst], [bM[gi]])
                yield
                mm(pb[:, 0:C], QR[sl, j, 0, :], BhT[sl, j, :], True, True, [bQR, bBhT], [bS])
                mm(pb[:, 256:320], QR[sl, j, 0, :], Hq[sl, sd, j, :], True, False, [bQR, bHq], [bS])
                mm(pb[:, 256:320], Mt[:, 0:C], Vb[:, hs], False, True, [bM[gi], bVb], [bS])
                tt(dve, Mt[:, C4:C4 + C], pb[:, 0:C], M3, ALU.mult, [bS, b_cst], [bM[gi]], acc=True)
                cp(act, Mt[:, C4 + C:C4 + C + 64], pb[:, 256:320], [bS], [bM[gi]], acc=True)
                yield
                Tt, o, bT = Mt, 3 * C, bM[gi]
                for lv in range(7):
                    XT_, X_, U_ = Tt[:, o:o + C], Tt[:, o + C:o + C2], Tt[:, o + C2:o + C2 + 64]
                    if lv < 5:
                        mm(pb[:, C:C2 + 64], XT_, Tt[:, o + C:o + C2 + 64], True, True, [bT], [bS])
                        mm(pb[:, 0:C], X_, XT_, True, True, [bT], [bS])
                    elif lv == 5:
                        mm(pb[:, C2:C2 + 64], XT_, U_, True, True, [bT], [bS])
                        mm(pb[:, 0:C], X_, XT_, True, True, [bT], [bS])
                    else:
                        mm(pb[:, C2:C2 + 64], XT_, U_, True, True, [bT], [bS])
                    if lv < 6:
                        nt, bn = XU[gi][lv], bXU[gi][lv]
                        ev = act
                        cp(ev, nt[:, 0:(C2 if lv < 5 else C)], pb[:, 0:(C2 if lv < 5 else C)], [bS], [bn])
                        tt(dve, nt[:, C2:C2 + 64], pb[:, C2:C2 + 64], U_, ALU.add, [bS, bT], [bn], acc=True)
                        Tt, o, bT = nt, 0, bn
                    else:
                        tt(dve, Uall[:, hs], pb[:, C2:C2 + 64], U_, ALU.add, [bS, bT], [B["Uall"]], acc=True)
                    yield
                yo = (h % 8) * 64
                mm(pY[:, yo:yo + 64], QR[sl, j, 1, :], Hq[sl, sd, j, :], True, False, [bQR, bHq], [B["pY"]])
                mm(pY[:, yo:yo + 64], Mt[:, C:C2], Vb[:, hs], False, False, [bM[gi], bVb], [B["pY"]])
                mm(pY[:, yo:yo + 64], Mt[:, C2:3 * C], Uall[:, hs], False, True, [bM[gi], B["Uall"]], [B["pY"]])

            for q in range(16 // NG):
                gens = [head(q * NG + gi, gi) for gi in range(NG)]
                while gens:
                    for gen in list(gens):
                        try:
                            next(gen)
                        except StopIteration:
                            gens.remove(gen)
                    yield
                for j in range(q * NG // 2, (q + 1) * NG // 2):
                    if j % 4 == 0 and j > 0 or False:
                        pass
                    jj = j % 4
                    mm(pH[:, jj, :], Kh[:, j * 128:(j + 1) * 128], Vb[:, j * 128:(j + 1) * 128], True, False,
                       [bKh, bVb], [B["pH"]])
                    mm(pH[:, jj, :], Bhn[:, j * 128:(j + 1) * 128], Uall[:, j * 128:(j + 1) * 128], False, True,
                       [bBhn, B["Uall"]], [B["pH"]])
                    if jj == 3:
                        hb = j // 4
                        cp(act, Ysb[:, hb * 512:(hb + 1) * 512], pY[:], [B["pY"]], [B["Ysb"]], acc=(hb == 1))
                        j0 = j - 3
                        for hp in range(2):
                            psl = slice(hp * 64, hp * 64 + 64)
                            tt(dve, tmpH[psl, :, :], H[psl, sd, j0:j0 + 4, :], pH[psl, :, hp * 64:hp * 64 + 64],
                               ALU.add, [bH, B["pH"]], [B["tmpH"]], acc=(hp == 1))
                        for hp in range(2):
                            psl = slice(hp * 64, hp * 64 + 64)
                            tt(pool, H[psl, sd, j0:j0 + 4, :], tmpH[psl, :, :],
                               eL[psl, j0:j0 + 4, last:last + 1].to_broadcast([64, 4, 64]), ALU.mult,
                               [B["tmpH"], beL], [bH], acc=True)
                yield
            cp(act, Hq[:, sd, :, :], H[:, sd, :, :], [bH], [bHq])
            st(ysc[d, g0:g0 + C, :], Ysb[:], [B["Ysb"]], [b_ysc])

        nchs = [T // C for (_, T) in cfg.seqs]
        units = []
        for i in range(max(nchs)):
            for si in range(len(cfg.seqs)):
                if i < nchs[si]:
                    units.append((si, i, 0))
                    units.append((si, nchs[si] - 1 - i, 1))

        def drain(gen):
            for _ in gen:
                pass

        prev = None
        for ui, (si, ch, d) in enumerate(units):
            u = mk_unit(si, ch, d, ui % 2)
            pg = prep(u)
            if prev is None:
                drain(pg)
            else:
                hg = heads(prev)
                r = 0
                pg_alive = True
                for _ in hg:
                    r += 1
                    if pg_alive and r % cfg.p2_every == 0:
                        try:
                            next(pg)
                        except StopIteration:
                            pg_alive = False
                if pg_alive:
                    drain(pg)
            prev = u
        drain(heads(prev))
        S.barrier()


def phase3(nc, cfg, S, l, E):
    g = lambda k: E[k]
    T_, P_, ld, st, tt, ts, actf, cp, mm, trp = (g(k) for k in
                                                   "T_ P_ ld st tt ts actf cp mm trp".split())
    pe, act, dve, pool = S.pe, S.act, S.dve, S.pool
    pcol, pder, b_par, identb, b_cst, cst = (g(k) for k in "pcol pder b_par identb b_cst cst".split())
    brow_d = g("brow_d")
    pT, b_pT, ysc, b_ysc, qkv, b_qkv = (g(k) for k in "pT b_pT ysc b_ysc qkv b_qkv".split())
    wpa_d, wpb_d, wo_d, rope_d = g("wpa_d"), g("wpb_d"), g("wo_d"), g("rope_d")
    x_src, x_dst, bx_src, bx_dst = g("x_src"), g("x_dst"), g("bx_src"), g("bx_dst")
    V, G, A, PE = nc.vector, nc.gpsimd, nc.scalar, nc.tensor
    identf = cst[:, CO_ID:CO_ID + 128]
    MPREV = cst[:, CO_MPREV:CO_MPREV + 128]
    MNEXT = cst[:, CO_MNEXT:CO_MNEXT + 128]
    with ExitStack() as ps:
        S.barrier()
        brow = T_(ps, "brow", [128, NBR])
        bder = T_(ps, "bder", [128, 64 + 16])
        ld(brow[:], brow_d[l, :].partition_broadcast(128), [b_par], acc=True)
        ts(dve, bder[:, 0:64], brow[:, BR_QG:BR_QG + 64], 0.125, ALU.mult, [b_par], [b_par], acc=True)
        actf(bder[:, 64:80], brow[:, BR_SINK:BR_SINK + 16], AF.Exp, [b_par], [b_par], acc=True)
        wts = [T_(ps, nm, [128, 8, D], BF16) for nm in ("wpa", "wpb", "wo")]
        wst = [T_(ps, f"wst3_{i}", [128, D]) for i in range(2)]
        yf, yb, bf_, bb_ = (T_(ps, nm, [128, D]) for nm in ("yf", "yb", "bf", "bb"))
        st16 = T_(ps, "st16", [128, 6, 16])
        gaT = T_(ps, "gaT", [128, 8, 128])
        yagT = T_(ps, "yagT", [128, 8, 128], BF16)
        qt = T_(ps, "qt", [128, D])
        tmpq = T_(ps, "tmpq", [128, D])
        qst = T_(ps, "qst", [128, 2, 16])
        qr = T_(ps, "qr", [128, D], BF16)
        qT = T_(ps, "qT", [64, 16, 128], BF16)
        kvraw = T_(ps, "kvraw", [128, 512])
        tmpk = T_(ps, "tmpk", [128, 256])
        kst = T_(ps, "kst", [128, 2, 4])
        kr = T_(ps, "kr", [128, 256], BF16)
        kT = [T_(ps, f"kT{i}", [64, 4, 128], BF16) for i in range(3)]
        vaug = [T_(ps, f"vaug{i}", [128, 4, 65], BF16) for i in range(3)]
        csq = T_(ps, "csq", [128, 64])
        csk = T_(ps, "csk", [128, 64])
        et = [T_(ps, f"et{i}", [128, 4, 128], BF16) for i in range(3)]
        den = T_(ps, "den", [128, 2, 4])
        og = T_(ps, "og", [128, D])
        gbT = T_(ps, "gbT", [128, 8, 128])
        ogT = T_(ps, "ogT", [128, 8, 128], BF16)
        mgT = T_(ps, "mgT", [128, 16, 128])
        t1 = T_(ps, "t1", [128, 8, 128])
        t2 = T_(ps, "t2", [128, 8, 128])
        mixT = T_(ps, "mixT", [128, 8, 128], BF16)
        xt = T_(ps, "xt3", [128, D])
        xo = T_(ps, "xo", [128, D])
        R0 = P_(ps, "R0", [128, D])
        R1 = P_(ps, "R1", [128, D])
        R2 = P_(ps, "R2", [128, D])
        R3 = P_(ps, "R3", [128, 512])
        R4 = P_(ps, "R4", [128, 512])
        names = ("w wst0 wst1 yf yb bf bb st16 gaT yagT qt tmpq qst qr qT kvraw tmpk kst kr kT0 kT1 kT2 "
                 "vaug0 vaug1 vaug2 csq csk et0 et1 et2 den og gbT ogT mgT t1 t2 mixT xt xo").split()
        B = {n: Buf(n) for n in names}
        for n_ in ("R0a", "R0b", "R1a", "R1b", "R2a", "R2b", "R3", "R4"):
            B[n_] = Buf(n_, True)
        bR0, bR1, bR2 = [B["R0a"], B["R0b"]], [B["R1a"], B["R1b"]], [B["R2a"], B["R2b"]]
        k = 0
        cast_engs = (act, pool, dve)
        for wi, wd_ in enumerate((wpa_d, wpb_d, wo_d)):
            for kc in range(8):
                i = k % 2
                ld(wst[i][:], wd_[l, kc * 128:(kc + 1) * 128, :], [B[f"wst{i}"]])
                cp(cast_engs[k % 3], wts[wi][:, kc, :], wst[i][:], [B[f"wst{i}"]], [B["w"]], acc=True)
                k += 1
        for i in range(3):
            S.op(pool, lambda: G.memset(vaug[i][:, :, 64:65], 1.0), (), [B[f"vaug{i}"]])
        wpa, wpb, wo = wts
        lng_bc = brow[:, BR_LNG:BR_LNG + D]
        lnb_bc = brow[:, BR_LNB:BR_LNB + D]
        qg8 = bder[:, 0:64]
        kg = brow[:, BR_KG:BR_KG + 64]
        esink = bder[:, 64:80]

        def v3(ap, h):
            return ap.rearrange("p (h v) -> p h v", h=h)

        def rope_norm(src3, nh, sq_t, st_t, cs_t, gvec, dst_bf, bsrc, bsq, bst, bcs, bdst):
            sq3 = v3(sq_t, nh)
            actf(sq_t, src3.rearrange("p h v -> p (h v)"), AF.Square, [bsrc], [bsq])
            S.op(dve, lambda: V.tensor_reduce(out=st_t[:, 0, :], in_=sq3, axis=AX.X, op=ALU.add), [bsq], [bst])
            actf(st_t[:, 1, :], st_t[:, 0, :], AF.Sqrt, [bst], [bst], bias=NORM_EPS, scale=1.0 / 64)
            S.op(dve, lambda: V.reciprocal(out=st_t[:, 1, :], in_=st_t[:, 1, :]), [bst], [bst])
            tt(dve, src3, src3, st_t[:, 1, :].unsqueeze(2).to_broadcast([128, nh, 64]), ALU.mult, [bsrc, bst], [bsrc])
            tt(dve, src3, src3, gvec.unsqueeze(1).to_broadcast([128, nh, 64]), ALU.mult, [bsrc, b_par], [bsrc])
            cos_bc = cs_t[:, 0:32].unsqueeze(1).to_broadcast([128, nh, 32])
            sin_bc = cs_t[:, 32:64].unsqueeze(1).to_broadcast([128, nh, 32])
            x1, x2 = src3[:, :, 0:32], src3[:, :, 32:64]
            s1_, s2_ = sq3[:, :, 0:32], sq3[:, :, 32:64]
            tt(dve, s1_, x1, cos_bc, ALU.mult, [bsrc, bcs], [bsq])
            tt(dve, s2_, x2, sin_bc, ALU.mult, [bsrc, bcs], [bsq])
            tt(dve, dst_bf[:, :, 0:32], s1_, s2_, ALU.subtract, [bsq], [bdst])
            tt(dve, s1_, x2, cos_bc, ALU.mult, [bsrc, bcs], [bsq])
            tt(dve, s2_, x1, sin_bc, ALU.mult, [bsrc, bcs], [bsq])
            tt(dve, dst_bf[:, :, 32:64], s1_, s2_, ALU.add, [bsq], [bdst])

        def proc_kv(s0, m):
            slot = m % 3
            gm = s0 + m * 128
            ld(kvraw[:], qkv[gm:gm + 128, 1024:1536], [B["kvraw"]], R=[b_qkv])
            ld(csk[:], rope_d[m * 128:(m + 1) * 128, :], [B["csk"]])
            cp(act, vaug[slot][:, :, 0:64], v3(kvraw[:, 256:512], 4), [B["kvraw"]], [B[f"vaug{slot}"]])
            rope_norm(v3(kvraw[:, 0:256], 4), 4, tmpk[:], kst, csk, kg, v3(kr[:], 4),
                      B["kvraw"], B["tmpk"], B["kst"], B["csk"], B["kr"])
            for gk in range(4):
                mm(R3[0:64, gk * 128:(gk + 1) * 128], kr[:, gk * 64:(gk + 1) * 64], identb[:], True, True,
                   [B["kr"], b_cst], [B["R3"]])
            cp(dve, kT[slot][:], R3[0:64, :].rearrange("p (g t) -> p g t", g=4), [B["R3"]], [B[f"kT{slot}"]])

        def tile3(si, n):
            s0, T = cfg.seqs[si]
            NB = T // 128
            g0 = s0 + n * 128
            ld(yf[:], ysc[0, g0:g0 + 128, :], [B["yf"]], R=[b_ysc])
            ld(yb[:], ysc[1, g0:g0 + 128, :], [B["yb"]], R=[b_ysc])
            ld(bf_[:], ysc[2, g0:g0 + 128, :], [B["bf"]], R=[b_ysc])
            ld(bb_[:], ysc[3, g0:g0 + 128, :], [B["bb"]], R=[b_ysc])
            ld(gaT[:], pT[3328:4352, g0:g0 + 128].rearrange("(j p) t -> p j t", p=128), [B["gaT"]], R=[b_pT])
            ld(qt[:], qkv[g0:g0 + 128, 0:1024], [B["qt"]], R=[b_qkv])
            ld(csq[:], rope_d[n * 128:(n + 1) * 128, :], [B["csq"]])
            ld(gbT[:], pT[5888:6912, g0:g0 + 128].rearrange("(j p) t -> p j t", p=128), [B["gbT"]], R=[b_pT])
            ld(mgT[:, 0:8, :], pT[6912:7936, g0:g0 + 128].rearrange("(j p) t -> p j t", p=128), [B["mgT"]], R=[b_pT])
            ld(mgT[:, 8:16, :], pT[7936:8960, g0:g0 + 128].rearrange("(j p) t -> p j t", p=128), [B["mgT"]], R=[b_pT],
               acc=True)
            ld(xt[:], x_src(l, g0, 128), [B["xt"]], R=[bx_src(l)])
            def genA():
                tt(dve, yf[:], yf[:], yb[:], ALU.add, [B["yf"], B["yb"]], [B["yf"]])
                tt(pool, bf_[:], bf_[:], bb_[:], ALU.add, [B["bf"], B["bb"]], [B["bf"]])
                tt(pool, bf_[:], bf_[:], lnb_bc, ALU.add, [B["bf"], b_par], [B["bf"]])
                yield
                S.op(dve, lambda: V.tensor_reduce(out=st16[:, 0, :], in_=v3(yf[:], 16), axis=AX.X, op=ALU.add),
                     [B["yf"]], [B["st16"]])
                actf(yb[:], yf[:], AF.Square, [B["yf"]], [B["yb"]])
                S.op(dve, lambda: V.tensor_reduce(out=st16[:, 1, :], in_=v3(yb[:], 16), axis=AX.X, op=ALU.add),
                     [B["yb"]], [B["st16"]])
                ts(dve, st16[:, 2, :], st16[:, 0, :], 1.0 / 64, ALU.mult, [B["st16"]], [B["st16"]])
                tt(dve, st16[:, 3, :], st16[:, 2, :], st16[:, 2, :], ALU.mult, [B["st16"]], [B["st16"]])
                S.op(dve, lambda: V.scalar_tensor_tensor(out=st16[:, 4, :], in0=st16[:, 1, :], scalar=1.0 / 64,
                                                         in1=st16[:, 3, :], op0=ALU.mult, op1=ALU.subtract),
                     [B["st16"]], [B["st16"]])
                actf(st16[:, 4, :], st16[:, 4, :], AF.Sqrt, [B["st16"]], [B["st16"]], bias=GN_EPS)
                S.op(dve, lambda: V.reciprocal(out=st16[:, 5, :], in_=st16[:, 4, :]), [B["st16"]], [B["st16"]])
                yield
                tt(dve, v3(yb[:], 16), v3(yf[:], 16), st16[:, 2, :].unsqueeze(2).to_broadcast([128, 16, 64]),
                   ALU.subtract, [B["yf"], B["st16"]], [B["yb"]])
                tt(dve, v3(yb[:], 16), v3(yb[:], 16), st16[:, 5, :].unsqueeze(2).to_broadcast([128, 16, 64]),
                   ALU.mult, [B["yb"], B["st16"]], [B["yb"]])
                tt(dve, yb[:], yb[:], lng_bc, ALU.mult, [B["yb"], b_par], [B["yb"]])
                tt(dve, yb[:], yb[:], bf_[:], ALU.add, [B["yb"], B["bf"]], [B["yb"]])
                yield
                for kc in range(8):
                    trp(R0[:, kc * 128:(kc + 1) * 128], yb[:, kc * 128:(kc + 1) * 128], identf, [B["yb"], b_cst],
                        [bR0[kc // 4]])
                actf(gaT[:], gaT[:], AF.Silu, [B["gaT"]], [B["gaT"]])
                tt(dve, yagT[:], R0[:].rearrange("p (j t) -> p j t", j=8), gaT[:], ALU.mult, bR0 + [B["gaT"]], [B["yagT"]])
                yield
                for db in range(8):
                    for kc in range(8):
                        mm(R1[:, db * 128:(db + 1) * 128], wpa[:, kc, db * 128:(db + 1) * 128], yagT[:, kc, :],
                           kc == 0, kc == 7, [B["w"], B["yagT"]], [bR1[db // 4]])
                    if db % 2 == 1:
                        yield

            def genB():
                if n == 0:
                    proc_kv(s0, 0)
                    yield
                if n + 1 < NB:
                    proc_kv(s0, n + 1)
                    yield
                rope_norm(v3(qt[:], 16), 16, tmpq[:], qst, csq, qg8, v3(qr[:], 16),
                          B["qt"], B["tmpq"], B["qst"], B["csq"], B["qr"])
                for rnd in range(2):
                    for hh in range(8):
                        h = rnd * 8 + hh
                        mm(R2[0:64, hh * 128:(hh + 1) * 128], qr[:, h * 64:(h + 1) * 64], identb[:], True, True,
                           [B["qr"], b_cst], [bR2[hh // 4]])
                    cp(act, qT[:, rnd * 8:(rnd + 1) * 8, :], R2[0:64, :].rearrange("p (h t) -> p h t", h=8), bR2,
                       [B["qT"]], acc=(rnd == 1))
                    yield
                cbs = [m for m in (n - 1, n, n + 1) if 0 <= m < NB]
                for gq in range(4):
                    for ci, m in enumerate(cbs):
                        slot = m % 3
                        mm(R3[:, :], kT[slot][:, gq, :], qT[:, 4 * gq:4 * gq + 4, :].rearrange("p h t -> p (h t)"),
                           True, True, [B[f"kT{slot}"], B["qT"]], [B["R3"]])
                        actf(et[ci][:].rearrange("p h t -> p (h t)"), R3[:, :], AF.Exp, [B["R3"]], [B[f"et{ci}"]])
                        if m != n:
                            msk = MPREV if m < n else MNEXT
                            tt(pool, et[ci][:], et[ci][:], msk.unsqueeze(1).to_broadcast([128, 4, 128]), ALU.mult,
                               [B[f"et{ci}"], b_cst], [B[f"et{ci}"]])
                    yield
                    for i4 in range(4):
                        for ci, m in enumerate(cbs):
                            slot = m % 3
                            mm(R4[:, i4 * 65:(i4 + 1) * 65], et[ci][:, i4, :], vaug[slot][:, gq, :],
                               ci == 0, ci == len(cbs) - 1, [B[f"et{ci}"], B[f"vaug{slot}"]], [B["R4"]])
                    r4v = R4[:, 0:260].rearrange("p (h v) -> p h v", h=4)
                    tt(dve, den[:, 0, :], r4v[:, :, 64], esink[:, 4 * gq:4 * gq + 4], ALU.add, [B["R4"], b_par], [B["den"]])
                    S.op(dve, lambda: V.reciprocal(out=den[:, 1, :], in_=den[:, 0, :]), [B["den"]], [B["den"]])
                    tt(dve, v3(og[:], 16)[:, 4 * gq:4 * gq + 4, :], r4v[:, :, 0:64],
                       den[:, 1, :].unsqueeze(2).to_broadcast([128, 4, 64]), ALU.mult, [B["R4"], B["den"]], [B["og"]],
                       acc=(gq > 0))
                    yield

            gens = [genA(), genB()]
            while gens:
                for gen_ in list(gens):
                    try:
                        next(gen_)
                    except StopIteration:
                        gens.remove(gen_)
            for kc in range(8):
                trp(R0[:, kc * 128:(kc + 1) * 128], og[:, kc * 128:(kc + 1) * 128], identf, [B["og"], b_cst],
                    [bR0[kc // 4]])
            actf(gbT[:], gbT[:], AF.Silu, [B["gbT"]], [B["gbT"]])
            tt(dve, ogT[:], R0[:].rearrange("p (j t) -> p j t", j=8), gbT[:], ALU.mult, bR0 + [B["gbT"]], [B["ogT"]])
            for db in range(8):
                for kc in range(8):
                    mm(R2[:, db * 128:(db + 1) * 128], wpb[:, kc, db * 128:(db + 1) * 128], ogT[:, kc, :],
                       kc == 0, kc == 7, [B["w"], B["ogT"]], [bR2[db // 4]])
            actf(mgT[:], mgT[:], AF.Sigmoid, [B["mgT"]], [B["mgT"]])
            tt(dve, t1[:], R1[:].rearrange("p (j t) -> p j t", j=8), mgT[:, 0:8, :], ALU.mult, bR1 + [B["mgT"]], [B["t1"]])
            tt(dve, t2[:], R2[:].rearrange("p (j t) -> p j t", j=8), mgT[:, 8:16, :], ALU.mult, bR2 + [B["mgT"]], [B["t2"]])
            tt(dve, mixT[:], t1[:], t2[:], ALU.add, [B["t1"], B["t2"]], [B["mixT"]])
            for hf, (Rb, bRb) in enumerate(((R3, B["R3"]), (R4, B["R4"]))):
                for kc in range(8):
                    mm(Rb[:, :], mixT[:, kc, :], wo[:, kc, hf * 512:(hf + 1) * 512], kc == 0, kc == 7,
                       [B["mixT"], B["w"]], [bRb])
                tt(dve, xo[:, hf * 512:(hf + 1) * 512], Rb[:, :], xt[:, hf * 512:(hf + 1) * 512], ALU.add,
                   [bRb, B["xt"]], [B["xo"]], acc=(hf == 1))
            st(x_dst(l, g0, 128), xo[:], [B["xo"]], [bx_dst(l)])

        S.barrier()
        for si in range(len(cfg.seqs)):
            for n in range(cfg.seqs[si][1] // 128):
                tile3(si, n)
        S.barrier()


def host_params(inp, L):
    f = lambda k: np.asarray(inp[k], np.float32)
    pcol = np.zeros((L, 128, NPC), np.float32)
    mu = f("shift_mu")
    for l in range(L):
        pcol[l, :, PC_NG:PC_NG + 8] = f("norm_g")[l].reshape(8, 128).T
        pcol[l, :, PC_MU:PC_MU + 24] = mu[l, :3072].reshape(24, 128).T
        pcol[l, :64, PC_MUWD] = mu[l, 3072:3136]
        pcol[l, :64, PC_MUWD + 1] = mu[l, 3136:3200]
        pcol[l, :64, PC_MUAD] = mu[l, 3200:3264]
        pcol[l, :64, PC_MUAD + 1] = mu[l, 3264:3328]
        pcol[l, :, PC_KK:PC_KK + 8] = f("k_k")[l].reshape(8, 128).T
        pcol[l, :, PC_KA:PC_KA + 8] = f("k_a")[l].reshape(8, 128).T
        pcol[l, :, PC_RK:PC_RK + 8] = f("r_k")[l].reshape(8, 128).T
    brow = np.concatenate([f("ln_x_g"), f("ln_x_b"), f("q_norm_g"), f("k_norm_g"), f("sink")], axis=1)
    wup = np.concatenate([f("w_lora_up"), f("w0")[:, :, None, :]], axis=2)
    aup = np.concatenate([f("a_lora_up"), f("a0")[:, :, None, :]], axis=2)
    return dict(pcol=pcol, brow=np.ascontiguousarray(brow), wup_aug=np.ascontiguousarray(wup),
                aup_aug=np.ascontiguousarray(aup))


def make_in_maps(inp, cfg, n_cores=8):
    L = cfg.depth
    hp = host_params(inp, L)
    consts = make_consts()
    rope = make_rope(max(cfg.TP, cfg.TS))
    shared = dict(w_in=np.ascontiguousarray(inp["w_in"][:L], dtype=np.float32),
                  w_proj_a=np.ascontiguousarray(inp["w_proj_a"][:L], dtype=np.float32),
                  w_proj_b=np.ascontiguousarray(inp["w_proj_b"][:L], dtype=np.float32),
                  w_out=np.ascontiguousarray(inp["w_out"][:L], dtype=np.float32),
                  consts=consts, rope=rope, **hp)
    xp, xs = np.asarray(inp["x_prompt"]), np.asarray(inp["x_sample"])
    maps = []
    for c in range(n_cores):
        m = dict(shared)
        m["xp"] = np.ascontiguousarray(xp[c % xp.shape[0]], dtype=np.float32)
        m["xs"] = np.ascontiguousarray(xs[c % xs.shape[0]], dtype=np.float32)
        maps.append(m)
    return maps


def kernel(**inputs):
    xp, xs = inputs["x_prompt"], inputs["x_sample"]
    cfg = Cfg(xp.shape[1], xs.shape[1], inputs["w_in"].shape[0])
    nc = build(cfg)
    maps = make_in_maps(inputs, cfg)
    res = run_bass_kernel_spmd(nc, maps, core_ids=list(range(8)))
    y_p = np.stack([res.results[c]["yp"] for c in range(xp.shape[0])], axis=0).astype(np.float32)
    y_s = np.stack([res.results[c]["ys"] for c in range(xs.shape[0])], axis=0).astype(np.float32)
    return (y_p, y_s)
```
